# Optimizing a Trainium2 kernel written in Bass

```python
import math
import jax, jax.numpy as jnp
from jax import lax
import numpy as np

D_MODEL = 1024
BATCH = 4
SEQ = 4096
DEPTH = 1
DEC_BATCH = 8
DEC_SEQ = 8192
PAST_LEN = 128

A_HEADS = 8
A_HEAD_DIM = 64
A_VALUE_DIM = 2 * A_HEAD_DIM
A_QK = A_HEADS * 2 * A_HEAD_DIM
A_WIDTH = A_HEADS * A_VALUE_DIM
Q_BLOCK = 128
NUM_BUCKETS = 32
MAX_DISTANCE = 128
G_HEADS = 8
G_KEY_DIM = 128
G_VALUE_DIM = 128
G_KWIDTH = G_HEADS * G_KEY_DIM
G_VWIDTH = G_HEADS * G_VALUE_DIM
CHUNK = 64
D_FF = 4 * D_MODEL
EPS = 1e-6

IN_SPLITS = (A_QK, A_QK, A_WIDTH, G_KWIDTH, G_KWIDTH, G_KWIDTH, G_VWIDTH, G_VWIDTH, D_MODEL, D_MODEL)
IN_WIDTH = sum(IN_SPLITS)
SPLIT_IDX = tuple(int(s) for s in np.cumsum(IN_SPLITS)[:-1])

kernel_name = 'hybrid_diffattn_hgrn2_gated_encoder'


def rmsnorm(x, g):
    xf = x.astype(jnp.float32)
    y = xf * lax.rsqrt(jnp.mean(xf * xf, axis=-1, keepdims=True) + EPS)
    return (y * g.astype(jnp.float32)).astype(x.dtype)


def t5_bucket(rel):
    nb = NUM_BUCKETS // 2
    ret = (rel > 0).astype(jnp.int32) * nb
    n = jnp.abs(rel)
    max_exact = nb // 2
    is_small = n < max_exact
    nf = jnp.maximum(n, 1).astype(jnp.float32)
    large = max_exact + (jnp.log(nf / max_exact) / math.log(MAX_DISTANCE / max_exact)
                         * (nb - max_exact)).astype(jnp.int32)
    large = jnp.minimum(large, nb - 1)
    return ret + jnp.where(is_small, n, large)


def diff_attention(q, k, v, rel_bias, lam, lam_init, g_sub):
    B, T = q.shape[0], q.shape[1]
    nblk = T // Q_BLOCK
    scale = A_HEAD_DIM ** -0.5
    kpos = jnp.arange(T, dtype=jnp.int32)
    qb = q.reshape(B, nblk, Q_BLOCK, A_HEADS, 2, A_HEAD_DIM).transpose(1, 0, 2, 3, 4, 5)
    starts = jnp.arange(nblk, dtype=jnp.int32) * Q_BLOCK

    def block(args):
        qblk, start = args
        s = jnp.einsum('bqhcd,bkhcd->bhcqk', qblk, k).astype(jnp.float32) * scale
        qpos = start + jnp.arange(Q_BLOCK, dtype=jnp.int32)
        bucket = t5_bucket(kpos[None, :] - qpos[:, None])
        bias = jnp.transpose(rel_bias[bucket].astype(jnp.float32), (2, 0, 1))
        p = jax.nn.softmax(s + bias[None, :, None], axis=-1)
        w = p[:, :, 0] - lam * p[:, :, 1]
        return jnp.einsum('bhqk,bkhv->bqhv', w.astype(v.dtype), v)

    o = lax.map(block, (qb, starts))
    o = o.transpose(1, 0, 2, 3, 4).reshape(B, T, A_HEADS, A_VALUE_DIM)
    o = rmsnorm(o, g_sub) * (1.0 - lam_init)
    return o.reshape(B, T, A_WIDTH)


def gla_chunk_scan(q, k, v, g):
    B, H, T, dk = q.shape
    dv = v.shape[-1]
    n = T // CHUNK

    def chunks(a):
        return jnp.moveaxis(a.reshape(B, H, n, CHUNK, a.shape[-1]), 2, 0)

    mask = jnp.tril(jnp.ones((CHUNK, CHUNK), dtype=bool))[..., None]

    def step(S, inp):
        qc, kc, vc, gc = inp
        b = jnp.cumsum(gc, axis=-2)
        diff = b[..., :, None, :] - b[..., None, :, :]
        decay = jnp.where(mask, jnp.exp(jnp.where(mask, diff, 0.0)), 0.0)
        A = jnp.einsum('bhtk,bhsk,bhtsk->bhts', qc, kc, decay)
        o = (jnp.einsum('bhtk,bhkv->bhtv', qc * jnp.exp(b), S)
             + jnp.einsum('bhts,bhsv->bhtv', A, vc))
        bl = b[..., -1:, :]
        S = (jnp.exp(bl)[..., 0, :, None] * S
             + jnp.einsum('bhsk,bhsv->bhkv', kc * jnp.exp(bl - b), vc))
        return S, o

    S0 = jnp.zeros((B, H, dk, dv), jnp.float32)
    _, o = lax.scan(step, S0, (chunks(q), chunks(k), chunks(v), chunks(g)))
    return jnp.moveaxis(o, 0, 2).reshape(B, H, T, dv)


def hgrn2_bidir(q, zf_fwd, zf_bwd, i, og, lb_f, lb_b, g_out):
    B, T = q.shape[0], q.shape[1]

    def heads(a, d):
        return a.astype(jnp.float32).reshape(B, T, G_HEADS, d).transpose(0, 2, 1, 3)

    qh = heads(q, G_KEY_DIM)
    vh = heads(i, G_VALUE_DIM)

    def direction(zf, lb, flip):
        f = lb + (1.0 - lb) * jax.nn.sigmoid(zf.astype(jnp.float32))
        kh = heads(1.0 - f, G_KEY_DIM)
        gh = heads(jnp.log(f), G_KEY_DIM)
        if flip:
            o = gla_chunk_scan(jnp.flip(qh, 2), jnp.flip(kh, 2), jnp.flip(vh, 2), jnp.flip(gh, 2))
            return jnp.flip(o, 2)
        return gla_chunk_scan(qh, kh, vh, gh)

    o = direction(zf_fwd, lb_f, False) + direction(zf_bwd, lb_b, True)
    o = o.transpose(0, 2, 1, 3)
    gate = jax.nn.silu(og.astype(jnp.float32)).reshape(B, T, G_HEADS, G_VALUE_DIM)
    o = rmsnorm(o, g_out) * gate
    return o.reshape(B, T, G_VWIDTH).astype(q.dtype)


def encoder_layer(x, l, rel_bias, g_mix_pre, w_in, lam_q1, lam_k1, lam_q2, lam_k2, g_attn_sub,
                  lb_fwd, lb_bwd, g_hgrn_out, w_proj_a, w_proj_b, w_out, g_mix_post,
                  g_mlp_pre, w_mlp_up, w_mlp_down, g_mlp_post):
    B, T = x.shape[0], x.shape[1]
    h = rmsnorm(x, g_mix_pre[l])
    proj = h @ w_in[l]
    aq, ak, av, gq, gff, gfb, gi, gog, ga, gb = jnp.split(proj, SPLIT_IDX, axis=-1)

    lam_init = 0.8 - 0.6 * math.exp(-0.3 * l)
    lam = (jnp.exp(jnp.sum(lam_q1[l].astype(jnp.float32) * lam_k1[l].astype(jnp.float32)))
           - jnp.exp(jnp.sum(lam_q2[l].astype(jnp.float32) * lam_k2[l].astype(jnp.float32)))
           + lam_init)
    o_a = diff_attention(aq.reshape(B, T, A_HEADS, 2, A_HEAD_DIM),
                         ak.reshape(B, T, A_HEADS, 2, A_HEAD_DIM),
                         av.reshape(B, T, A_HEADS, A_VALUE_DIM),
                         rel_bias, lam, lam_init, g_attn_sub[l])

    lb_f = jnp.cumsum(jax.nn.softmax(lb_fwd.astype(jnp.float32), axis=0), axis=0)[l]
    lb_b = jnp.cumsum(jax.nn.softmax(lb_bwd.astype(jnp.float32), axis=0), axis=0)[l]
    o_b = hgrn2_bidir(gq, gff, gfb, gi, gog, lb_f, lb_b, g_hgrn_out[l])

    merged = jax.nn.sigmoid(ga) * (o_a @ w_proj_a[l]) + jax.nn.sigmoid(gb) * (o_b @ w_proj_b[l])
    x = x + rmsnorm(merged @ w_out[l], g_mix_post[l])

    h = rmsnorm(x, g_mlp_pre[l])
    u = jnp.square(jax.nn.relu(h @ w_mlp_up[l]))
    return x + rmsnorm(u @ w_mlp_down[l], g_mlp_post[l])


def setup_inputs(seed: int = 0) -> dict:
    key = jax.random.key(seed)
    ks = jax.random.split(key, 24)
    nrm = jax.random.normal

    def gain(k, d):
        return 1.0 + 0.05 * nrm(k, (DEPTH, d), jnp.float32)

    return {
        'x_prompt': nrm(ks[0], (BATCH, SEQ, D_MODEL), jnp.float32),
        'x_sample': nrm(ks[1], (DEC_BATCH, DEC_SEQ, D_MODEL), jnp.float32),
        'rel_bias': 0.5 * nrm(ks[2], (NUM_BUCKETS, A_HEADS), jnp.float32),
        'g_mix_pre': gain(ks[3], D_MODEL),
        'w_in': nrm(ks[4], (DEPTH, D_MODEL, IN_WIDTH), jnp.float32) * D_MODEL ** -0.5,
        'lam_q1': 0.1 * nrm(ks[5], (DEPTH, A_HEAD_DIM), jnp.float32),
        'lam_k1': 0.1 * nrm(ks[6], (DEPTH, A_HEAD_DIM), jnp.float32),
        'lam_q2': 0.1 * nrm(ks[7], (DEPTH, A_HEAD_DIM), jnp.float32),
        'lam_k2': 0.1 * nrm(ks[8], (DEPTH, A_HEAD_DIM), jnp.float32),
        'g_attn_sub': gain(ks[9], A_VALUE_DIM),
        'lb_fwd': 0.5 * nrm(ks[10], (DEPTH + 1, G_KWIDTH), jnp.float32),
        'lb_bwd': 0.5 * nrm(ks[11], (DEPTH + 1, G_KWIDTH), jnp.float32),
        'g_hgrn_out': gain(ks[12], G_VALUE_DIM),
        'w_proj_a': nrm(ks[13], (DEPTH, A_WIDTH, D_MODEL), jnp.float32) * A_WIDTH ** -0.5,
        'w_proj_b': nrm(ks[14], (DEPTH, G_VWIDTH, D_MODEL), jnp.float32) * G_VWIDTH ** -0.5,
        'w_out': nrm(ks[15], (DEPTH, D_MODEL, D_MODEL), jnp.float32) * D_MODEL ** -0.5,
        'g_mix_post': gain(ks[16], D_MODEL),
        'g_mlp_pre': gain(ks[17], D_MODEL),
        'w_mlp_up': nrm(ks[18], (DEPTH, D_MODEL, D_FF), jnp.float32) * D_MODEL ** -0.5,
        'w_mlp_down': nrm(ks[19], (DEPTH, D_FF, D_MODEL), jnp.float32) * D_FF ** -0.5,
        'g_mlp_post': gain(ks[20], D_MODEL),
    }


def reference(x_prompt, x_sample, rel_bias, g_mix_pre, w_in, lam_q1, lam_k1, lam_q2, lam_k2,
              g_attn_sub, lb_fwd, lb_bwd, g_hgrn_out, w_proj_a, w_proj_b, w_out, g_mix_post,
              g_mlp_pre, w_mlp_up, w_mlp_down, g_mlp_post):
    def trunk(x):
        for l in range(DEPTH):
            x = encoder_layer(x, l, rel_bias, g_mix_pre, w_in, lam_q1, lam_k1, lam_q2, lam_k2,
                              g_attn_sub, lb_fwd, lb_bwd, g_hgrn_out, w_proj_a, w_proj_b, w_out,
                              g_mix_post, g_mlp_pre, w_mlp_up, w_mlp_down, g_mlp_post)
        return x

    y_prompt = trunk(x_prompt)
    y_sample = trunk(x_sample)
    return (y_prompt, y_sample)
```

```python
import math
import numpy as np
import ml_dtypes
from contextlib import ExitStack
import concourse.bass as bass
import concourse.mybir as mybir
from concourse.bass_utils import run_bass_kernel_spmd

F32 = mybir.dt.float32
BF16 = mybir.dt.bfloat16
AF = mybir.ActivationFunctionType
ALU = mybir.AluOpType
AX = mybir.AxisListType
NDMA = 16
EPS = 1e-6
D = 1024
NJ = 1280
J0 = 639


class Buf:
    def __init__(self, ctx, t):
        self.t = t
        self.wr = None
        self.rd = []
        self.subs = {}
        ctx.bufs.append(self)

    def __getitem__(self, k):
        return self.t[k]


class EngW:
    def __init__(self, name, eng, sem, same_sync):
        self.name, self.eng, self.sem = name, eng, sem
        self.count = 0
        self.waited = {}
        self.same_sync = same_sync

    def wait(self, deps):
        for d in deps:
            if d is None:
                continue
            sem, val = d
            if val <= 0:
                continue
            if sem is self.sem and not self.same_sync:
                continue
            if self.waited.get(sem, 0) >= val:
                continue
            self.eng.wait_ge(sem, val)
            self.waited[sem] = val


class Ctx:
    def __init__(self, nc, es):
        self.nc = nc
        self.E = {}
        self.bufs = []
        for name, eng, ss in [('pe', nc.tensor, False), ('act', nc.scalar, True),
                              ('dve', nc.vector, True), ('pool', nc.gpsimd, True),
                              ('sp', nc.sync, False)]:
            sem = es.enter_context(nc.semaphore('s_' + name))
            self.E[name] = EngW(name, eng, sem, ss)
        self.dma_sems = [es.enter_context(nc.semaphore('d%d' % i)) for i in range(2 * NDMA)]
        self.dma_val = [0] * (2 * NDMA)
        self.dma_next = [0, 0]

    @staticmethod
    def _bk(x):
        return x if isinstance(x, tuple) else (x, None)

    def _deps(self, reads, writes, extra):
        deps = list(extra)
        for x in reads:
            b, k = self._bk(x)
            deps.append(b.wr)
            if k is None:
                for sw, srd in b.subs.values():
                    deps.append(sw)
            elif k in b.subs:
                deps.append(b.subs[k][0])
        for x in writes:
            b, k = self._bk(x)
            deps.append(b.wr)
            deps.extend(b.rd)
            if k is None:
                for sw, srd in b.subs.values():
                    deps.append(sw)
                    deps.extend(srd)
            elif k in b.subs:
                deps.append(b.subs[k][0])
                deps.extend(b.subs[k][1])
        return deps

    def _mark(self, tok, reads, writes):
        for x in reads:
            b, k = self._bk(x)
            if k is None:
                b.rd.append(tok)
            else:
                b.subs.setdefault(k, [None, []])[1].append(tok)
        for x in writes:
            b, k = self._bk(x)
            if k is None:
                b.wr = tok
                b.rd = []
                b.subs = {}
            else:
                b.subs[k] = [tok, []]

    def op(self, en, fn, reads=(), writes=(), extra=()):
        e = self.E[en]
        e.wait(self._deps(reads, writes, extra))
        ins = fn(e.eng)
        e.count += 1
        ins.then_inc(e.sem, 1)
        tok = (e.sem, e.count)
        self._mark(tok, reads, writes)
        return tok

    def dma(self, en, out, in_, reads=(), writes=(), extra=()):
        e = self.E[en]
        grp = 1 if en == 'pool' else 0
        k = grp * NDMA + self.dma_next[grp]
        self.dma_next[grp] = (self.dma_next[grp] + 1) % NDMA
        sem = self.dma_sems[k]
        e.wait(self._deps(reads, writes, extra) + [(sem, self.dma_val[k])])
        ins = e.eng.dma_start(out=out, in_=in_)
        self.dma_val[k] += 16
        ins.then_inc(sem, 16)
        tok = (sem, self.dma_val[k])
        self._mark(tok, reads, writes)
        return tok

    def end_iter(self):
        nc = self.nc
        sp = self.E['sp']
        sp.wait([(s, v) for s, v in zip(self.dma_sems, self.dma_val)])
        nc.all_engine_barrier()
        for e in self.E.values():
            if e.count:
                nc.sync.sem_clear(e.sem)
        for s, v in zip(self.dma_sems[:NDMA], self.dma_val[:NDMA]):
            if v:
                nc.sync.sem_clear(s)
        nc.all_engine_barrier()
        for e in self.E.values():
            e.count = 0
            e.waited = {}
        self.dma_val = [0] * NDMA + self.dma_val[NDMA:]
        self.dma_next = [0, self.dma_next[1]]
        for b in self.bufs:
            b.wr = None
            b.rd = []
            b.subs = {}


def ds(start, size):
    if isinstance(start, int):
        return slice(start, start + size)
    return bass.ds(start, size)


import os
_LOOPK = [0]


class _Stop(Exception):
    pass


def _en(n):
    ph = os.environ.get("KPH")
    return ph is None or str(n) in ph.split(",")


def loop_iter(nc, n, static):
    k = _LOOPK[0] % 7
    _LOOPK[0] += 1
    en = os.environ.get("KLOOPS")
    if en is not None and str(k) not in en.split(","):
        return
    if static:
        for i in range(n):
            yield i
    else:
        with nc.Fori(0, n) as i:
            yield i


_PROBE = [0]


def free_regs(eng):
    regs = []
    try:
        for k in range(200):
            _PROBE[0] += 1
            regs.append(eng.alloc_register("probe%d" % _PROBE[0]))
    except Exception:
        pass
    for r in regs:
        eng.free_register(r)
    return len(regs)


_UID = [0]


class Pool_:
    def __init__(self, c, nc):
        self.c, self.nc = c, nc
        self.es = ExitStack()
        self.n = 0

    def __enter__(self):
        self.es.__enter__()
        return self

    def __exit__(self, *a):
        return self.es.__exit__(*a)

    def sb(self, shape, dt, name=None):
        self.n += 1
        _UID[0] += 1
        t = self.es.enter_context(self.nc.sbuf_tensor("%s_%d" % (name or "t", _UID[0]), shape, dt))
        return Buf(self.c, t)

    def ps(self, shape, dt=F32, name=None):
        self.n += 1
        _UID[0] += 1
        t = self.es.enter_context(self.nc.psum_tensor("%s_%d" % (name or "p", _UID[0]), shape, dt))
        return Buf(self.c, t)


def build(TS, TP, dbg=False):
    nc = bass.Bass("TRN2", target_bir_lowering=False)
    TM = max(TS, TP)
    TQP = TP // 2

    def din(name, shape, dt=F32):
        return nc.dram_tensor(name, shape, dt, kind="ExternalInput").ap()

    def dscr(name, shape, dt):
        return nc.dram_tensor(name, shape, dt, kind="ExternalOutput" if dbg else "Internal").ap()

    xs = din("xs", [TS, D])
    xp = din("xp", [TP, D])
    w_in = din("w_in", [D, 10 * D])
    w_fsw = din("w_fsw", [D, 2 * D])
    w_pa = din("w_pa", [D, D])
    w_pb = din("w_pb", [D, D])
    w_o = din("w_o", [D, D])
    w_up = din("w_up", [D, 4 * D])
    w_dn = din("w_dn", [4 * D, D])
    vc_in = {"s": din("vc_s", [128, 50]), "p": din("vc_p", [128, 50])}
    vb_in = {"s": din("vb_s", [128, 6400]), "p": din("vb_p", [128, 6400])}
    oht_in = {"s": din("oht_s", [32, NJ]), "p": din("oht_p", [32, NJ])}
    cfar_in = {"s": din("cfar_s", [8, 128, 2]), "p": din("cfar_p", [8, 128, 2])}
    relb_in = din("relb", [32, 8])
    cbf_in = din("cbf", [128, 256], BF16)
    cf_in = din("cf", [128, 256])
    ys = nc.dram_tensor("ys", [TS, D], F32, kind="ExternalOutput").ap()
    yp = nc.dram_tensor("yp", [TQP, D], F32, kind="ExternalOutput").ap()

    hT_d = dscr("hT_d", [8, 128, TM], BF16)
    aqT_d = dscr("aqT_d", [8, 128, TM], BF16)
    akT_d = dscr("akT_d", [8, 128, TM], BF16)
    gqT_d = dscr("gqT_d", [8, 128, TM], BF16)
    kT_d = [dscr("kfT_d", [8, 128, TM], BF16), dscr("kbT_d", [8, 128, TM], BF16)]
    av_d = dscr("av_d", [TM, D], BF16)
    gi_d = dscr("gi_d", [TM, D], BF16)
    ktm_d = [dscr("kftm_d", [TM, D], BF16), dscr("kbtm_d", [TM, D], BF16)]
    g_d = [dscr("gf_d", [TM, D], F32), dscr("gb_d", [TM, D], F32)]
    gogT_d = dscr("gogT_d", [8, 128, TM], BF16)
    gaT_d = dscr("gaT_d", [8, 128, TM], BF16)
    gbT_d = dscr("gbT_d", [8, 128, TM], BF16)
    oaT_d = dscr("oaT_d", [8, 128, TM], BF16)
    oT_d = [dscr("ofT_d", [8, 128, TM], F32), dscr("obT_d", [8, 128, TM], F32)]
    x1_d = dscr("x1_d", [TM, D], F32)
    tab_d = dscr("tab_d", [8, NJ], F32)
    bias_d = dscr("bias_d", [48, 128, 512], F32)

    with ExitStack() as es:
        c = Ctx(nc, es)

        with Pool_(c, nc) as gp:
            cbf = gp.sb([128, 256], BF16, "cbf")
            cf = gp.sb([128, 256], F32, "cf")
            c.dma('sp', cbf[:], cbf_in[:, :], writes=[cbf])
            c.dma('sp', cf[:], cf_in[:, :], writes=[cf])
            ident = cbf.t[:, 0:128]
            ones_bf = cbf.t[:, 128:256]
            antiI = cf.t[:, 0:128]
            U_f = cf.t[0:64, 128:192]
            Ut_f = cf.t[0:64, 192:256]
            c.end_iter()

            for job in os.environ.get("KJOBS", "sp"):
                T = TS if job == "s" else TP
                Tq = TS if job == "s" else TQP
                x_d = xs if job == "s" else xp
                y_d = ys if job == "s" else yp
                run_job(nc, c, job, T, Tq, x_d, y_d, locals())
    print('free regs:', {n: free_regs(e.eng) for n, e in c.E.items()}, 'ninstr', nc.n_instructions if not callable(nc.n_instructions) else nc.n_instructions())
    return nc


def run_job(nc, c, job, T, Tq, x_d, y_d, G):
    ident, ones_bf, antiI, U_f, Ut_f = G["ident"], G["ones_bf"], G["antiI"], G["U_f"], G["Ut_f"]
    cbf, cf = G["cbf"], G["cf"]
    vc_d, vb_d, oht_d, cfar_d = G["vc_in"][job], G["vb_in"][job], G["oht_in"][job], G["cfar_in"][job]
    w_in, w_fsw = G["w_in"], G["w_fsw"]
    hT_d, aqT_d, akT_d, gqT_d, kT_d = G["hT_d"], G["aqT_d"], G["akT_d"], G["gqT_d"], G["kT_d"]
    av_d, gi_d, ktm_d, g_d = G["av_d"], G["gi_d"], G["ktm_d"], G["g_d"]
    gogT_d, gaT_d, gbT_d, oaT_d, oT_d, x1_d = G["gogT_d"], G["gaT_d"], G["gbT_d"], G["oaT_d"], G["oT_d"], G["x1_d"]
    tab_d, bias_d, relb_in = G["tab_d"], G["bias_d"], G["relb_in"]
    NB = T // 512
    NBQ = Tq // 512
    static = (job == "p")

    with Pool_(c, nc) as jp:
        vc = jp.sb([128, 50], F32, "vc")
        c.dma('sp', vc[:], vc_d[:, :], writes=[vc])
        der = jp.sb([128, 32], F32, "der")
        lamt = jp.sb([128, 256], F32, "lamt")
        c.dma('sp', lamt[:], vb_d[:, 6144:6400], writes=[lamt])
        c.op('dve', lambda e: e.tensor_tensor(out=der.t[:, 0:8], in0=vc.t[:, 24:32], in1=vc.t[:, 16:24], op=ALU.subtract),
             reads=[vc], writes=[der])
        c.op('dve', lambda e: e.tensor_tensor(out=der.t[:, 8:16], in0=vc.t[:, 40:48], in1=vc.t[:, 32:40], op=ALU.subtract),
             reads=[vc], writes=[der])
        c.op('act', lambda e: e.activation(out=der.t[:, 0:16], in_=der.t[:, 0:16], func=AF.Sigmoid), reads=[der], writes=[der])
        c.op('dve', lambda e: e.tensor_scalar(out=der.t[:, 16:17], in0=vc.t[:, 48:49], scalar1=0.8, scalar2=None, op0=ALU.mult),
             reads=[vc], writes=[der])
        c.op('dve', lambda e: e.tensor_tensor(out=lamt.t[:, 0:128], in0=lamt.t[:, 0:128], in1=lamt.t[:, 128:256], op=ALU.mult),
             reads=[lamt], writes=[lamt])
        c.op('dve', lambda e: e.reduce_sum(out=der.t[:, 20:22], in_=lamt.t[:, 0:128].rearrange("p (a b) -> p a b", a=2), axis=AX.X),
             reads=[lamt, der], writes=[der])
        c.op('act', lambda e: e.activation(out=der.t[:, 20:22], in_=der.t[:, 20:22], func=AF.Exp), reads=[der], writes=[der])
        c.op('dve', lambda e: e.tensor_tensor(out=der.t[:, 22:23], in0=der.t[:, 21:22], in1=der.t[:, 20:21], op=ALU.subtract),
             reads=[der], writes=[der])
        c.op('dve', lambda e: e.tensor_scalar(out=der.t[:, 22:23], in0=der.t[:, 22:23], scalar1=-0.2, scalar2=None, op0=ALU.add),
             reads=[der], writes=[der])
        c.end_iter()
        g1col = vc.t[:, 0:8]
        g2col = vc.t[:, 8:16]
        omlb_col = [der.t[:, 0:8], der.t[:, 8:16]]
        gsubcol = der.t[:, 16:17]
        ghcol = vc.t[:, 49:50]
        neglam = der.t[:, 22:23]

        if _en(0):
            with Pool_(c, nc) as p:
                oht = p.sb([32, NJ], F32)
                relb = p.sb([32, 8], F32)
                tabs = p.sb([8, NJ], F32)
                c.dma('sp', oht[:], oht_d[:, :], writes=[oht])
                c.dma('sp', relb[:], relb_in[:, :], writes=[relb])
                pt = p.ps([8, 512], F32)
                for k0 in range(0, NJ, 512):
                    n = min(512, NJ - k0)
                    c.op('pe', lambda e: e.matmul(pt.t[:, 0:n], lhsT=relb.t[:, :], rhs=oht.t[:, k0:k0 + n], start=True, stop=True),
                         reads=[relb, oht], writes=[pt])
                    c.op('act', lambda e: e.activation(out=tabs.t[:, k0:k0 + n], in_=pt.t[:, 0:n], func=AF.Copy),
                         reads=[pt], writes=[tabs])
                c.dma('sp', tab_d[:, :], tabs[:], reads=[tabs])
                c.end_iter()
                hk = [p.sb([128, 512], F32) for _ in range(2)]
                bt = [p.sb([128, 512], F32) for _ in range(2)]
                pb = [p.ps([128, 512], F32) for _ in range(2)]
                n = 0
                for h in range(8):
                    for di in range(6):
                        delta = -128 + 128 * di
                        off = J0 - 127 - delta
                        src = bass.AP(tab_d.tensor, h * NJ + off, [[1, 128], [1, 512]])
                        c.dma('sp', hk[n % 2][:], src, writes=[hk[n % 2]])
                        c.op('pe', lambda e: e.matmul(pb[n % 2].t[:, :], lhsT=antiI, rhs=hk[n % 2].t[:, :], start=True, stop=True),
                             reads=[hk[n % 2], cf], writes=[pb[n % 2]])
                        c.op('act', lambda e: e.activation(out=bt[n % 2].t[:, :], in_=pb[n % 2].t[:, :], func=AF.Copy),
                             reads=[pb[n % 2]], writes=[bt[n % 2]])
                        c.dma('sp', bias_d[h * 6 + di, :, :], bt[n % 2][:], reads=[bt[n % 2]])
                        n += 1
                c.end_iter()

        def norm_transpose(p, xt_list, hts, gcol, scr):
            junk, xn, ss, ptr = scr
            for j in range(4):
                c.op('act', lambda e: e.activation(out=junk.t[:], in_=xt_list.t[:, j, :], func=AF.Square,
                                                   accum_out=ss.t[:, j:j + 1]),
                     reads=[xt_list], writes=[junk, ss])
                c.op('dve', lambda e: e.tensor_scalar(out=ss.t[:, 4 + j:5 + j], in0=ss.t[:, j:j + 1], scalar1=1.0 / D,
                                                      scalar2=EPS, op0=ALU.mult, op1=ALU.add), reads=[ss], writes=[ss])
                c.op('act', lambda e: e.activation(out=ss.t[:, 4 + j:5 + j], in_=ss.t[:, 4 + j:5 + j], func=AF.Sqrt),
                     reads=[ss], writes=[ss])
                c.op('dve', lambda e: e.reciprocal(out=ss.t[:, 8 + j:9 + j], in_=ss.t[:, 4 + j:5 + j]), reads=[ss], writes=[ss])
                c.op('dve', lambda e: e.tensor_scalar(out=xn[j % 2].t[:], in0=xt_list.t[:, j, :], scalar1=ss.t[:, 8 + j:9 + j],
                                                      scalar2=None, op0=ALU.mult),
                     reads=[xt_list, ss], writes=[xn[j % 2]])
                for cc in range(8):
                    c.op('pe', lambda e: e.transpose(out=ptr[j % 2].t[:, cc, :], in_=xn[j % 2].t[:, cc * 128:(cc + 1) * 128],
                                                     identity=ident),
                         reads=[xn[j % 2], cbf], writes=[ptr[j % 2]])
                c.op('act', lambda e: e.activation(out=hts.t[:, :, j * 128:(j + 1) * 128], in_=ptr[j % 2].t[:], func=AF.Copy),
                     reads=[ptr[j % 2]], writes=[hts])
            c.op('pool', lambda e: e.tensor_tensor(out=hts.t[:], in0=hts.t[:], in1=gcol.unsqueeze(2).to_broadcast([128, 8, 512]),
                                                   op=ALU.mult), reads=[hts, vc], writes=[hts])

        def nt_scratch(p):
            return (p.sb([128, 1024], BF16), [p.sb([128, 1024], BF16) for _ in range(2)], p.sb([128, 12], F32),
                    [p.ps([128, 8, 128], BF16) for _ in range(2)])

        if _en(1):
            with Pool_(c, nc) as p:
                xt4 = p.sb([128, 4, 1024], F32)
                hts = p.sb([128, 8, 512], BF16)
                scr = nt_scratch(p)

                for i in loop_iter(nc, NB, static):
                    c.dma('sp', xt4[:], x_d[ds(i * 512, 512), :].rearrange("(j p) d -> p j d", p=128), writes=[xt4])
                    norm_transpose(p, xt4, hts, g1col, scr)
                    c.dma('sp', hT_d[:, :, ds(i * 512, 512)].rearrange("c p t -> p c t"), hts[:], reads=[hts])
                    c.end_iter()

        def load_w(p, src_ap, ncols=1024, nk=8):
            w = p.sb([128, nk, ncols], BF16, "w")
            for kc in range(nk):
                for c0 in range(0, ncols, 1024):
                    c.dma('pool', w.t[:, kc, c0:c0 + 1024], src_ap[kc * 128:(kc + 1) * 128, c0:c0 + 1024], writes=[w])
            return w

        class PsRot:
            def __init__(self, p, n):
                self.b = [p.ps([128, 512], F32) for _ in range(n)]
                self.i = 0

            def next(self):
                b = self.b[self.i % len(self.b)]
                self.i += 1
                return b

        def proj_cm(w, act, psr, evac, ncol_chunks=8, nk=8):
            for j in range(ncol_chunks):
                ps = psr.next()
                for k in range(nk):
                    c.op('pe', lambda e: e.matmul(ps.t[:, :], lhsT=w.t[:, k, j * 128:(j + 1) * 128], rhs=act.t[:, k, :],
                                                  start=(k == 0), stop=(k == nk - 1)),
                         reads=[w, act], writes=[ps])
                evac(j, ps)

        def proj_tm(w, act, psr, evac, ncols=1024, nk=8, ntok=4):
            for s in range(ntok):
                for hf in range(ncols // 512):
                    ps = psr.next()
                    for k in range(nk):
                        c.op('pe', lambda e: e.matmul(ps.t[:, :], lhsT=act.t[:, k, s * 128:(s + 1) * 128],
                                                      rhs=w.t[:, k, hf * 512:(hf + 1) * 512],
                                                      start=(k == 0), stop=(k == nk - 1)),
                             reads=[w, act], writes=[ps])
                    evac(s, hf, ps)

        wf_src = [w_in[:, 4096:5120], w_in[:, 5120:6144]] if job == "s" else [w_fsw[:, 0:1024], w_fsw[:, 1024:2048]]

        if _en(2):
            with Pool_(c, nc) as p:
                W = {"aq": load_w(p, w_in[:, 0:1024]), "ak": load_w(p, w_in[:, 1024:2048]),
                     "av": load_w(p, w_in[:, 2048:3072]), "gq": load_w(p, w_in[:, 3072:4096]),
                     "f0": load_w(p, wf_src[0]), "f1": load_w(p, wf_src[1]), "gi": load_w(p, w_in[:, 6144:7168])}
                lbB = p.sb([128, 4, 1024], F32, "lbB")
                c.dma('sp', lbB[:], vb_d[:, 2048:6144].rearrange("p (a b) -> p a b", a=4), writes=[lbB])
                dtmp = p.sb([128, 2, 1024], F32, "dtmp")
                for d_ in range(2):
                    c.op('dve', lambda e: e.tensor_tensor(out=dtmp.t[:, d_, :], in0=lbB.t[:, 2 * d_, :], in1=lbB.t[:, 2 * d_ + 1, :],
                                                          op=ALU.subtract), reads=[lbB], writes=[dtmp])
                for d_ in range(2):
                    c.op('act', lambda e: e.activation(out=lbB.t[:, 2 * d_, :], in_=dtmp.t[:, d_, :], func=AF.Sigmoid),
                         reads=[dtmp], writes=[lbB])
                    c.op('act', lambda e: e.activation(out=lbB.t[:, 2 * d_ + 1, :], in_=dtmp.t[:, d_, :], func=AF.Sigmoid, scale=-1.0),
                         reads=[dtmp], writes=[lbB])
                c.end_iter()
                hts = p.sb([128, 8, 512], BF16, "hts")
                stg_cm = [p.sb([128, 8, 512], BF16, "stgcm") for _ in range(2)]
                stg_bf = [p.sb([128, 1024], BF16, "stgbf") for _ in range(2)]
                stg_f = [p.sb([128, 1024], F32, "stgf") for _ in range(2)]
                sig = [p.sb([128, 512], F32, "sig") for _ in range(4)]
                psr = PsRot(p, 8)
                cnt = {"cm": 0, "tm": 0, "sg": 0}
                tq = 'sp' if static else 'act'
                for i in loop_iter(nc, NB, static):
                    c.dma('sp', hts[:], hT_d[:, :, ds(i * 512, 512)].rearrange("c p t -> p c t"), writes=[hts])

                    def cm_plain(wname, dst):
                        st = stg_cm[cnt["cm"] % 2]
                        cnt["cm"] += 1
                        eng = ['act', 'dve']

                        def ev(j, ps):
                            if j % 2 == 0:
                                c.op('act', lambda e: e.activation(out=st.t[:, j, :], in_=ps.t[:, :], func=AF.Copy),
                                     reads=[ps], writes=[(st, j)])
                            else:
                                c.op('dve', lambda e: e.tensor_copy(out=st.t[:, j, :], in_=ps.t[:, :]), reads=[ps], writes=[(st, j)])
                        proj_cm(W[wname], hts, psr, ev)
                        c.dma('sp', dst[:, :, ds(i * 512, 512)].rearrange("c p t -> p c t"), st[:], reads=[st])

                    def tm_plain(wname, dst):
                        sts = {}

                        def ev(s, hf, ps):
                            if hf == 0:
                                sts[s] = stg_bf[cnt["tm"] % 2]
                                cnt["tm"] += 1
                            st = sts[s]
                            if hf == 0:
                                c.op('act', lambda e: e.activation(out=st.t[:, 0:512], in_=ps.t[:, :], func=AF.Copy),
                                     reads=[ps], writes=[(st, 0)])
                            else:
                                c.op('dve', lambda e: e.tensor_copy(out=st.t[:, 512:1024], in_=ps.t[:, :]), reads=[ps], writes=[(st, 1)])
                                c.dma(tq, dst[ds(i * 512 + s * 128, 128), :], st[:], reads=[st])
                        proj_tm(W[wname], hts, psr, ev)

                    def f_cm(d_):
                        st = stg_cm[cnt["cm"] % 2]
                        cnt["cm"] += 1

                        def ev(j, ps):
                            sg = sig[cnt["sg"] % 4]
                            cnt["sg"] += 1
                            c.op('act', lambda e: e.activation(out=sg.t[:, :], in_=ps.t[:, :], func=AF.Sigmoid, scale=-1.0),
                                 reads=[ps], writes=[sg])
                            c.op('pool', lambda e: e.tensor_scalar(out=st.t[:, j, :], in0=sg.t[:, :], scalar1=omlb_col[d_][:, j:j + 1],
                                                                   scalar2=None, op0=ALU.mult), reads=[sg, der], writes=[(st, j)])
                        proj_cm(W["f%d" % d_], hts, psr, ev)
                        c.dma('sp', kT_d[d_][:, :, ds(i * 512, 512)].rearrange("c p t -> p c t"), st[:], reads=[st])

                    def f_tm(d_):
                        sts = {}

                        def ev(s, hf, ps):
                            if hf == 0:
                                sts[s] = (stg_bf[cnt["tm"] % 2], stg_f[cnt["tm"] % 2])
                                cnt["tm"] += 1
                            sb_, sf_ = sts[s]
                            sg = sig[cnt["sg"] % 4]
                            cnt["sg"] += 1
                            cs = slice(hf * 512, (hf + 1) * 512)
                            c.op('act', lambda e: e.activation(out=sg.t[:, :], in_=ps.t[:, :], func=AF.Sigmoid), reads=[ps], writes=[sg])
                            c.op('dve', lambda e: e.tensor_tensor(out=sg.t[:, :], in0=sg.t[:, :], in1=lbB.t[:, 2 * d_ + 1, cs], op=ALU.mult),
                                 reads=[sg, lbB], writes=[sg])
                            c.op('dve', lambda e: e.tensor_tensor(out=sg.t[:, :], in0=sg.t[:, :], in1=lbB.t[:, 2 * d_, cs], op=ALU.add),
                                 reads=[sg, lbB], writes=[sg])
                            c.op('act', lambda e: e.activation(out=sf_.t[:, cs], in_=sg.t[:, :], func=AF.Ln), reads=[sg], writes=[(sf_, hf)])
                            c.op('pool', lambda e: e.tensor_scalar(out=sb_.t[:, cs], in0=sg.t[:, :], scalar1=-1.0, scalar2=1.0,
                                                                   op0=ALU.mult, op1=ALU.add), reads=[sg], writes=[(sb_, hf)])
                            if hf == 1:
                                c.dma(tq, g_d[d_][ds(i * 512 + s * 128, 128), :], sf_[:], reads=[sf_])
                                c.dma(tq, ktm_d[d_][ds(i * 512 + s * 128, 128), :], sb_[:], reads=[sb_])
                        proj_tm(W["f%d" % d_], hts, psr, ev)

                    cm_plain("aq", aqT_d)
                    cm_plain("ak", akT_d)
                    tm_plain("av", av_d)
                    cm_plain("gq", gqT_d)
                    tm_plain("gi", gi_d)
                    f_cm(0)
                    f_cm(1)
                    f_tm(0)
                    f_tm(1)
                    c.end_iter()

        if _en(3):
            with Pool_(c, nc) as p:
                W = {"og": load_w(p, w_in[:, 7168:8192]), "ga": load_w(p, w_in[:, 8192:9216]), "gb": load_w(p, w_in[:, 9216:10240])}
                hts = p.sb([128, 8, 512], BF16, "hts")
                stg_cm = [p.sb([128, 8, 512], BF16, "stgcm") for _ in range(2)]
                psr = PsRot(p, 8)
                c.end_iter()
                for i in loop_iter(nc, NBQ, static):
                    c.dma('sp', hts[:], hT_d[:, :, ds(i * 512, 512)].rearrange("c p t -> p c t"), writes=[hts])
                    for n_, (wname, dst, fn) in enumerate((("ga", gaT_d, AF.Sigmoid), ("gb", gbT_d, AF.Sigmoid), ("og", gogT_d, AF.Silu))):
                        st = stg_cm[n_ % 2]

                        def ev(j, ps):
                            c.op('act', lambda e: e.activation(out=st.t[:, j, :], in_=ps.t[:, :], func=fn), reads=[ps], writes=[(st, j)])
                        proj_cm(W[wname], hts, psr, ev)
                        c.dma('sp', dst[:, :, ds(i * 512, 512)].rearrange("c p t -> p c t"), st[:], reads=[st])
                    c.end_iter()

        NKB = T // 128
        if _en(4):
            with Pool_(c, nc) as p:
                kT = p.sb([128, 1, T], BF16, "kT")
                vv = p.sb([128, NKB, 128], BF16, "vv")
                biasT = p.sb([128, 6, 512], F32, "biasT")
                cfar = p.sb([128, 1, 2], F32, "cfar")
                qT = p.sb([128, 1, Tq], BF16, "qT")
                pS = [[p.ps([128, 512], F32, "pS") for _ in range(2)] for _ in range(2)]
                pO = [p.ps([128, 512], F32, "pO") for _ in range(2)]
                pZ = [p.ps([128, 512], F32, "pZ") for _ in range(2)]
                pT = [[p.sb([128, 512], BF16, "pT") for _ in range(4)] for _ in range(2)]
                tmpS = [p.sb([128, 512], F32, "tmpS") for _ in range(2)]
                rz = [p.sb([128, 512], F32, "rz") for _ in range(2)]
                o01 = [p.sb([128, 512], F32, "o01") for _ in range(2)]
                osq = p.sb([128, 512], BF16, "osq")
                rs = p.sb([128, 512], F32, "rs")
                oout = p.sb([128, 1, Tq], BF16, "oout")
                for h in loop_iter(nc, 8, static):
                    c.dma('sp', kT[:], akT_d[ds(h, 1), :, 0:T].rearrange("a p t -> p a t"), writes=[kT])
                    c.dma('sp', vv[:], av_d[0:T, ds(h * 128, 128)].rearrange("(kb p) d -> p kb d", p=128), writes=[vv])
                    c.dma('sp', biasT[:], bias_d[ds(h * 6, 6), :, :].rearrange("d p q -> p d q"), writes=[biasT])
                    c.dma('sp', cfar[:], cfar_d[ds(h, 1), :, :].rearrange("a p s -> p a s"), writes=[cfar])
                    c.dma('sp', qT[:], aqT_d[ds(h, 1), :, 0:Tq].rearrange("a p t -> p a t"), writes=[qT])
                    for qb in range(Tq // 512):
                        q_ = qT
                        qsl = slice(qb * 512, (qb + 1) * 512)
                        def qk_exp(kb):
                            delta = kb * 128 - qb * 512
                            near = -128 <= delta <= 512
                            pts = []
                            pss = []
                            for m in range(2):
                                ps = pS[m][kb % 2]
                                pr = slice(m * 64, (m + 1) * 64)
                                c.op('pe', lambda e: e.matmul(ps.t[:, :], lhsT=kT.t[pr, 0, kb * 128:(kb + 1) * 128], rhs=q_.t[pr, 0, qsl],
                                                              start=True, stop=True), reads=[kT, q_], writes=[ps])
                                pss.append(ps)
                            for m in range(2):
                                ps = pss[m]
                                pt_ = pT[m][kb % 4]
                                if near:
                                    di = (delta + 128) // 128
                                    tm_ = tmpS[m]
                                    c.op('dve', lambda e: e.scalar_tensor_tensor(out=tm_.t[:, :], in0=ps.t[:, :], scalar=0.125,
                                                                                 in1=biasT.t[:, di, :], op0=ALU.mult, op1=ALU.add),
                                         reads=[ps, biasT], writes=[tm_])
                                    c.op('act', lambda e: e.activation(out=pt_.t[:, :], in_=tm_.t[:, :], func=AF.Exp),
                                         reads=[tm_], writes=[pt_])
                                else:
                                    side = 0 if delta > 0 else 1
                                    c.op('act', lambda e: e.activation(out=pt_.t[:, :], in_=ps.t[:, :], func=AF.Exp,
                                                                       bias=cfar.t[:, 0, side:side + 1], scale=0.125),
                                         reads=[ps, cfar], writes=[pt_])
                                pts.append(pt_)
                            return pts

                        def pv_z(kb, pts):
                            for m in range(2):
                                pt_ = pts[m]
                                c.op('pe', lambda e: e.matmul(pO[m].t[:, :], lhsT=vv.t[:, kb, :], rhs=pt_.t[:, :],
                                                              start=(kb == 0), stop=(kb == NKB - 1)), reads=[vv, pt_], writes=[pO[m]])
                                c.op('pe', lambda e: e.matmul(pZ[m].t[:, :], lhsT=ones_bf, rhs=pt_.t[:, :],
                                                              start=(kb == 0), stop=(kb == NKB - 1)), reads=[cbf, pt_], writes=[pZ[m]])

                        pend = qk_exp(0)
                        for kb in range(NKB):
                            nxt = qk_exp(kb + 1) if kb + 1 < NKB else None
                            pv_z(kb, pend)
                            pend = nxt
                        for m in range(2):
                            c.op('dve', lambda e: e.reciprocal(out=rz[m].t[:, :], in_=pZ[m].t[:, :]), reads=[pZ[m]], writes=[rz[m]])
                            c.op('dve', lambda e: e.tensor_tensor(out=o01[m].t[:, :], in0=pO[m].t[:, :], in1=rz[m].t[:, :], op=ALU.mult),
                                 reads=[pO[m], rz[m]], writes=[o01[m]])
                        c.op('dve', lambda e: e.scalar_tensor_tensor(out=o01[0].t[:, :], in0=o01[1].t[:, :], scalar=neglam,
                                                                     in1=o01[0].t[:, :], op0=ALU.mult, op1=ALU.add),
                             reads=[o01[0], o01[1], der], writes=[o01[0]])
                        c.op('pool', lambda e: e.tensor_tensor(out=osq.t[:, :], in0=o01[0].t[:, :], in1=o01[0].t[:, :], op=ALU.mult),
                             reads=[o01[0]], writes=[osq])
                        c.op('pe', lambda e: e.matmul(pZ[0].t[:, :], lhsT=ones_bf, rhs=osq.t[:, :], start=True, stop=True),
                             reads=[cbf, osq], writes=[pZ[0]])
                        c.op('dve', lambda e: e.tensor_scalar(out=rs.t[:, :], in0=pZ[0].t[:, :], scalar1=1.0 / 128, scalar2=EPS,
                                                              op0=ALU.mult, op1=ALU.add), reads=[pZ[0]], writes=[rs])
                        c.op('act', lambda e: e.activation(out=rs.t[:, :], in_=rs.t[:, :], func=AF.Sqrt), reads=[rs], writes=[rs])
                        c.op('dve', lambda e: e.reciprocal(out=rs.t[:, :], in_=rs.t[:, :]), reads=[rs], writes=[rs])
                        oo = oout
                        c.op('dve', lambda e: e.scalar_tensor_tensor(out=oo.t[:, 0, qsl], in0=o01[0].t[:, :], scalar=gsubcol,
                                                                     in1=rs.t[:, :], op0=ALU.mult, op1=ALU.mult),
                             reads=[o01[0], rs, der], writes=[oo])
                    c.dma('sp', oaT_d[ds(h, 1), :, 0:Tq].rearrange("a p t -> p a t"), oout[:], reads=[oout])
                    c.end_iter()

        NG = T // 256
        if _en(5):
            with Pool_(c, nc) as p:
                S = [p.sb([128, 8, 128], F32, "S") for _ in range(2)]
                Sbf = [p.sb([128, 8, 128], BF16, "Sbf") for _ in range(2)]
                qTt = [p.sb([128, 8, 256], BF16, "qTt") for _ in range(2)]
                kTt = [p.sb([128, 8, 256], BF16, "kTt") for _ in range(2)]
                ktm = [p.sb([64, 4, 1024], BF16, "ktm") for _ in range(2)]
                gtm = [p.sb([64, 4, 1024], F32, "gtm") for _ in range(2)]
                vtm = [p.sb([64, 4, 1024], BF16, "vtm") for _ in range(2)]
                ostg = [p.sb([128, 8, 256], F32, "ostg") for _ in range(2)]
                ET = p.sb([128, 8, 64], F32, "ET")
                EiT = p.sb([128, 8, 64], F32, "EiT")
                Eitm = p.sb([64, 1024], F32, "Eitm")
                qt_ = p.sb([128, 8, 64], BF16, "qt_")
                kt_ = p.sb([128, 8, 64], BF16, "kt_")
                ktm_ = p.sb([64, 1024], BF16, "ktm_")
                ATm = p.sb([64, 8, 64], BF16, "ATm")
                Stmp = p.sb([128, 8, 128], F32, "Stmp")
                p_bT = p.ps([128, 8, 64], F32, "p_bT")
                p_btm = p.ps([64, 1024], F32, "p_btm")
                p_AT = p.ps([64, 8, 64], F32, "p_AT")
                p_oT = p.ps([128, 8, 64], F32, "p_oT")
                p_dS = p.ps([128, 8, 128], F32, "p_dS")
                for d_ in range(2):
                    c.op('pool', lambda e: e.memset(S[d_].t[:], 0.0), writes=[S[d_]])
                    c.op('pool', lambda e: e.memset(Sbf[d_].t[:], 0.0), writes=[Sbf[d_]])
                c.end_iter()
                dq = 'sp' if static else 'act'
                for i in loop_iter(nc, NG, static):
                    for d_ in range(2):
                        t0 = i * 256 if d_ == 0 else (NG - 1 - i) * 256
                        c.dma(dq, qTt[d_][:], gqT_d[:, :, ds(t0, 256)].rearrange("c p t -> p c t"), writes=[qTt[d_]])
                        c.dma(dq, kTt[d_][:], kT_d[d_][:, :, ds(t0, 256)].rearrange("c p t -> p c t"), writes=[kTt[d_]])
                        c.dma(dq, ktm[d_][:], ktm_d[d_][ds(t0, 256), :].rearrange("(a s) d -> s a d", s=64), writes=[ktm[d_]])
                        c.dma(dq, gtm[d_][:], g_d[d_][ds(t0, 256), :].rearrange("(a s) d -> s a d", s=64), writes=[gtm[d_]])
                        c.dma(dq, vtm[d_][:], gi_d[ds(t0, 256), :].rearrange("(a s) d -> s a d", s=64), writes=[vtm[d_]])
                    for cc in range(4):
                        for d_ in range(2):
                            ch = cc if d_ == 0 else 3 - cc
                            Um = U_f if d_ == 0 else Ut_f
                            last = 63 if d_ == 0 else 0
                            tsl = slice(ch * 64, (ch + 1) * 64)
                            for hh in range(8):
                                c.op('pe', lambda e: e.matmul(p_bT.t[:, hh, :], lhsT=gtm[d_].t[:, ch, hh * 128:(hh + 1) * 128], rhs=Um,
                                                              start=True, stop=True), reads=[gtm[d_], cf], writes=[p_bT])
                            for hf in range(2):
                                c.op('pe', lambda e: e.matmul(p_btm.t[:, hf * 512:(hf + 1) * 512], lhsT=Um,
                                                              rhs=gtm[d_].t[:, ch, hf * 512:(hf + 1) * 512], start=True, stop=True),
                                     reads=[gtm[d_], cf], writes=[p_btm])
                            c.op('act', lambda e: e.activation(out=ET.t[:], in_=p_bT.t[:], func=AF.Exp), reads=[p_bT], writes=[ET])
                            c.op('act', lambda e: e.activation(out=EiT.t[:], in_=p_bT.t[:], func=AF.Exp, scale=-1.0),
                                 reads=[p_bT], writes=[EiT])
                            c.op('act', lambda e: e.activation(out=Eitm.t[:], in_=p_btm.t[:], func=AF.Exp, scale=-1.0),
                                 reads=[p_btm], writes=[Eitm])
                            c.op('dve', lambda e: e.tensor_tensor(out=qt_.t[:], in0=qTt[d_].t[:, :, tsl], in1=ET.t[:], op=ALU.mult),
                                 reads=[qTt[d_], ET], writes=[qt_])
                            c.op('pool', lambda e: e.tensor_tensor(out=kt_.t[:], in0=kTt[d_].t[:, :, tsl], in1=EiT.t[:], op=ALU.mult),
                                 reads=[kTt[d_], EiT], writes=[kt_])
                            c.op('pool', lambda e: e.tensor_tensor(out=ktm_.t[:], in0=ktm[d_].t[:, ch, :], in1=Eitm.t[:], op=ALU.mult),
                                 reads=[ktm[d_], Eitm], writes=[ktm_])
                            for hh in range(8):
                                c.op('pe', lambda e: e.matmul(p_AT.t[:, hh, :], lhsT=kt_.t[:, hh, :], rhs=qt_.t[:, hh, :],
                                                              start=True, stop=True), reads=[kt_, qt_], writes=[p_AT])
                            c.op('dve', lambda e: e.tensor_tensor(out=ATm.t[:], in0=p_AT.t[:],
                                                                  in1=Um.unsqueeze(1).to_broadcast([64, 8, 64]), op=ALU.mult),
                                 reads=[p_AT, cf], writes=[ATm])
                            for hh in range(8):
                                c.op('pe', lambda e: e.matmul(p_oT.t[:, hh, :], lhsT=Sbf[d_].t[:, hh, :], rhs=qt_.t[:, hh, :],
                                                              start=True, stop=False), reads=[Sbf[d_], qt_], writes=[p_oT])
                                c.op('pe', lambda e: e.matmul(p_oT.t[:, hh, :], lhsT=vtm[d_].t[:, ch, hh * 128:(hh + 1) * 128],
                                                              rhs=ATm.t[:, hh, :], start=False, stop=True),
                                     reads=[vtm[d_], ATm], writes=[p_oT])
                            c.op('act', lambda e: e.activation(out=ostg[d_].t[:, :, tsl], in_=p_oT.t[:], func=AF.Copy),
                                 reads=[p_oT], writes=[ostg[d_]])
                            for hh in range(8):
                                c.op('pe', lambda e: e.matmul(p_dS.t[:, hh, :], lhsT=ktm_.t[:, hh * 128:(hh + 1) * 128],
                                                              rhs=vtm[d_].t[:, ch, hh * 128:(hh + 1) * 128], start=True, stop=True),
                                     reads=[ktm_, vtm[d_]], writes=[p_dS])
                            c.op('dve', lambda e: e.tensor_tensor(out=Stmp.t[:], in0=S[d_].t[:], in1=p_dS.t[:], op=ALU.add),
                                 reads=[S[d_], p_dS], writes=[Stmp])
                            c.op('dve', lambda e: e.tensor_tensor(out=S[d_].t[:], in0=Stmp.t[:],
                                                                  in1=ET.t[:, :, last:last + 1].to_broadcast([128, 8, 128]), op=ALU.mult),
                                 reads=[Stmp, ET], writes=[S[d_]])
                            c.op('pool', lambda e: e.tensor_copy(out=Sbf[d_].t[:], in_=S[d_].t[:]), reads=[S[d_]], writes=[Sbf[d_]])
                    for d_ in range(2):
                        t0 = i * 256 if d_ == 0 else (NG - 1 - i) * 256
                        c.dma(dq, oT_d[d_][:, :, ds(t0, 256)].rearrange("c p t -> p c t"), ostg[d_][:], reads=[ostg[d_]])
                    c.end_iter()

        if _en(6):
            with Pool_(c, nc) as p:
                Wa = load_w(p, G["w_pa"][:, :])
                Wb = load_w(p, G["w_pb"][:, :])
                Wo = load_w(p, G["w_o"][:, :])
                gpost = p.sb([128, 1024], F32, "gpost")
                c.dma('sp', gpost[:], vb_d[:, 0:1024], writes=[gpost])
                c.end_iter()
                oa = p.sb([128, 8, 512], BF16, "oa")
                of_ = p.sb([128, 8, 512], F32, "of")
                ob_ = p.sb([128, 8, 512], F32, "ob")
                og = p.sb([128, 8, 512], BF16, "og")
                ga = p.sb([128, 8, 512], BF16, "ga")
                gb = p.sb([128, 8, 512], BF16, "gb")
                obn = p.sb([128, 8, 512], BF16, "obn")
                mT = p.sb([128, 8, 512], BF16, "mT")
                sq = [p.sb([128, 512], BF16, "sq") for _ in range(2)]
                rs = [p.sb([128, 512], F32, "rs") for _ in range(2)]
                t1 = [p.sb([128, 512], F32, "t1") for _ in range(2)]
                t2 = [p.sb([128, 512], F32, "t2") for _ in range(2)]
                xt4 = p.sb([128, 4, 1024], F32, "xt4")
                yt4 = p.sb([128, 4, 1024], F32, "yt4")
                junk = p.sb([128, 512], BF16, "junk")
                ss = p.sb([128, 16], F32, "ss")
                psr = PsRot(p, 8)
                for i in loop_iter(nc, NBQ, static):
                    tsl = ds(i * 512, 512)
                    c.dma('sp', oa[:], oaT_d[:, :, tsl].rearrange("c p t -> p c t"), writes=[oa])
                    c.dma('sp', of_[:], oT_d[0][:, :, tsl].rearrange("c p t -> p c t"), writes=[of_])
                    c.dma('sp', ob_[:], oT_d[1][:, :, tsl].rearrange("c p t -> p c t"), writes=[ob_])
                    c.dma('sp', og[:], gogT_d[:, :, tsl].rearrange("c p t -> p c t"), writes=[og])
                    c.dma('sp', ga[:], gaT_d[:, :, tsl].rearrange("c p t -> p c t"), writes=[ga])
                    c.dma('sp', gb[:], gbT_d[:, :, tsl].rearrange("c p t -> p c t"), writes=[gb])
                    c.dma('sp', xt4[:], x_d[ds(i * 512, 512), :].rearrange("(j p) d -> p j d", p=128), writes=[xt4])
                    c.op('pool', lambda e: e.tensor_tensor(out=of_.t[:], in0=of_.t[:], in1=ob_.t[:], op=ALU.add),
                         reads=[of_, ob_], writes=[of_])
                    for hh in range(8):
                        s_ = sq[hh % 2]
                        r_ = rs[hh % 2]
                        c.op('act', lambda e: e.activation(out=s_.t[:, :], in_=of_.t[:, hh, :], func=AF.Square), reads=[of_], writes=[s_])
                        ps = psr.next()
                        c.op('pe', lambda e: e.matmul(ps.t[:, :], lhsT=ones_bf, rhs=s_.t[:, :], start=True, stop=True),
                             reads=[cbf, s_], writes=[ps])
                        c.op('dve', lambda e: e.tensor_scalar(out=r_.t[:, :], in0=ps.t[:, :], scalar1=1.0 / 128, scalar2=EPS,
                                                              op0=ALU.mult, op1=ALU.add), reads=[ps], writes=[r_])
                        c.op('act', lambda e: e.activation(out=r_.t[:, :], in_=r_.t[:, :], func=AF.Sqrt), reads=[r_], writes=[r_])
                        c.op('dve', lambda e: e.reciprocal(out=r_.t[:, :], in_=r_.t[:, :]), reads=[r_], writes=[r_])
                        c.op('dve', lambda e: e.scalar_tensor_tensor(out=r_.t[:, :], in0=of_.t[:, hh, :], scalar=ghcol, in1=r_.t[:, :],
                                                                     op0=ALU.mult, op1=ALU.mult), reads=[of_, r_, vc], writes=[r_])
                        c.op('pool', lambda e: e.tensor_tensor(out=obn.t[:, hh, :], in0=r_.t[:, :], in1=og.t[:, hh, :], op=ALU.mult),
                             reads=[r_, og], writes=[(obn, hh)])
                    for j in range(8):
                        psa = psr.next()
                        for k in range(8):
                            c.op('pe', lambda e: e.matmul(psa.t[:, :], lhsT=Wa.t[:, k, j * 128:(j + 1) * 128], rhs=oa.t[:, k, :],
                                                          start=(k == 0), stop=(k == 7)), reads=[Wa, oa], writes=[psa])
                        psb = psr.next()
                        for k in range(8):
                            c.op('pe', lambda e: e.matmul(psb.t[:, :], lhsT=Wb.t[:, k, j * 128:(j + 1) * 128], rhs=obn.t[:, k, :],
                                                          start=(k == 0), stop=(k == 7)), reads=[Wb, obn], writes=[psb])
                        a_, b_ = t1[j % 2], t2[j % 2]
                        c.op('dve', lambda e: e.tensor_tensor(out=a_.t[:, :], in0=psa.t[:, :], in1=ga.t[:, j, :], op=ALU.mult),
                             reads=[psa, ga], writes=[a_])
                        c.op('dve', lambda e: e.tensor_tensor(out=b_.t[:, :], in0=psb.t[:, :], in1=gb.t[:, j, :], op=ALU.mult),
                             reads=[psb, gb], writes=[b_])
                        c.op('pool', lambda e: e.tensor_tensor(out=mT.t[:, j, :], in0=a_.t[:, :], in1=b_.t[:, :], op=ALU.add),
                             reads=[a_, b_], writes=[(mT, j)])
                    for s in range(4):
                        pss = []
                        for hf in range(2):
                            ps = psr.next()
                            pss.append(ps)
                            for k in range(8):
                                c.op('pe', lambda e: e.matmul(ps.t[:, :], lhsT=mT.t[:, k, s * 128:(s + 1) * 128],
                                                              rhs=Wo.t[:, k, hf * 512:(hf + 1) * 512], start=(k == 0), stop=(k == 7)),
                                     reads=[Wo, mT], writes=[ps])
                            c.op('act', lambda e: e.activation(out=junk.t[:, :], in_=ps.t[:, :], func=AF.Square,
                                                               accum_out=ss.t[:, hf:hf + 1]), reads=[ps], writes=[junk, ss])
                        res_tail(c, ss, pss, gpost, xt4, xt4.t[:, s, :], yt4, yt4.t[:, s, :])
                    c.dma('sp', x1_d[ds(i * 512, 512), :].rearrange("(j p) d -> p j d", p=128), yt4[:], reads=[yt4])
                    c.end_iter()

        if _en(7):
            with Pool_(c, nc) as p:
                Wu = load_w(p, G["w_up"][:, :], ncols=4096, nk=8)
                Wd = load_w(p, G["w_dn"][:, :], ncols=1024, nk=32)
                gpost = p.sb([128, 1024], F32, "gpost2")
                c.dma('sp', gpost[:], vb_d[:, 1024:2048], writes=[gpost])
                c.end_iter()
                xt2 = p.sb([128, 2, 1024], F32, "xt2")
                yt2 = p.sb([128, 2, 1024], F32, "yt2")
                h2 = p.sb([128, 8, 256], BF16, "h2")
                uT = p.sb([128, 32, 256], BF16, "uT")
                rl = [p.sb([128, 256], BF16, "rl") for _ in range(2)]
                junkb = p.sb([128, 1024], BF16, "junkb")
                xn = [p.sb([128, 1024], BF16, "xn") for _ in range(2)]
                ss = p.sb([128, 16], F32, "ss")
                ptr = [p.ps([128, 8, 128], BF16, "ptr") for _ in range(2)]
                psr = PsRot(p, 6)
                for i in loop_iter(nc, Tq // 256, static):
                    c.dma('sp', xt2[:], x1_d[ds(i * 256, 256), :].rearrange("(j p) d -> p j d", p=128), writes=[xt2])
                    for j in range(2):
                        x_ = xt2
                        c.op('act', lambda e: e.activation(out=junkb.t[:], in_=x_.t[:, j, :], func=AF.Square, accum_out=ss.t[:, 8 + j:9 + j]),
                             reads=[x_], writes=[junkb, ss])
                        c.op('dve', lambda e: e.tensor_scalar(out=ss.t[:, 10 + j:11 + j], in0=ss.t[:, 8 + j:9 + j], scalar1=1.0 / D,
                                                              scalar2=EPS, op0=ALU.mult, op1=ALU.add), reads=[ss], writes=[ss])
                        c.op('act', lambda e: e.activation(out=ss.t[:, 10 + j:11 + j], in_=ss.t[:, 10 + j:11 + j], func=AF.Sqrt),
                             reads=[ss], writes=[ss])
                        c.op('dve', lambda e: e.reciprocal(out=ss.t[:, 12 + j:13 + j], in_=ss.t[:, 10 + j:11 + j]), reads=[ss], writes=[ss])
                        c.op('dve', lambda e: e.tensor_scalar(out=xn[j].t[:], in0=x_.t[:, j, :], scalar1=ss.t[:, 12 + j:13 + j],
                                                              scalar2=None, op0=ALU.mult), reads=[x_, ss], writes=[xn[j]])
                        for cc in range(8):
                            c.op('pe', lambda e: e.transpose(out=ptr[j].t[:, cc, :], in_=xn[j].t[:, cc * 128:(cc + 1) * 128],
                                                             identity=ident), reads=[xn[j], cbf], writes=[ptr[j]])
                        c.op('act', lambda e: e.activation(out=h2.t[:, :, j * 128:(j + 1) * 128], in_=ptr[j].t[:], func=AF.Copy),
                             reads=[ptr[j]], writes=[h2])
                    c.op('pool', lambda e: e.tensor_tensor(out=h2.t[:], in0=h2.t[:], in1=g2col.unsqueeze(2).to_broadcast([128, 8, 256]),
                                                           op=ALU.mult), reads=[h2, vc], writes=[h2])
                    for f in range(32):
                        ps = psr.next()
                        for k in range(8):
                            c.op('pe', lambda e: e.matmul(ps.t[:, 0:256], lhsT=Wu.t[:, k, f * 128:(f + 1) * 128], rhs=h2.t[:, k, :],
                                                          start=(k == 0), stop=(k == 7)), reads=[Wu, h2], writes=[ps])
                        r_ = rl[f % 2]
                        c.op('act', lambda e: e.activation(out=r_.t[:, :], in_=ps.t[:, 0:256], func=AF.Relu), reads=[ps], writes=[r_])
                        c.op('pool' if f % 2 else 'dve', lambda e: e.tensor_tensor(out=uT.t[:, f, :], in0=r_.t[:, :], in1=r_.t[:, :], op=ALU.mult),
                             reads=[r_], writes=[(uT, f)])
                    for s in range(2):
                        pss = []
                        for hf in range(2):
                            ps = psr.next()
                            pss.append(ps)
                            for k in range(32):
                                c.op('pe', lambda e: e.matmul(ps.t[:, :], lhsT=uT.t[:, k, s * 128:(s + 1) * 128],
                                                              rhs=Wd.t[:, k, hf * 512:(hf + 1) * 512], start=(k == 0), stop=(k == 31)),
                                     reads=[Wd, uT], writes=[ps])
                            c.op('act', lambda e: e.activation(out=junkb.t[:, 0:512], in_=ps.t[:, :], func=AF.Square,
                                                               accum_out=ss.t[:, hf:hf + 1]), reads=[ps], writes=[junkb, ss])
                        res_tail(c, ss, pss, gpost, xt2, xt2.t[:, s, :], yt2, yt2.t[:, s, :])
                    c.dma('sp', y_d[ds(i * 256, 256), :].rearrange("(j p) d -> p j d", p=128), yt2[:], reads=[yt2])
                    c.end_iter()


def res_tail(c, ss, pss, gpost, xb, xv, yb, yv):
    c.op('dve', lambda e: e.tensor_tensor(out=ss.t[:, 2:3], in0=ss.t[:, 0:1], in1=ss.t[:, 1:2], op=ALU.add), reads=[ss], writes=[ss])
    c.op('dve', lambda e: e.tensor_scalar(out=ss.t[:, 2:3], in0=ss.t[:, 2:3], scalar1=1.0 / D, scalar2=EPS,
                                          op0=ALU.mult, op1=ALU.add), reads=[ss], writes=[ss])
    c.op('act', lambda e: e.activation(out=ss.t[:, 2:3], in_=ss.t[:, 2:3], func=AF.Sqrt), reads=[ss], writes=[ss])
    c.op('dve', lambda e: e.reciprocal(out=ss.t[:, 3:4], in_=ss.t[:, 2:3]), reads=[ss], writes=[ss])
    for hf in range(2):
        cs = slice(hf * 512, (hf + 1) * 512)
        c.op('dve', lambda e: e.scalar_tensor_tensor(out=yv[:, cs], in0=pss[hf].t[:, :], scalar=ss.t[:, 3:4], in1=gpost.t[:, cs],
                                                     op0=ALU.mult, op1=ALU.mult), reads=[pss[hf], ss, gpost], writes=[yb])
    c.op('pool', lambda e: e.tensor_tensor(out=yv, in0=yv, in1=xv, op=ALU.add), reads=[yb, xb], writes=[yb])


def _t5_bucket_np(rel):
    nb = 16
    ret = (rel > 0).astype(np.int64) * nb
    n = np.abs(rel)
    max_exact = 8
    is_small = n < max_exact
    nf = np.maximum(n, 1).astype(np.float32)
    large = max_exact + (np.log(nf / np.float32(max_exact)) / np.float32(math.log(128 / max_exact))
                         * np.float32(nb - max_exact)).astype(np.int64)
    large = np.minimum(large, nb - 1)
    return ret + np.where(is_small, n, large)


def _consts():
    cbf = np.zeros((128, 256), np.float32)
    cbf[:, 0:128] = np.eye(128)
    cbf[:, 128:256] = 1.0
    cf = np.zeros((128, 256), np.float32)
    cf[:, 0:128] = np.eye(128)[::-1]
    U = np.triu(np.ones((64, 64), np.float32))
    cf[0:64, 128:192] = U
    cf[0:64, 192:256] = U.T
    return cbf.astype(ml_dtypes.bfloat16), cf


def _oht(flip):
    d = J0 - np.arange(NJ)
    if flip:
        d = -d
    b = _t5_bucket_np(d)
    oh = np.zeros((32, NJ), np.float32)
    oh[b, np.arange(NJ)] = 1.0
    return oh


def _cols(v):
    return np.ascontiguousarray(v.reshape(8, 128).T)


def _vec_inputs(I, swap):
    lbf, lbb = I["lb_fwd"], I["lb_bwd"]
    if swap:
        lbf, lbb = lbb, lbf
    vc = np.zeros((128, 50), np.float32)
    vc[:, 0:8] = _cols(I["g_mix_pre"][0])
    vc[:, 8:16] = _cols(I["g_mlp_pre"][0])
    vc[:, 16:24] = _cols(lbf[0])
    vc[:, 24:32] = _cols(lbf[1])
    vc[:, 32:40] = _cols(lbb[0])
    vc[:, 40:48] = _cols(lbb[1])
    vc[:, 48] = I["g_attn_sub"][0]
    vc[:, 49] = I["g_hgrn_out"][0]
    row = np.concatenate([I["g_mix_post"][0], I["g_mlp_post"][0], lbf[0], lbf[1], lbb[0], lbb[1],
                          I["lam_q1"][0], I["lam_q2"][0], I["lam_k1"][0], I["lam_k2"][0]]).astype(np.float32)
    vb = np.ascontiguousarray(np.broadcast_to(row[None, :], (128, 6400)))
    return vc, vb


def make_in_maps(I, n_cores=8, TS=None, TP=None):
    I = {k: np.asarray(v) for k, v in I.items()}
    xs_all, xp_all = I["x_sample"], I["x_prompt"]
    w_in = np.ascontiguousarray(I["w_in"][0])
    cbf, cf = _consts()
    relb = np.ascontiguousarray(I["rel_bias"], dtype=np.float32)
    wf = [np.ascontiguousarray(w_in[:, 4096:6144]),
          np.ascontiguousarray(np.concatenate([w_in[:, 5120:6144], w_in[:, 4096:5120]], axis=1))]
    vcs, vbs = _vec_inputs(I, False)
    vcp = [vcs, _vec_inputs(I, True)[0]]
    vbp = [vbs, _vec_inputs(I, True)[1]]
    oht = [_oht(False), _oht(True)]

    def cfar(flip):
        a, b = (31, 15) if not flip else (15, 31)
        out = np.zeros((8, 128, 2), np.float32)
        out[:, :, 0] = relb[a][:, None]
        out[:, :, 1] = relb[b][:, None]
        return out
    cfars = [cfar(False), cfar(True)]
    shared = {"w_in": w_in, "w_pa": np.ascontiguousarray(I["w_proj_a"][0]), "w_pb": np.ascontiguousarray(I["w_proj_b"][0]),
              "w_o": np.ascontiguousarray(I["w_out"][0]), "w_up": np.ascontiguousarray(I["w_mlp_up"][0]),
              "w_dn": np.ascontiguousarray(I["w_mlp_down"][0]), "relb": relb, "cbf": cbf, "cf": cf,
              "vc_s": vcs, "vb_s": vbs, "oht_s": oht[0], "cfar_s": cfars[0]}
    maps = []
    for cid in range(n_cores):
        par = cid % 2
        xp = xp_all[(cid // 2) % xp_all.shape[0]]
        if par:
            xp = xp[::-1]
        m = dict(shared)
        m.update({"xs": np.ascontiguousarray(xs_all[cid % xs_all.shape[0]]), "xp": np.ascontiguousarray(xp),
                  "w_fsw": wf[par], "vc_p": vcp[par], "vb_p": vbp[par], "oht_p": oht[par], "cfar_p": cfars[par]})
        maps.append(m)
    return maps


_NC_CACHE = {}


def kernel(**inputs):
    TS = inputs["x_sample"].shape[1]
    TP = inputs["x_prompt"].shape[1]
    key = (TS, TP)
    if key not in _NC_CACHE:
        _NC_CACHE[key] = build(TS, TP)
    nc = _NC_CACHE[key]
    maps = make_in_maps(inputs)
    res = run_bass_kernel_spmd(nc, maps, core_ids=list(range(8)))
    B, Bs = inputs["x_prompt"].shape[0], inputs["x_sample"].shape[0]
    y_s = np.stack([np.asarray(res.results[cid]["ys"], dtype=np.float32) for cid in range(Bs)], axis=0)
    y_p = np.zeros((B, TP, D), np.float32)
    hq = TP // 2
    for cid in range(8):
        b, par = cid // 2, cid % 2
        yp = np.asarray(res.results[cid]["yp"], dtype=np.float32)
        if par == 0:
            y_p[b, 0:hq] = yp
        else:
            y_p[b, hq:] = yp[::-1]
    return (y_p, y_s)
```

```python
import math
import numpy as np
import ml_dtypes
from contextlib import ExitStack
import concourse.bass as bass
import concourse.mybir as mybir
from concourse.bass_utils import run_bass_kernel_spmd

F32 = mybir.dt.float32
BF16 = mybir.dt.bfloat16
AF = mybir.ActivationFunctionType
ALU = mybir.AluOpType
AX = mybir.AxisListType
NDMA = 16
EPS = 1e-6
D = 1024
NJ = 1280
J0 = 639


class Buf:
    def __init__(self, ctx, t):
        self.t = t
        self.wr = None
        self.rd = []
        self.subs = {}
        ctx.bufs.append(self)

    def __getitem__(self, k):
        return self.t[k]


class EngW:
    def __init__(self, name, eng, sem, same_sync):
        self.name, self.eng, self.sem = name, eng, sem
        self.count = 0
        self.waited = {}
        self.same_sync = same_sync

    def wait(self, deps):
        for d in deps:
            if d is None:
                continue
            sem, val = d
            if val <= 0:
                continue
            if sem is self.sem and not self.same_sync:
                continue
            if self.waited.get(sem, 0) >= val:
                continue
            self.eng.wait_ge(sem, val)
            self.waited[sem] = val


class Ctx:
    def __init__(self, nc, es):
        self.nc = nc
        self.E = {}
        self.bufs = []
        for name, eng, ss in [('pe', nc.tensor, False), ('act', nc.scalar, True),
                              ('dve', nc.vector, True), ('pool', nc.gpsimd, True),
                              ('sp', nc.sync, False)]:
            sem = es.enter_context(nc.semaphore('s_' + name))
            self.E[name] = EngW(name, eng, sem, ss)
        self.dma_sems = [es.enter_context(nc.semaphore('d%d' % i)) for i in range(2 * NDMA)]
        self.dma_val = [0] * (2 * NDMA)
        self.dma_next = [0, 0]

    @staticmethod
    def _bk(x):
        return x if isinstance(x, tuple) else (x, None)

    def _deps(self, reads, writes, extra):
        deps = list(extra)
        for x in reads:
            b, k = self._bk(x)
            deps.append(b.wr)
            if k is None:
                for sw, srd in b.subs.values():
                    deps.append(sw)
            elif k in b.subs:
                deps.append(b.subs[k][0])
        for x in writes:
            b, k = self._bk(x)
            deps.append(b.wr)
            deps.extend(b.rd)
            if k is None:
                for sw, srd in b.subs.values():
                    deps.append(sw)
                    deps.extend(srd)
            elif k in b.subs:
                deps.append(b.subs[k][0])
                deps.extend(b.subs[k][1])
        return deps

    def _mark(self, tok, reads, writes):
        for x in reads:
            b, k = self._bk(x)
            if k is None:
                b.rd.append(tok)
            else:
                b.subs.setdefault(k, [None, []])[1].append(tok)
        for x in writes:
            b, k = self._bk(x)
            if k is None:
                b.wr = tok
                b.rd = []
                b.subs = {}
            else:
                b.subs[k] = [tok, []]

    def op(self, en, fn, reads=(), writes=(), extra=()):
        e = self.E[en]
        e.wait(self._deps(reads, writes, extra))
        ins = fn(e.eng)
        e.count += 1
        ins.then_inc(e.sem, 1)
        tok = (e.sem, e.count)
        self._mark(tok, reads, writes)
        return tok

    def dma(self, en, out, in_, reads=(), writes=(), extra=()):
        e = self.E[en]
        grp = 1 if en == 'pool' else 0
        k = grp * NDMA + self.dma_next[grp]
        self.dma_next[grp] = (self.dma_next[grp] + 1) % NDMA
        sem = self.dma_sems[k]
        e.wait(self._deps(reads, writes, extra) + [(sem, self.dma_val[k])])
        ins = e.eng.dma_start(out=out, in_=in_)
        self.dma_val[k] += 16
        ins.then_inc(sem, 16)
        tok = (sem, self.dma_val[k])
        self._mark(tok, reads, writes)
        return tok

    def end_iter(self):
        nc = self.nc
        sp = self.E['sp']
        sp.wait([(s, v) for s, v in zip(self.dma_sems, self.dma_val)])
        nc.all_engine_barrier()
        for e in self.E.values():
            if e.count:
                nc.sync.sem_clear(e.sem)
        for s, v in zip(self.dma_sems[:NDMA], self.dma_val[:NDMA]):
            if v:
                nc.sync.sem_clear(s)
        nc.all_engine_barrier()
        for e in self.E.values():
            e.count = 0
            e.waited = {}
        self.dma_val = [0] * NDMA + self.dma_val[NDMA:]
        self.dma_next = [0, self.dma_next[1]]
        for b in self.bufs:
            b.wr = None
            b.rd = []
            b.subs = {}


def ds(start, size):
    if isinstance(start, int):
        return slice(start, start + size)
    return bass.ds(start, size)


import os
_LOOPK = [0]


class _Stop(Exception):
    pass


def _en(n):
    ph = os.environ.get("KPH")
    return ph is None or str(n) in ph.split(",")


def loop_iter(nc, n, static):
    k = _LOOPK[0] % 7
    _LOOPK[0] += 1
    en = os.environ.get("KLOOPS")
    if en is not None and str(k) not in en.split(","):
        return
    if static:
        for i in range(n):
            yield i
    else:
        with nc.Fori(0, n) as i:
            yield i


_PROBE = [0]


def free_regs(eng):
    regs = []
    try:
        for k in range(200):
            _PROBE[0] += 1
            regs.append(eng.alloc_register("probe%d" % _PROBE[0]))
    except Exception:
        pass
    for r in regs:
        eng.free_register(r)
    return len(regs)


_UID = [0]


class Pool_:
    def __init__(self, c, nc):
        self.c, self.nc = c, nc
        self.es = ExitStack()
        self.n = 0

    def __enter__(self):
        self.es.__enter__()
        return self

    def __exit__(self, *a):
        return self.es.__exit__(*a)

    def sb(self, shape, dt, name=None):
        self.n += 1
        _UID[0] += 1
        t = self.es.enter_context(self.nc.sbuf_tensor("%s_%d" % (name or "t", _UID[0]), shape, dt))
        return Buf(self.c, t)

    def ps(self, shape, dt=F32, name=None):
        self.n += 1
        _UID[0] += 1
        t = self.es.enter_context(self.nc.psum_tensor("%s_%d" % (name or "p", _UID[0]), shape, dt))
        return Buf(self.c, t)


def build(TS, TP, dbg=False):
    nc = bass.Bass("TRN2", target_bir_lowering=False)
    TM = max(TS, TP)
    TQP = TP // 2

    def din(name, shape, dt=F32):
        return nc.dram_tensor(name, shape, dt, kind="ExternalInput").ap()

    def dscr(name, shape, dt):
        return nc.dram_tensor(name, shape, dt, kind="ExternalOutput" if dbg else "Internal").ap()

    xs = din("xs", [TS, D])
    xp = din("xp", [TP, D])
    w_in = din("w_in", [D, 10 * D])
    w_fsw = din("w_fsw", [D, 2 * D])
    w_pa = din("w_pa", [D, D])
    w_pb = din("w_pb", [D, D])
    w_o = din("w_o", [D, D])
    w_up = din("w_up", [D, 4 * D])
    w_dn = din("w_dn", [4 * D, D])
    vc_in = {"s": din("vc_s", [128, 50]), "p": din("vc_p", [128, 50])}
    vb_in = {"s": din("vb_s", [128, 6400]), "p": din("vb_p", [128, 6400])}
    oht_in = {"s": din("oht_s", [32, NJ]), "p": din("oht_p", [32, NJ])}
    cfar_in = {"s": din("cfar_s", [8, 128, 2]), "p": din("cfar_p", [8, 128, 2])}
    relb_in = din("relb", [32, 8])
    cbf_in = din("cbf", [128, 256], BF16)
    cf_in = din("cf", [128, 384])
    ys = nc.dram_tensor("ys", [TS, D], F32, kind="ExternalOutput").ap()
    yp = nc.dram_tensor("yp", [TQP, D], F32, kind="ExternalOutput").ap()

    hT_d = dscr("hT_d", [8, 128, TM], BF16)
    aqT_d = dscr("aqT_d", [8, 128, TM], BF16)
    akT_d = dscr("akT_d", [8, 128, TM], BF16)
    gqT_d = dscr("gqT_d", [8, 128, TM], BF16)
    kT_d = [dscr("kfT_d", [8, 128, TM], BF16), dscr("kbT_d", [8, 128, TM], BF16)]
    av_d = dscr("av_d", [TM, D], BF16)
    gi_d = dscr("gi_d", [TM, D], BF16)
    ktm_d = [dscr("kftm_d", [TM, D], BF16), dscr("kbtm_d", [TM, D], BF16)]
    g_d = [dscr("gf_d", [TM, D], F32), dscr("gb_d", [TM, D], F32)]
    gogT_d = dscr("gogT_d", [8, 128, TM], BF16)
    gaT_d = dscr("gaT_d", [8, 128, TM], BF16)
    gbT_d = dscr("gbT_d", [8, 128, TM], BF16)
    oaT_d = dscr("oaT_d", [8, 128, TM], BF16)
    oT_d = [dscr("ofT_d", [8, 128, TM], F32), dscr("obT_d", [8, 128, TM], F32)]
    x1_d = dscr("x1_d", [TM, D], F32)
    tab_d = dscr("tab_d", [8, NJ], F32)
    bias_d = dscr("bias_d", [48, 128, 512], F32)

    with ExitStack() as es:
        c = Ctx(nc, es)

        with Pool_(c, nc) as gp:
            cbf = gp.sb([128, 256], BF16, "cbf")
            cf = gp.sb([128, 384], F32, "cf")
            c.dma('sp', cbf[:], cbf_in[:, :], writes=[cbf])
            c.dma('sp', cf[:], cf_in[:, :], writes=[cf])
            ident = cbf.t[:, 0:128]
            ones_bf = cbf.t[:, 128:256]
            antiI = cf.t[:, 0:128]
            U_f = cf.t[0:64, 128:192]
            Ut_f = cf.t[0:64, 192:256]
            ones_f = cf.t[:, 256:384]
            c.end_iter()

            for job in os.environ.get("KJOBS", "sp"):
                T = TS if job == "s" else TP
                Tq = TS if job == "s" else TQP
                x_d = xs if job == "s" else xp
                y_d = ys if job == "s" else yp
                run_job(nc, c, job, T, Tq, x_d, y_d, locals())
    print('free regs:', {n: free_regs(e.eng) for n, e in c.E.items()}, 'ninstr', nc.n_instructions if not callable(nc.n_instructions) else nc.n_instructions())
    return nc


def run_job(nc, c, job, T, Tq, x_d, y_d, G):
    ident, ones_bf, antiI, U_f, Ut_f, ones_f = G["ident"], G["ones_bf"], G["antiI"], G["U_f"], G["Ut_f"], G["ones_f"]
    cbf, cf = G["cbf"], G["cf"]
    vc_d, vb_d, oht_d, cfar_d = G["vc_in"][job], G["vb_in"][job], G["oht_in"][job], G["cfar_in"][job]
    w_in, w_fsw = G["w_in"], G["w_fsw"]
    hT_d, aqT_d, akT_d, gqT_d, kT_d = G["hT_d"], G["aqT_d"], G["akT_d"], G["gqT_d"], G["kT_d"]
    av_d, gi_d, ktm_d, g_d = G["av_d"], G["gi_d"], G["ktm_d"], G["g_d"]
    gogT_d, gaT_d, gbT_d, oaT_d, oT_d, x1_d = G["gogT_d"], G["gaT_d"], G["gbT_d"], G["oaT_d"], G["oT_d"], G["x1_d"]
    tab_d, bias_d, relb_in = G["tab_d"], G["bias_d"], G["relb_in"]
    NB = T // 512
    NBQ = Tq // 512
    static = (job == "p")

    with Pool_(c, nc) as jp:
        vc = jp.sb([128, 50], F32, "vc")
        c.dma('sp', vc[:], vc_d[:, :], writes=[vc])
        der = jp.sb([128, 32], F32, "der")
        lamt = jp.sb([128, 256], F32, "lamt")
        c.dma('sp', lamt[:], vb_d[:, 6144:6400], writes=[lamt])
        c.op('dve', lambda e: e.tensor_tensor(out=der.t[:, 0:8], in0=vc.t[:, 24:32], in1=vc.t[:, 16:24], op=ALU.subtract),
             reads=[vc], writes=[der])
        c.op('dve', lambda e: e.tensor_tensor(out=der.t[:, 8:16], in0=vc.t[:, 40:48], in1=vc.t[:, 32:40], op=ALU.subtract),
             reads=[vc], writes=[der])
        c.op('act', lambda e: e.activation(out=der.t[:, 0:16], in_=der.t[:, 0:16], func=AF.Sigmoid), reads=[der], writes=[der])
        c.op('dve', lambda e: e.tensor_scalar(out=der.t[:, 16:17], in0=vc.t[:, 48:49], scalar1=0.8, scalar2=None, op0=ALU.mult),
             reads=[vc], writes=[der])
        c.op('dve', lambda e: e.tensor_tensor(out=lamt.t[:, 0:128], in0=lamt.t[:, 0:128], in1=lamt.t[:, 128:256], op=ALU.mult),
             reads=[lamt], writes=[lamt])
        c.op('dve', lambda e: e.reduce_sum(out=der.t[:, 20:22], in_=lamt.t[:, 0:128].rearrange("p (a b) -> p a b", a=2), axis=AX.X),
             reads=[lamt, der], writes=[der])
        c.op('act', lambda e: e.activation(out=der.t[:, 20:22], in_=der.t[:, 20:22], func=AF.Exp), reads=[der], writes=[der])
        c.op('dve', lambda e: e.tensor_tensor(out=der.t[:, 22:23], in0=der.t[:, 21:22], in1=der.t[:, 20:21], op=ALU.subtract),
             reads=[der], writes=[der])
        c.op('dve', lambda e: e.tensor_scalar(out=der.t[:, 22:23], in0=der.t[:, 22:23], scalar1=-0.2, scalar2=None, op0=ALU.add),
             reads=[der], writes=[der])
        c.end_iter()
        g1col = vc.t[:, 0:8]
        g2col = vc.t[:, 8:16]
        omlb_col = [der.t[:, 0:8], der.t[:, 8:16]]
        gsubcol = der.t[:, 16:17]
        ghcol = vc.t[:, 49:50]
        neglam = der.t[:, 22:23]

        if _en(0):
            with Pool_(c, nc) as p:
                oht = p.sb([32, NJ], F32)
                relb = p.sb([32, 8], F32)
                tabs = p.sb([8, NJ], F32)
                c.dma('sp', oht[:], oht_d[:, :], writes=[oht])
                c.dma('sp', relb[:], relb_in[:, :], writes=[relb])
                pt = p.ps([8, 512], F32)
                for k0 in range(0, NJ, 512):
                    n = min(512, NJ - k0)
                    c.op('pe', lambda e: e.matmul(pt.t[:, 0:n], lhsT=relb.t[:, :], rhs=oht.t[:, k0:k0 + n], start=True, stop=True),
                         reads=[relb, oht], writes=[pt])
                    c.op('act', lambda e: e.activation(out=tabs.t[:, k0:k0 + n], in_=pt.t[:, 0:n], func=AF.Copy),
                         reads=[pt], writes=[tabs])
                c.dma('sp', tab_d[:, :], tabs[:], reads=[tabs])
                c.end_iter()
                hk = [p.sb([128, 512], F32) for _ in range(2)]
                bt = [p.sb([128, 512], F32) for _ in range(2)]
                pb = [p.ps([128, 512], F32) for _ in range(2)]
                n = 0
                for h in range(8):
                    for di in range(6):
                        delta = -128 + 128 * di
                        off = J0 - 127 - delta
                        src = bass.AP(tab_d.tensor, h * NJ + off, [[1, 128], [1, 512]])
                        c.dma('sp', hk[n % 2][:], src, writes=[hk[n % 2]])
                        c.op('pe', lambda e: e.matmul(pb[n % 2].t[:, :], lhsT=antiI, rhs=hk[n % 2].t[:, :], start=True, stop=True),
                             reads=[hk[n % 2], cf], writes=[pb[n % 2]])
                        c.op('act', lambda e: e.activation(out=bt[n % 2].t[:, :], in_=pb[n % 2].t[:, :], func=AF.Copy),
                             reads=[pb[n % 2]], writes=[bt[n % 2]])
                        c.dma('sp', bias_d[h * 6 + di, :, :], bt[n % 2][:], reads=[bt[n % 2]])
                        n += 1
                c.end_iter()

        def norm_transpose(p, xt_list, hts, gcol, scr):
            junk, xn, ss, ptr = scr
            for j in range(4):
                c.op('act', lambda e: e.activation(out=junk.t[:], in_=xt_list.t[:, j, :], func=AF.Square,
                                                   accum_out=ss.t[:, j:j + 1]),
                     reads=[xt_list], writes=[junk, ss])
                c.op('dve', lambda e: e.tensor_scalar(out=ss.t[:, 4 + j:5 + j], in0=ss.t[:, j:j + 1], scalar1=1.0 / D,
                                                      scalar2=EPS, op0=ALU.mult, op1=ALU.add), reads=[ss], writes=[ss])
                c.op('act', lambda e: e.activation(out=ss.t[:, 4 + j:5 + j], in_=ss.t[:, 4 + j:5 + j], func=AF.Sqrt),
                     reads=[ss], writes=[ss])
                c.op('dve', lambda e: e.reciprocal(out=ss.t[:, 8 + j:9 + j], in_=ss.t[:, 4 + j:5 + j]), reads=[ss], writes=[ss])
                c.op('dve', lambda e: e.tensor_scalar(out=xn[j % 2].t[:], in0=xt_list.t[:, j, :], scalar1=ss.t[:, 8 + j:9 + j],
                                                      scalar2=None, op0=ALU.mult),
                     reads=[xt_list, ss], writes=[xn[j % 2]])
                for cc in range(8):
                    c.op('pe', lambda e: e.transpose(out=ptr[j % 2].t[:, cc, :], in_=xn[j % 2].t[:, cc * 128:(cc + 1) * 128],
                                                     identity=ident),
                         reads=[xn[j % 2], cbf], writes=[ptr[j % 2]])
                c.op('act', lambda e: e.activation(out=hts.t[:, :, j * 128:(j + 1) * 128], in_=ptr[j % 2].t[:], func=AF.Copy),
                     reads=[ptr[j % 2]], writes=[hts])
            c.op('pool', lambda e: e.tensor_tensor(out=hts.t[:], in0=hts.t[:], in1=gcol.unsqueeze(2).to_broadcast([128, 8, 512]),
                                                   op=ALU.mult), reads=[hts, vc], writes=[hts])

        def nt_scratch(p):
            return (p.sb([128, 1024], BF16), [p.sb([128, 1024], BF16) for _ in range(2)], p.sb([128, 12], F32),
                    [p.ps([128, 8, 128], BF16) for _ in range(2)])

        if _en(1):
            with Pool_(c, nc) as p:
                xt4 = p.sb([128, 4, 1024], F32)
                hts = p.sb([128, 8, 512], BF16)
                scr = nt_scratch(p)

                for i in loop_iter(nc, NB, static):
                    c.dma('sp', xt4[:], x_d[ds(i * 512, 512), :].rearrange("(j p) d -> p j d", p=128), writes=[xt4])
                    norm_transpose(p, xt4, hts, g1col, scr)
                    c.dma('sp', hT_d[:, :, ds(i * 512, 512)].rearrange("c p t -> p c t"), hts[:], reads=[hts])
                    c.end_iter()

        def load_w(p, src_ap, ncols=1024, nk=8):
            w = p.sb([128, nk, ncols], BF16, "w")
            for kc in range(nk):
                for c0 in range(0, ncols, 1024):
                    c.dma('pool', w.t[:, kc, c0:c0 + 1024], src_ap[kc * 128:(kc + 1) * 128, c0:c0 + 1024], writes=[w])
            return w

        class PsRot:
            def __init__(self, p, n):
                self.b = [p.ps([128, 512], F32) for _ in range(n)]
                self.i = 0

            def next(self):
                b = self.b[self.i % len(self.b)]
                self.i += 1
                return b

        def proj_cm(w, act, psr, evac, ncol_chunks=8, nk=8):
            for j in range(ncol_chunks):
                ps = psr.next()
                for k in range(nk):
                    c.op('pe', lambda e: e.matmul(ps.t[:, :], lhsT=w.t[:, k, j * 128:(j + 1) * 128], rhs=act.t[:, k, :],
                                                  start=(k == 0), stop=(k == nk - 1)),
                         reads=[w, act], writes=[ps])
                evac(j, ps)

        def proj_tm(w, act, psr, evac, ncols=1024, nk=8, ntok=4):
            for s in range(ntok):
                for hf in range(ncols // 512):
                    ps = psr.next()
                    for k in range(nk):
                        c.op('pe', lambda e: e.matmul(ps.t[:, :], lhsT=act.t[:, k, s * 128:(s + 1) * 128],
                                                      rhs=w.t[:, k, hf * 512:(hf + 1) * 512],
                                                      start=(k == 0), stop=(k == nk - 1)),
                             reads=[w, act], writes=[ps])
                    evac(s, hf, ps)

        wf_src = [w_in[:, 4096:5120], w_in[:, 5120:6144]] if job == "s" else [w_fsw[:, 0:1024], w_fsw[:, 1024:2048]]

        if _en(2):
            with Pool_(c, nc) as p:
                W = {"aq": load_w(p, w_in[:, 0:1024]), "ak": load_w(p, w_in[:, 1024:2048]),
                     "av": load_w(p, w_in[:, 2048:3072]), "gq": load_w(p, w_in[:, 3072:4096]),
                     "f0": load_w(p, wf_src[0]), "f1": load_w(p, wf_src[1]), "gi": load_w(p, w_in[:, 6144:7168])}
                lbB = p.sb([128, 4, 1024], F32, "lbB")
                c.dma('sp', lbB[:], vb_d[:, 2048:6144].rearrange("p (a b) -> p a b", a=4), writes=[lbB])
                with Pool_(c, nc) as tp_:
                    dtmp = tp_.sb([128, 2, 1024], F32, "dtmp")
                    for d_ in range(2):
                        c.op('dve', lambda e: e.tensor_tensor(out=dtmp.t[:, d_, :], in0=lbB.t[:, 2 * d_, :], in1=lbB.t[:, 2 * d_ + 1, :],
                                                              op=ALU.subtract), reads=[lbB], writes=[dtmp])
                    for d_ in range(2):
                        c.op('act', lambda e: e.activation(out=lbB.t[:, 2 * d_, :], in_=dtmp.t[:, d_, :], func=AF.Sigmoid),
                             reads=[dtmp], writes=[lbB])
                        c.op('act', lambda e: e.activation(out=lbB.t[:, 2 * d_ + 1, :], in_=dtmp.t[:, d_, :], func=AF.Sigmoid, scale=-1.0),
                             reads=[dtmp], writes=[lbB])
                    c.end_iter()
                hts = p.sb([128, 8, 512], BF16, "hts")
                stg_cm = [p.sb([128, 8, 512], BF16, "stgcm") for _ in range(2)]
                stg_bf = [p.sb([128, 1024], BF16, "stgbf") for _ in range(4)]
                stg_f = [p.sb([128, 1024], F32, "stgf") for _ in range(4)]
                sig = [p.sb([128, 512], F32, "sig") for _ in range(8)]
                psr = PsRot(p, 8)
                cnt = {"cm": 0, "tm": 0, "sg": 0}
                tq = 'sp' if static else 'act'
                for i in loop_iter(nc, NB, static):
                    c.dma('sp', hts[:], hT_d[:, :, ds(i * 512, 512)].rearrange("c p t -> p c t"), writes=[hts])

                    def cm_plain(wname, dst):
                        st = stg_cm[cnt["cm"] % 2]
                        cnt["cm"] += 1
                        eng = ['act', 'dve']

                        def ev(j, ps):
                            if j % 2 == 0:
                                c.op('act', lambda e: e.activation(out=st.t[:, j, :], in_=ps.t[:, :], func=AF.Copy),
                                     reads=[ps], writes=[(st, j)])
                            else:
                                c.op('dve', lambda e: e.tensor_copy(out=st.t[:, j, :], in_=ps.t[:, :]), reads=[ps], writes=[(st, j)])
                        proj_cm(W[wname], hts, psr, ev)
                        c.dma('sp', dst[:, :, ds(i * 512, 512)].rearrange("c p t -> p c t"), st[:], reads=[st])

                    def tm_plain(wname, dst):
                        sts = {}

                        def ev(s, hf, ps):
                            if hf == 0:
                                sts[s] = stg_bf[cnt["tm"] % 4]
                                cnt["tm"] += 1
                            st = sts[s]
                            if hf == 0:
                                c.op('act', lambda e: e.activation(out=st.t[:, 0:512], in_=ps.t[:, :], func=AF.Copy),
                                     reads=[ps], writes=[(st, 0)])
                            else:
                                c.op('dve', lambda e: e.tensor_copy(out=st.t[:, 512:1024], in_=ps.t[:, :]), reads=[ps], writes=[(st, 1)])
                                c.dma(tq, dst[ds(i * 512 + s * 128, 128), :], st[:], reads=[st])
                        proj_tm(W[wname], hts, psr, ev)

                    def f_cm(d_):
                        st = stg_cm[cnt["cm"] % 2]
                        cnt["cm"] += 1

                        def ev(j, ps):
                            sg = sig[cnt["sg"] % 8]
                            cnt["sg"] += 1
                            c.op('act', lambda e: e.activation(out=sg.t[:, :], in_=ps.t[:, :], func=AF.Sigmoid, scale=-1.0),
                                 reads=[ps], writes=[sg])
                            c.op('dve', lambda e: e.tensor_scalar(out=st.t[:, j, :], in0=sg.t[:, :], scalar1=omlb_col[d_][:, j:j + 1],
                                                                  scalar2=None, op0=ALU.mult), reads=[sg, der], writes=[(st, j)])
                        proj_cm(W["f%d" % d_], hts, psr, ev)
                        c.dma('sp', kT_d[d_][:, :, ds(i * 512, 512)].rearrange("c p t -> p c t"), st[:], reads=[st])

                    def f_tm(d_):
                        sts = {}

                        def ev(s, hf, ps):
                            if hf == 0:
                                sts[s] = (stg_bf[cnt["tm"] % 4], stg_f[cnt["tm"] % 4])
                                cnt["tm"] += 1
                            sb_, sf_ = sts[s]
                            sg = sig[cnt["sg"] % 8]
                            cnt["sg"] += 1
                            cs = slice(hf * 512, (hf + 1) * 512)
                            c.op('act', lambda e: e.activation(out=sg.t[:, :], in_=ps.t[:, :], func=AF.Sigmoid), reads=[ps], writes=[sg])
                            c.op('dve', lambda e: e.tensor_tensor(out=sg.t[:, :], in0=sg.t[:, :], in1=lbB.t[:, 2 * d_ + 1, cs], op=ALU.mult),
                                 reads=[sg, lbB], writes=[sg])
                            c.op('dve', lambda e: e.tensor_tensor(out=sg.t[:, :], in0=sg.t[:, :], in1=lbB.t[:, 2 * d_, cs], op=ALU.add),
                                 reads=[sg, lbB], writes=[sg])
                            def post(sg=sg, sf_=sf_, sb_=sb_, cs=cs, hf=hf, s=s):
                                c.op('act', lambda e: e.activation(out=sf_.t[:, cs], in_=sg.t[:, :], func=AF.Ln), reads=[sg], writes=[(sf_, hf)])
                                c.op('pool', lambda e: e.tensor_scalar(out=sb_.t[:, cs], in0=sg.t[:, :], scalar1=-1.0, scalar2=1.0,
                                                                       op0=ALU.mult, op1=ALU.add), reads=[sg], writes=[(sb_, hf)])
                                if hf == 1:
                                    c.dma(tq, g_d[d_][ds(i * 512 + s * 128, 128), :], sf_[:], reads=[sf_])
                                    c.dma(tq, ktm_d[d_][ds(i * 512 + s * 128, 128), :], sb_[:], reads=[sb_])
                            posts.append(post)
                        posts = []
                        proj_tm(W["f%d" % d_], hts, psr, ev)
                        for po in posts:
                            po()

                    cm_plain("aq", aqT_d)
                    cm_plain("ak", akT_d)
                    tm_plain("av", av_d)
                    cm_plain("gq", gqT_d)
                    tm_plain("gi", gi_d)
                    f_cm(0)
                    f_cm(1)
                    f_tm(0)
                    f_tm(1)
                    c.end_iter()

        if _en(3):
            with Pool_(c, nc) as p:
                W = {"og": load_w(p, w_in[:, 7168:8192]), "ga": load_w(p, w_in[:, 8192:9216]), "gb": load_w(p, w_in[:, 9216:10240])}
                hts = p.sb([128, 8, 512], BF16, "hts")
                stg_cm = [p.sb([128, 8, 512], BF16, "stgcm") for _ in range(2)]
                psr = PsRot(p, 8)
                c.end_iter()
                for i in loop_iter(nc, NBQ, static):
                    c.dma('sp', hts[:], hT_d[:, :, ds(i * 512, 512)].rearrange("c p t -> p c t"), writes=[hts])
                    for n_, (wname, dst, fn) in enumerate((("ga", gaT_d, AF.Sigmoid), ("gb", gbT_d, AF.Sigmoid), ("og", gogT_d, AF.Silu))):
                        st = stg_cm[n_ % 2]

                        def ev(j, ps):
                            c.op('act', lambda e: e.activation(out=st.t[:, j, :], in_=ps.t[:, :], func=fn), reads=[ps], writes=[(st, j)])
                        proj_cm(W[wname], hts, psr, ev)
                        c.dma('sp', dst[:, :, ds(i * 512, 512)].rearrange("c p t -> p c t"), st[:], reads=[st])
                    c.end_iter()

        NKB = T // 128
        if _en(4):
            with Pool_(c, nc) as p:
                kT = p.sb([128, 1, T], BF16, "kT")
                vv = p.sb([128, NKB, 128], BF16, "vv")
                biasT = p.sb([128, 6, 512], F32, "biasT")
                cfar = p.sb([128, 1, 2], F32, "cfar")
                qT = p.sb([128, 1, Tq], BF16, "qT")
                pS = [[p.ps([128, 512], F32, "pS") for _ in range(2)] for _ in range(2)]
                pO = [p.ps([128, 512], F32, "pO") for _ in range(2)]
                pZ = [p.ps([128, 512], F32, "pZ") for _ in range(2)]
                pT = [[p.sb([128, 512], BF16, "pT") for _ in range(4)] for _ in range(2)]
                tmpS = [p.sb([128, 512], F32, "tmpS") for _ in range(2)]
                zacc = [p.sb([128, 512], F32, "zacc") for _ in range(2)]
                rz = [p.sb([128, 512], F32, "rz") for _ in range(2)]
                o01 = [p.sb([128, 512], F32, "o01") for _ in range(2)]
                osq = p.sb([128, 512], BF16, "osq")
                rs = p.sb([128, 512], F32, "rs")
                oout = p.sb([128, 1, Tq], BF16, "oout")
                for h in loop_iter(nc, 8, static):
                    c.dma('sp', kT[:], akT_d[ds(h, 1), :, 0:T].rearrange("a p t -> p a t"), writes=[kT])
                    c.dma('sp', vv[:], av_d[0:T, ds(h * 128, 128)].rearrange("(kb p) d -> p kb d", p=128), writes=[vv])
                    c.dma('sp', biasT[:], bias_d[ds(h * 6, 6), :, :].rearrange("d p q -> p d q"), writes=[biasT])
                    c.dma('sp', cfar[:], cfar_d[ds(h, 1), :, :].rearrange("a p s -> p a s"), writes=[cfar])
                    c.dma('sp', qT[:], aqT_d[ds(h, 1), :, 0:Tq].rearrange("a p t -> p a t"), writes=[qT])
                    for qb in range(Tq // 512):
                        q_ = qT
                        qsl = slice(qb * 512, (qb + 1) * 512)
                        def qk_exp(kb):
                            delta = kb * 128 - qb * 512
                            near = -128 <= delta <= 512
                            pts = []
                            pss = []
                            for m in range(2):
                                ps = pS[m][kb % 2]
                                pr = slice(m * 64, (m + 1) * 64)
                                c.op('pe', lambda e: e.matmul(ps.t[:, :], lhsT=kT.t[pr, 0, kb * 128:(kb + 1) * 128], rhs=q_.t[pr, 0, qsl],
                                                              start=True, stop=True), reads=[kT, q_], writes=[ps])
                                pss.append(ps)
                            for m in range(2):
                                ps = pss[m]
                                pt_ = pT[m][kb % 4]
                                if near:
                                    di = (delta + 128) // 128
                                    tm_ = tmpS[m]
                                    c.op('dve', lambda e: e.scalar_tensor_tensor(out=tm_.t[:, :], in0=ps.t[:, :], scalar=0.125,
                                                                                 in1=biasT.t[:, di, :], op0=ALU.mult, op1=ALU.add),
                                         reads=[ps, biasT], writes=[tm_])
                                    c.op('act', lambda e: e.activation(out=pt_.t[:, :], in_=tm_.t[:, :], func=AF.Exp),
                                         reads=[tm_], writes=[pt_])
                                else:
                                    side = 0 if delta > 0 else 1
                                    c.op('act', lambda e: e.activation(out=pt_.t[:, :], in_=ps.t[:, :], func=AF.Exp,
                                                                       bias=cfar.t[:, 0, side:side + 1], scale=0.125),
                                         reads=[ps, cfar], writes=[pt_])
                                pts.append(pt_)
                            return pts

                        def pv_z(kb, pts):
                            for m in range(2):
                                pt_ = pts[m]
                                c.op('pe', lambda e: e.matmul(pO[m].t[:, :], lhsT=vv.t[:, kb, :], rhs=pt_.t[:, :],
                                                              start=(kb == 0), stop=(kb == NKB - 1)), reads=[vv, pt_], writes=[pO[m]])
                                zeng = 'dve' if m == 0 else 'pool'
                                if kb == 0:
                                    c.op(zeng, lambda e: e.tensor_copy(out=zacc[m].t[:, :], in_=pt_.t[:, :]), reads=[pt_], writes=[zacc[m]])
                                else:
                                    c.op(zeng, lambda e: e.tensor_tensor(out=zacc[m].t[:, :], in0=zacc[m].t[:, :], in1=pt_.t[:, :], op=ALU.add),
                                         reads=[zacc[m], pt_], writes=[zacc[m]])

                        pend = qk_exp(0)
                        for kb in range(NKB):
                            nxt = qk_exp(kb + 1) if kb + 1 < NKB else None
                            pv_z(kb, pend)
                            pend = nxt
                        for m in range(2):
                            c.op('pe', lambda e: e.matmul(pZ[m].t[:, :], lhsT=ones_f, rhs=zacc[m].t[:, :], start=True, stop=True),
                                 reads=[cf, zacc[m]], writes=[pZ[m]])
                        for m in range(2):
                            c.op('dve', lambda e: e.reciprocal(out=rz[m].t[:, :], in_=pZ[m].t[:, :]), reads=[pZ[m]], writes=[rz[m]])
                            c.op('dve', lambda e: e.tensor_tensor(out=o01[m].t[:, :], in0=pO[m].t[:, :], in1=rz[m].t[:, :], op=ALU.mult),
                                 reads=[pO[m], rz[m]], writes=[o01[m]])
                        c.op('dve', lambda e: e.scalar_tensor_tensor(out=o01[0].t[:, :], in0=o01[1].t[:, :], scalar=neglam,
                                                                     in1=o01[0].t[:, :], op0=ALU.mult, op1=ALU.add),
                             reads=[o01[0], o01[1], der], writes=[o01[0]])
                        c.op('pool', lambda e: e.tensor_tensor(out=osq.t[:, :], in0=o01[0].t[:, :], in1=o01[0].t[:, :], op=ALU.mult),
                             reads=[o01[0]], writes=[osq])
                        c.op('pe', lambda e: e.matmul(pZ[0].t[:, :], lhsT=ones_bf, rhs=osq.t[:, :], start=True, stop=True),
                             reads=[cbf, osq], writes=[pZ[0]])
                        c.op('dve', lambda e: e.tensor_scalar(out=rs.t[:, :], in0=pZ[0].t[:, :], scalar1=1.0 / 128, scalar2=EPS,
                                                              op0=ALU.mult, op1=ALU.add), reads=[pZ[0]], writes=[rs])
                        c.op('act', lambda e: e.activation(out=rs.t[:, :], in_=rs.t[:, :], func=AF.Ln), reads=[rs], writes=[rs])
                        c.op('act', lambda e: e.activation(out=rs.t[:, :], in_=rs.t[:, :], func=AF.Exp, scale=-0.5), reads=[rs], writes=[rs])
                        oo = oout
                        c.op('dve', lambda e: e.scalar_tensor_tensor(out=oo.t[:, 0, qsl], in0=o01[0].t[:, :], scalar=gsubcol,
                                                                     in1=rs.t[:, :], op0=ALU.mult, op1=ALU.mult),
                             reads=[o01[0], rs, der], writes=[oo])
                    c.dma('sp', oaT_d[ds(h, 1), :, 0:Tq].rearrange("a p t -> p a t"), oout[:], reads=[oout])
                    c.end_iter()

        NG = T // 256
        if _en(5):
            with Pool_(c, nc) as p:
                S = [p.sb([128, 8, 128], F32, "S") for _ in range(2)]
                Sbf = [p.sb([128, 8, 128], BF16, "Sbf") for _ in range(2)]
                qTt = [p.sb([128, 8, 256], BF16, "qTt") for _ in range(2)]
                kTt = [p.sb([128, 8, 256], BF16, "kTt") for _ in range(2)]
                ktm = [p.sb([64, 4, 1024], BF16, "ktm") for _ in range(2)]
                gtm = [p.sb([64, 4, 1024], F32, "gtm") for _ in range(2)]
                vtm = [p.sb([64, 4, 1024], BF16, "vtm") for _ in range(2)]
                ostg = [p.sb([128, 8, 256], F32, "ostg") for _ in range(2)]
                ET = p.sb([128, 8, 64], F32, "ET")
                EiT = p.sb([128, 8, 64], F32, "EiT")
                Eitm = p.sb([64, 1024], F32, "Eitm")
                qt_ = p.sb([128, 8, 64], BF16, "qt_")
                kt_ = p.sb([128, 8, 64], BF16, "kt_")
                ktm_ = p.sb([64, 1024], BF16, "ktm_")
                ATm = p.sb([64, 8, 64], BF16, "ATm")
                Stmp = p.sb([128, 8, 128], F32, "Stmp")
                p_bT = p.ps([128, 8, 64], F32, "p_bT")
                p_btm = p.ps([64, 1024], F32, "p_btm")
                p_AT = p.ps([64, 8, 64], F32, "p_AT")
                p_oT = p.ps([128, 8, 64], F32, "p_oT")
                p_dS = p.ps([128, 8, 128], F32, "p_dS")
                for d_ in range(2):
                    c.op('pool', lambda e: e.memset(S[d_].t[:], 0.0), writes=[S[d_]])
                    c.op('pool', lambda e: e.memset(Sbf[d_].t[:], 0.0), writes=[Sbf[d_]])
                c.end_iter()
                dq = 'sp' if static else 'act'
                for i in loop_iter(nc, NG, static):
                    for d_ in range(2):
                        t0 = i * 256 if d_ == 0 else (NG - 1 - i) * 256
                        c.dma(dq, qTt[d_][:], gqT_d[:, :, ds(t0, 256)].rearrange("c p t -> p c t"), writes=[qTt[d_]])
                        c.dma(dq, kTt[d_][:], kT_d[d_][:, :, ds(t0, 256)].rearrange("c p t -> p c t"), writes=[kTt[d_]])
                        c.dma(dq, ktm[d_][:], ktm_d[d_][ds(t0, 256), :].rearrange("(a s) d -> s a d", s=64), writes=[ktm[d_]])
                        c.dma(dq, gtm[d_][:], g_d[d_][ds(t0, 256), :].rearrange("(a s) d -> s a d", s=64), writes=[gtm[d_]])
                        c.dma(dq, vtm[d_][:], gi_d[ds(t0, 256), :].rearrange("(a s) d -> s a d", s=64), writes=[vtm[d_]])
                    for cc in range(4):
                        for d_ in range(2):
                            ch = cc if d_ == 0 else 3 - cc
                            Um = U_f if d_ == 0 else Ut_f
                            last = 63 if d_ == 0 else 0
                            tsl = slice(ch * 64, (ch + 1) * 64)
                            for hh in range(8):
                                c.op('pe', lambda e: e.matmul(p_bT.t[:, hh, :], lhsT=gtm[d_].t[:, ch, hh * 128:(hh + 1) * 128], rhs=Um,
                                                              start=True, stop=True), reads=[gtm[d_], cf], writes=[p_bT])
                            for hf in range(2):
                                c.op('pe', lambda e: e.matmul(p_btm.t[:, hf * 512:(hf + 1) * 512], lhsT=Um,
                                                              rhs=gtm[d_].t[:, ch, hf * 512:(hf + 1) * 512], start=True, stop=True),
                                     reads=[gtm[d_], cf], writes=[p_btm])
                            c.op('act', lambda e: e.activation(out=ET.t[:], in_=p_bT.t[:], func=AF.Exp), reads=[p_bT], writes=[ET])
                            c.op('act', lambda e: e.activation(out=EiT.t[:], in_=p_bT.t[:], func=AF.Exp, scale=-1.0),
                                 reads=[p_bT], writes=[EiT])
                            c.op('act', lambda e: e.activation(out=Eitm.t[:], in_=p_btm.t[:], func=AF.Exp, scale=-1.0),
                                 reads=[p_btm], writes=[Eitm])
                            c.op('dve', lambda e: e.tensor_tensor(out=qt_.t[:], in0=qTt[d_].t[:, :, tsl], in1=ET.t[:], op=ALU.mult),
                                 reads=[qTt[d_], ET], writes=[qt_])
                            c.op('pool', lambda e: e.tensor_tensor(out=kt_.t[:], in0=kTt[d_].t[:, :, tsl], in1=EiT.t[:], op=ALU.mult),
                                 reads=[kTt[d_], EiT], writes=[kt_])
                            c.op('pool', lambda e: e.tensor_tensor(out=ktm_.t[:], in0=ktm[d_].t[:, ch, :], in1=Eitm.t[:], op=ALU.mult),
                                 reads=[ktm[d_], Eitm], writes=[ktm_])
                            for hh in range(8):
                                c.op('pe', lambda e: e.matmul(p_AT.t[:, hh, :], lhsT=kt_.t[:, hh, :], rhs=qt_.t[:, hh, :],
                                                              start=True, stop=True), reads=[kt_, qt_], writes=[p_AT])
                            c.op('dve', lambda e: e.tensor_tensor(out=ATm.t[:], in0=p_AT.t[:],
                                                                  in1=Um.unsqueeze(1).to_broadcast([64, 8, 64]), op=ALU.mult),
                                 reads=[p_AT, cf], writes=[ATm])
                            for hh in range(8):
                                c.op('pe', lambda e: e.matmul(p_oT.t[:, hh, :], lhsT=Sbf[d_].t[:, hh, :], rhs=qt_.t[:, hh, :],
                                                              start=True, stop=False), reads=[Sbf[d_], qt_], writes=[p_oT])
                                c.op('pe', lambda e: e.matmul(p_oT.t[:, hh, :], lhsT=vtm[d_].t[:, ch, hh * 128:(hh + 1) * 128],
                                                              rhs=ATm.t[:, hh, :], start=False, stop=True),
                                     reads=[vtm[d_], ATm], writes=[p_oT])
                            c.op('act', lambda e: e.activation(out=ostg[d_].t[:, :, tsl], in_=p_oT.t[:], func=AF.Copy),
                                 reads=[p_oT], writes=[ostg[d_]])
                            for hh in range(8):
                                c.op('pe', lambda e: e.matmul(p_dS.t[:, hh, :], lhsT=ktm_.t[:, hh * 128:(hh + 1) * 128],
                                                              rhs=vtm[d_].t[:, ch, hh * 128:(hh + 1) * 128], start=True, stop=True),
                                     reads=[ktm_, vtm[d_]], writes=[p_dS])
                            c.op('dve', lambda e: e.tensor_tensor(out=Stmp.t[:], in0=S[d_].t[:], in1=p_dS.t[:], op=ALU.add),
                                 reads=[S[d_], p_dS], writes=[Stmp])
                            c.op('dve', lambda e: e.tensor_tensor(out=S[d_].t[:], in0=Stmp.t[:],
                                                                  in1=ET.t[:, :, last:last + 1].to_broadcast([128, 8, 128]), op=ALU.mult),
                                 reads=[Stmp, ET], writes=[S[d_]])
                            c.op('pool', lambda e: e.tensor_copy(out=Sbf[d_].t[:], in_=S[d_].t[:]), reads=[S[d_]], writes=[Sbf[d_]])
                    for d_ in range(2):
                        t0 = i * 256 if d_ == 0 else (NG - 1 - i) * 256
                        c.dma(dq, oT_d[d_][:, :, ds(t0, 256)].rearrange("c p t -> p c t"), ostg[d_][:], reads=[ostg[d_]])
                    c.end_iter()

        if _en(6):
            with Pool_(c, nc) as p:
                Wa = load_w(p, G["w_pa"][:, :])
                Wb = load_w(p, G["w_pb"][:, :])
                Wo = load_w(p, G["w_o"][:, :])
                gpost = p.sb([128, 1024], F32, "gpost")
                c.dma('sp', gpost[:], vb_d[:, 0:1024], writes=[gpost])
                c.end_iter()
                oa = p.sb([128, 8, 512], BF16, "oa")
                of_ = p.sb([128, 8, 512], F32, "of")
                ob_ = p.sb([128, 8, 512], F32, "ob")
                og = p.sb([128, 8, 512], BF16, "og")
                ga = p.sb([128, 8, 512], BF16, "ga")
                gb = p.sb([128, 8, 512], BF16, "gb")
                obn = p.sb([128, 8, 512], BF16, "obn")
                mT = p.sb([128, 8, 512], BF16, "mT")
                sq = [p.sb([128, 512], BF16, "sq") for _ in range(2)]
                rs = [p.sb([128, 512], F32, "rs") for _ in range(2)]
                t1 = [p.sb([128, 512], F32, "t1") for _ in range(2)]
                t2 = [p.sb([128, 512], F32, "t2") for _ in range(2)]
                xt4 = p.sb([128, 4, 1024], F32, "xt4")
                yt4 = p.sb([128, 4, 1024], F32, "yt4")
                junk = p.sb([128, 512], BF16, "junk")
                ss = p.sb([128, 16], F32, "ss")
                psr = PsRot(p, 8)
                for i in loop_iter(nc, NBQ, static):
                    tsl = ds(i * 512, 512)
                    c.dma('sp', oa[:], oaT_d[:, :, tsl].rearrange("c p t -> p c t"), writes=[oa])
                    c.dma('sp', of_[:], oT_d[0][:, :, tsl].rearrange("c p t -> p c t"), writes=[of_])
                    c.dma('sp', ob_[:], oT_d[1][:, :, tsl].rearrange("c p t -> p c t"), writes=[ob_])
                    c.dma('sp', og[:], gogT_d[:, :, tsl].rearrange("c p t -> p c t"), writes=[og])
                    c.dma('sp', ga[:], gaT_d[:, :, tsl].rearrange("c p t -> p c t"), writes=[ga])
                    c.dma('sp', gb[:], gbT_d[:, :, tsl].rearrange("c p t -> p c t"), writes=[gb])
                    c.dma('sp', xt4[:], x_d[ds(i * 512, 512), :].rearrange("(j p) d -> p j d", p=128), writes=[xt4])
                    c.op('pool', lambda e: e.tensor_tensor(out=of_.t[:], in0=of_.t[:], in1=ob_.t[:], op=ALU.add),
                         reads=[of_, ob_], writes=[of_])
                    for hh in range(8):
                        s_ = sq[hh % 2]
                        r_ = rs[hh % 2]
                        c.op('act', lambda e: e.activation(out=s_.t[:, :], in_=of_.t[:, hh, :], func=AF.Square), reads=[of_], writes=[s_])
                        ps = psr.next()
                        c.op('pe', lambda e: e.matmul(ps.t[:, :], lhsT=ones_bf, rhs=s_.t[:, :], start=True, stop=True),
                             reads=[cbf, s_], writes=[ps])
                        c.op('dve', lambda e: e.tensor_scalar(out=r_.t[:, :], in0=ps.t[:, :], scalar1=1.0 / 128, scalar2=EPS,
                                                              op0=ALU.mult, op1=ALU.add), reads=[ps], writes=[r_])
                        c.op('act', lambda e: e.activation(out=r_.t[:, :], in_=r_.t[:, :], func=AF.Ln), reads=[r_], writes=[r_])
                        c.op('act', lambda e: e.activation(out=r_.t[:, :], in_=r_.t[:, :], func=AF.Exp, scale=-0.5), reads=[r_], writes=[r_])
                        c.op('dve', lambda e: e.scalar_tensor_tensor(out=r_.t[:, :], in0=of_.t[:, hh, :], scalar=ghcol, in1=r_.t[:, :],
                                                                     op0=ALU.mult, op1=ALU.mult), reads=[of_, r_, vc], writes=[r_])
                        c.op('pool', lambda e: e.tensor_tensor(out=obn.t[:, hh, :], in0=r_.t[:, :], in1=og.t[:, hh, :], op=ALU.mult),
                             reads=[r_, og], writes=[(obn, hh)])
                    for j in range(8):
                        psa = psr.next()
                        for k in range(8):
                            c.op('pe', lambda e: e.matmul(psa.t[:, :], lhsT=Wa.t[:, k, j * 128:(j + 1) * 128], rhs=oa.t[:, k, :],
                                                          start=(k == 0), stop=(k == 7)), reads=[Wa, oa], writes=[psa])
                        psb = psr.next()
                        for k in range(8):
                            c.op('pe', lambda e: e.matmul(psb.t[:, :], lhsT=Wb.t[:, k, j * 128:(j + 1) * 128], rhs=obn.t[:, k, :],
                                                          start=(k == 0), stop=(k == 7)), reads=[Wb, obn], writes=[psb])
                        a_, b_ = t1[j % 2], t2[j % 2]
                        c.op('dve', lambda e: e.tensor_tensor(out=a_.t[:, :], in0=psa.t[:, :], in1=ga.t[:, j, :], op=ALU.mult),
                             reads=[psa, ga], writes=[a_])
                        c.op('dve', lambda e: e.tensor_tensor(out=b_.t[:, :], in0=psb.t[:, :], in1=gb.t[:, j, :], op=ALU.mult),
                             reads=[psb, gb], writes=[b_])
                        c.op('pool', lambda e: e.tensor_tensor(out=mT.t[:, j, :], in0=a_.t[:, :], in1=b_.t[:, :], op=ALU.add),
                             reads=[a_, b_], writes=[(mT, j)])
                    for s in range(4):
                        pss = []
                        for hf in range(2):
                            ps = psr.next()
                            pss.append(ps)
                            for k in range(8):
                                c.op('pe', lambda e: e.matmul(ps.t[:, :], lhsT=mT.t[:, k, s * 128:(s + 1) * 128],
                                                              rhs=Wo.t[:, k, hf * 512:(hf + 1) * 512], start=(k == 0), stop=(k == 7)),
                                     reads=[Wo, mT], writes=[ps])
                            c.op('act', lambda e: e.activation(out=junk.t[:, :], in_=ps.t[:, :], func=AF.Square,
                                                               accum_out=ss.t[:, hf:hf + 1]), reads=[ps], writes=[junk, ss])
                        res_tail(c, ss, pss, gpost, xt4, xt4.t[:, s, :], yt4, yt4.t[:, s, :])
                    c.dma('sp', x1_d[ds(i * 512, 512), :].rearrange("(j p) d -> p j d", p=128), yt4[:], reads=[yt4])
                    c.end_iter()

        if _en(7):
            with Pool_(c, nc) as p:
                Wu = load_w(p, G["w_up"][:, :], ncols=4096, nk=8)
                Wd = load_w(p, G["w_dn"][:, :], ncols=1024, nk=32)
                gpost = p.sb([128, 1024], F32, "gpost2")
                c.dma('sp', gpost[:], vb_d[:, 1024:2048], writes=[gpost])
                c.end_iter()
                xt2 = p.sb([128, 2, 1024], F32, "xt2")
                yt2 = p.sb([128, 2, 1024], F32, "yt2")
                h2 = p.sb([128, 8, 256], BF16, "h2")
                uT = p.sb([128, 32, 256], BF16, "uT")
                rl = [p.sb([128, 256], BF16, "rl") for _ in range(2)]
                junkb = p.sb([128, 1024], BF16, "junkb")
                xn = [p.sb([128, 1024], BF16, "xn") for _ in range(2)]
                ss = p.sb([128, 16], F32, "ss")
                ptr = [p.ps([128, 8, 128], BF16, "ptr") for _ in range(2)]
                psr = PsRot(p, 6)
                for i in loop_iter(nc, Tq // 256, static):
                    c.dma('sp', xt2[:], x1_d[ds(i * 256, 256), :].rearrange("(j p) d -> p j d", p=128), writes=[xt2])
                    for j in range(2):
                        x_ = xt2
                        c.op('act', lambda e: e.activation(out=junkb.t[:], in_=x_.t[:, j, :], func=AF.Square, accum_out=ss.t[:, 8 + j:9 + j]),
                             reads=[x_], writes=[junkb, ss])
                        c.op('dve', lambda e: e.tensor_scalar(out=ss.t[:, 10 + j:11 + j], in0=ss.t[:, 8 + j:9 + j], scalar1=1.0 / D,
                                                              scalar2=EPS, op0=ALU.mult, op1=ALU.add), reads=[ss], writes=[ss])
                        c.op('act', lambda e: e.activation(out=ss.t[:, 10 + j:11 + j], in_=ss.t[:, 10 + j:11 + j], func=AF.Sqrt),
                             reads=[ss], writes=[ss])
                        c.op('dve', lambda e: e.reciprocal(out=ss.t[:, 12 + j:13 + j], in_=ss.t[:, 10 + j:11 + j]), reads=[ss], writes=[ss])
                        c.op('dve', lambda e: e.tensor_scalar(out=xn[j].t[:], in0=x_.t[:, j, :], scalar1=ss.t[:, 12 + j:13 + j],
                                                              scalar2=None, op0=ALU.mult), reads=[x_, ss], writes=[xn[j]])
                        for cc in range(8):
                            c.op('pe', lambda e: e.transpose(out=ptr[j].t[:, cc, :], in_=xn[j].t[:, cc * 128:(cc + 1) * 128],
                                                             identity=ident), reads=[xn[j], cbf], writes=[ptr[j]])
                        c.op('act', lambda e: e.activation(out=h2.t[:, :, j * 128:(j + 1) * 128], in_=ptr[j].t[:], func=AF.Copy),
                             reads=[ptr[j]], writes=[h2])
                    c.op('pool', lambda e: e.tensor_tensor(out=h2.t[:], in0=h2.t[:], in1=g2col.unsqueeze(2).to_broadcast([128, 8, 256]),
                                                           op=ALU.mult), reads=[h2, vc], writes=[h2])
                    for f in range(32):
                        ps = psr.next()
                        for k in range(8):
                            c.op('pe', lambda e: e.matmul(ps.t[:, 0:256], lhsT=Wu.t[:, k, f * 128:(f + 1) * 128], rhs=h2.t[:, k, :],
                                                          start=(k == 0), stop=(k == 7)), reads=[Wu, h2], writes=[ps])
                        r_ = rl[f % 2]
                        c.op('act', lambda e: e.activation(out=r_.t[:, :], in_=ps.t[:, 0:256], func=AF.Relu), reads=[ps], writes=[r_])
                        c.op('pool' if f % 2 else 'dve', lambda e: e.tensor_tensor(out=uT.t[:, f, :], in0=r_.t[:, :], in1=r_.t[:, :], op=ALU.mult),
                             reads=[r_], writes=[(uT, f)])
                    for s in range(2):
                        pss = []
                        for hf in range(2):
                            ps = psr.next()
                            pss.append(ps)
                            for k in range(32):
                                c.op('pe', lambda e: e.matmul(ps.t[:, :], lhsT=uT.t[:, k, s * 128:(s + 1) * 128],
                                                              rhs=Wd.t[:, k, hf * 512:(hf + 1) * 512], start=(k == 0), stop=(k == 31)),
                                     reads=[Wd, uT], writes=[ps])
                            c.op('act', lambda e: e.activation(out=junkb.t[:, 0:512], in_=ps.t[:, :], func=AF.Square,
                                                               accum_out=ss.t[:, hf:hf + 1]), reads=[ps], writes=[junkb, ss])
                        res_tail(c, ss, pss, gpost, xt2, xt2.t[:, s, :], yt2, yt2.t[:, s, :])
                    c.dma('sp', y_d[ds(i * 256, 256), :].rearrange("(j p) d -> p j d", p=128), yt2[:], reads=[yt2])
                    c.end_iter()


def res_tail(c, ss, pss, gpost, xb, xv, yb, yv):
    c.op('dve', lambda e: e.tensor_tensor(out=ss.t[:, 2:3], in0=ss.t[:, 0:1], in1=ss.t[:, 1:2], op=ALU.add), reads=[ss], writes=[ss])
    c.op('dve', lambda e: e.tensor_scalar(out=ss.t[:, 2:3], in0=ss.t[:, 2:3], scalar1=1.0 / D, scalar2=EPS,
                                          op0=ALU.mult, op1=ALU.add), reads=[ss], writes=[ss])
    c.op('act', lambda e: e.activation(out=ss.t[:, 2:3], in_=ss.t[:, 2:3], func=AF.Sqrt), reads=[ss], writes=[ss])
    c.op('dve', lambda e: e.reciprocal(out=ss.t[:, 3:4], in_=ss.t[:, 2:3]), reads=[ss], writes=[ss])
    for hf in range(2):
        cs = slice(hf * 512, (hf + 1) * 512)
        c.op('dve', lambda e: e.scalar_tensor_tensor(out=yv[:, cs], in0=pss[hf].t[:, :], scalar=ss.t[:, 3:4], in1=gpost.t[:, cs],
                                                     op0=ALU.mult, op1=ALU.mult), reads=[pss[hf], ss, gpost], writes=[yb])
    c.op('pool', lambda e: e.tensor_tensor(out=yv, in0=yv, in1=xv, op=ALU.add), reads=[yb, xb], writes=[yb])


def _t5_bucket_np(rel):
    nb = 16
    ret = (rel > 0).astype(np.int64) * nb
    n = np.abs(rel)
    max_exact = 8
    is_small = n < max_exact
    nf = np.maximum(n, 1).astype(np.float32)
    large = max_exact + (np.log(nf / np.float32(max_exact)) / np.float32(math.log(128 / max_exact))
                         * np.float32(nb - max_exact)).astype(np.int64)
    large = np.minimum(large, nb - 1)
    return ret + np.where(is_small, n, large)


def _consts():
    cbf = np.zeros((128, 256), np.float32)
    cbf[:, 0:128] = np.eye(128)
    cbf[:, 128:256] = 1.0
    cf = np.zeros((128, 384), np.float32)
    cf[:, 256:384] = 1.0
    cf[:, 0:128] = np.eye(128)[::-1]
    U = np.triu(np.ones((64, 64), np.float32))
    cf[0:64, 128:192] = U
    cf[0:64, 192:256] = U.T
    return cbf.astype(ml_dtypes.bfloat16), cf


def _oht(flip):
    d = J0 - np.arange(NJ)
    if flip:
        d = -d
    b = _t5_bucket_np(d)
    oh = np.zeros((32, NJ), np.float32)
    oh[b, np.arange(NJ)] = 1.0
    return oh


def _cols(v):
    return np.ascontiguousarray(v.reshape(8, 128).T)


def _vec_inputs(I, swap):
    lbf, lbb = I["lb_fwd"], I["lb_bwd"]
    if swap:
        lbf, lbb = lbb, lbf
    vc = np.zeros((128, 50), np.float32)
    vc[:, 0:8] = _cols(I["g_mix_pre"][0])
    vc[:, 8:16] = _cols(I["g_mlp_pre"][0])
    vc[:, 16:24] = _cols(lbf[0])
    vc[:, 24:32] = _cols(lbf[1])
    vc[:, 32:40] = _cols(lbb[0])
    vc[:, 40:48] = _cols(lbb[1])
    vc[:, 48] = I["g_attn_sub"][0]
    vc[:, 49] = I["g_hgrn_out"][0]
    row = np.concatenate([I["g_mix_post"][0], I["g_mlp_post"][0], lbf[0], lbf[1], lbb[0], lbb[1],
                          I["lam_q1"][0], I["lam_q2"][0], I["lam_k1"][0], I["lam_k2"][0]]).astype(np.float32)
    vb = np.ascontiguousarray(np.broadcast_to(row[None, :], (128, 6400)))
    return vc, vb


def make_in_maps(I, n_cores=8, TS=None, TP=None):
    I = {k: np.asarray(v) for k, v in I.items()}
    xs_all, xp_all = I["x_sample"], I["x_prompt"]
    w_in = np.ascontiguousarray(I["w_in"][0])
    cbf, cf = _consts()
    relb = np.ascontiguousarray(I["rel_bias"], dtype=np.float32)
    wf = [np.ascontiguousarray(w_in[:, 4096:6144]),
          np.ascontiguousarray(np.concatenate([w_in[:, 5120:6144], w_in[:, 4096:5120]], axis=1))]
    vcs, vbs = _vec_inputs(I, False)
    vcp = [vcs, _vec_inputs(I, True)[0]]
    vbp = [vbs, _vec_inputs(I, True)[1]]
    oht = [_oht(False), _oht(True)]

    def cfar(flip):
        a, b = (31, 15) if not flip else (15, 31)
        out = np.zeros((8, 128, 2), np.float32)
        out[:, :, 0] = relb[a][:, None]
        out[:, :, 1] = relb[b][:, None]
        return out
    cfars = [cfar(False), cfar(True)]
    shared = {"w_in": w_in, "w_pa": np.ascontiguousarray(I["w_proj_a"][0]), "w_pb": np.ascontiguousarray(I["w_proj_b"][0]),
              "w_o": np.ascontiguousarray(I["w_out"][0]), "w_up": np.ascontiguousarray(I["w_mlp_up"][0]),
              "w_dn": np.ascontiguousarray(I["w_mlp_down"][0]), "relb": relb, "cbf": cbf, "cf": cf,
              "vc_s": vcs, "vb_s": vbs, "oht_s": oht[0], "cfar_s": cfars[0]}
    maps = []
    for cid in range(n_cores):
        par = cid % 2
        xp = xp_all[(cid // 2) % xp_all.shape[0]]
        if par:
            xp = xp[::-1]
        m = dict(shared)
        m.update({"xs": np.ascontiguousarray(xs_all[cid % xs_all.shape[0]]), "xp": np.ascontiguousarray(xp),
                  "w_fsw": wf[par], "vc_p": vcp[par], "vb_p": vbp[par], "oht_p": oht[par], "cfar_p": cfars[par]})
        maps.append(m)
    return maps


_NC_CACHE = {}


def kernel(**inputs):
    TS = inputs["x_sample"].shape[1]
    TP = inputs["x_prompt"].shape[1]
    key = (TS, TP)
    if key not in _NC_CACHE:
        _NC_CACHE[key] = build(TS, TP)
    nc = _NC_CACHE[key]
    maps = make_in_maps(inputs)
    res = run_bass_kernel_spmd(nc, maps, core_ids=list(range(8)))
    B, Bs = inputs["x_prompt"].shape[0], inputs["x_sample"].shape[0]
    y_s = np.stack([np.asarray(res.results[cid]["ys"], dtype=np.float32) for cid in range(Bs)], axis=0)
    y_p = np.zeros((B, TP, D), np.float32)
    hq = TP // 2
    for cid in range(8):
        b, par = cid // 2, cid % 2
        yp = np.asarray(res.results[cid]["yp"], dtype=np.float32)
        if par == 0:
            y_p[b, 0:hq] = yp
        else:
            y_p[b, hq:] = yp[::-1]
    return (y_p, y_s)
```

```python
import math
import numpy as np
import ml_dtypes
from contextlib import ExitStack
import concourse.bass as bass
import concourse.mybir as mybir
from concourse.bass_utils import run_bass_kernel_spmd

F32 = mybir.dt.float32
BF16 = mybir.dt.bfloat16
AF = mybir.ActivationFunctionType
ALU = mybir.AluOpType
AX = mybir.AxisListType
NDMA = 16
EPS = 1e-6
D = 1024
NJ = 1280
J0 = 639


class Buf:
    def __init__(self, ctx, t):
        self.t = t
        self.wr = None
        self.rd = []
        self.subs = {}
        ctx.bufs.append(self)

    def __getitem__(self, k):
        return self.t[k]


class EngW:
    def __init__(self, name, eng, sem, same_sync):
        self.name, self.eng, self.sem = name, eng, sem
        self.count = 0
        self.waited = {}
        self.same_sync = same_sync

    def wait(self, deps):
        for d in deps:
            if d is None:
                continue
            sem, val = d
            if val <= 0:
                continue
            if sem is self.sem and not self.same_sync:
                continue
            if self.waited.get(sem, 0) >= val:
                continue
            self.eng.wait_ge(sem, val)
            self.waited[sem] = val


class Ctx:
    def __init__(self, nc, es):
        self.nc = nc
        self.E = {}
        self.bufs = []
        for name, eng, ss in [('pe', nc.tensor, False), ('act', nc.scalar, True),
                              ('dve', nc.vector, True), ('pool', nc.gpsimd, True),
                              ('sp', nc.sync, False)]:
            sem = es.enter_context(nc.semaphore('s_' + name))
            self.E[name] = EngW(name, eng, sem, ss)
        self.dma_sems = [es.enter_context(nc.semaphore('d%d' % i)) for i in range(2 * NDMA)]
        self.dma_val = [0] * (2 * NDMA)
        self.dma_next = [0, 0]

    @staticmethod
    def _bk(x):
        return x if isinstance(x, tuple) else (x, None)

    def _deps(self, reads, writes, extra):
        deps = list(extra)
        for x in reads:
            b, k = self._bk(x)
            deps.append(b.wr)
            if k is None:
                for sw, srd in b.subs.values():
                    deps.append(sw)
            elif k in b.subs:
                deps.append(b.subs[k][0])
        for x in writes:
            b, k = self._bk(x)
            deps.append(b.wr)
            deps.extend(b.rd)
            if k is None:
                for sw, srd in b.subs.values():
                    deps.append(sw)
                    deps.extend(srd)
            elif k in b.subs:
                deps.append(b.subs[k][0])
                deps.extend(b.subs[k][1])
        return deps

    def _mark(self, tok, reads, writes):
        for x in reads:
            b, k = self._bk(x)
            if k is None:
                b.rd.append(tok)
            else:
                b.subs.setdefault(k, [None, []])[1].append(tok)
        for x in writes:
            b, k = self._bk(x)
            if k is None:
                b.wr = tok
                b.rd = []
                b.subs = {}
            else:
                b.subs[k] = [tok, []]

    def op(self, en, fn, reads=(), writes=(), extra=()):
        e = self.E[en]
        e.wait(self._deps(reads, writes, extra))
        ins = fn(e.eng)
        e.count += 1
        ins.then_inc(e.sem, 1)
        tok = (e.sem, e.count)
        self._mark(tok, reads, writes)
        return tok

    def dma(self, en, out, in_, reads=(), writes=(), extra=()):
        e = self.E[en]
        grp = 1 if en == 'pool' else 0
        k = grp * NDMA + self.dma_next[grp]
        self.dma_next[grp] = (self.dma_next[grp] + 1) % NDMA
        sem = self.dma_sems[k]
        e.wait(self._deps(reads, writes, extra) + [(sem, self.dma_val[k])])
        ins = e.eng.dma_start(out=out, in_=in_)
        self.dma_val[k] += 16
        ins.then_inc(sem, 16)
        tok = (sem, self.dma_val[k])
        self._mark(tok, reads, writes)
        return tok

    def end_iter(self):
        nc = self.nc
        sp = self.E['sp']
        sp.wait([(s, v) for s, v in zip(self.dma_sems, self.dma_val)])
        nc.all_engine_barrier()
        for e in self.E.values():
            if e.count:
                nc.sync.sem_clear(e.sem)
        for s, v in zip(self.dma_sems[:NDMA], self.dma_val[:NDMA]):
            if v:
                nc.sync.sem_clear(s)
        nc.all_engine_barrier()
        for e in self.E.values():
            e.count = 0
            e.waited = {}
        self.dma_val = [0] * NDMA + self.dma_val[NDMA:]
        self.dma_next = [0, self.dma_next[1]]
        for b in self.bufs:
            b.wr = None
            b.rd = []
            b.subs = {}


def ds(start, size):
    if isinstance(start, int):
        return slice(start, start + size)
    return bass.ds(start, size)


import os
_LOOPK = [0]


class _Stop(Exception):
    pass


def _en(n):
    ph = os.environ.get("KPH")
    return ph is None or str(n) in ph.split(",")


def loop_iter(nc, n, static):
    k = _LOOPK[0] % 7
    _LOOPK[0] += 1
    en = os.environ.get("KLOOPS")
    if en is not None and str(k) not in en.split(","):
        return
    if static:
        for i in range(n):
            yield i
    else:
        with nc.Fori(0, n) as i:
            yield i


_PROBE = [0]


def free_regs(eng):
    regs = []
    try:
        for k in range(200):
            _PROBE[0] += 1
            regs.append(eng.alloc_register("probe%d" % _PROBE[0]))
    except Exception:
        pass
    for r in regs:
        eng.free_register(r)
    return len(regs)


_UID = [0]


class Pool_:
    def __init__(self, c, nc):
        self.c, self.nc = c, nc
        self.es = ExitStack()
        self.n = 0

    def __enter__(self):
        self.es.__enter__()
        return self

    def __exit__(self, *a):
        return self.es.__exit__(*a)

    def sb(self, shape, dt, name=None):
        self.n += 1
        _UID[0] += 1
        t = self.es.enter_context(self.nc.sbuf_tensor("%s_%d" % (name or "t", _UID[0]), shape, dt))
        return Buf(self.c, t)

    def ps(self, shape, dt=F32, name=None):
        self.n += 1
        _UID[0] += 1
        t = self.es.enter_context(self.nc.psum_tensor("%s_%d" % (name or "p", _UID[0]), shape, dt))
        return Buf(self.c, t)


def build(TS, TP, dbg=False):
    nc = bass.Bass("TRN2", target_bir_lowering=False)
    TM = max(TS, TP)
    TQP = TP // 2

    def din(name, shape, dt=F32):
        return nc.dram_tensor(name, shape, dt, kind="ExternalInput").ap()

    def dscr(name, shape, dt):
        return nc.dram_tensor(name, shape, dt, kind="ExternalOutput" if dbg else "Internal").ap()

    xs = din("xs", [TS, D])
    xp = din("xp", [TP, D])
    w_in = din("w_in", [D, 10 * D])
    w_fsw = din("w_fsw", [D, 2 * D])
    w_pa = din("w_pa", [D, D])
    w_pb = din("w_pb", [D, D])
    w_o = din("w_o", [D, D])
    w_up = din("w_up", [D, 4 * D])
    w_dn = din("w_dn", [4 * D, D])
    vc_in = {"s": din("vc_s", [128, 50]), "p": din("vc_p", [128, 50])}
    vb_in = {"s": din("vb_s", [128, 6400]), "p": din("vb_p", [128, 6400])}
    oht_in = {"s": din("oht_s", [32, NJ]), "p": din("oht_p", [32, NJ])}
    cfar_in = {"s": din("cfar_s", [8, 128, 2]), "p": din("cfar_p", [8, 128, 2])}
    relb_in = din("relb", [32, 8])
    cbf_in = din("cbf", [128, 256], BF16)
    cf_in = din("cf", [128, 384])
    ys = nc.dram_tensor("ys", [TS, D], F32, kind="ExternalOutput").ap()
    yp = nc.dram_tensor("yp", [TQP, D], F32, kind="ExternalOutput").ap()

    hT_d = dscr("hT_d", [8, 128, TM], BF16)
    aqT_d = dscr("aqT_d", [8, 128, TM], BF16)
    akT_d = dscr("akT_d", [8, 128, TM], BF16)
    gqT_d = dscr("gqT_d", [8, 128, TM], BF16)
    kT_d = [dscr("kfT_d", [8, 128, TM], BF16), dscr("kbT_d", [8, 128, TM], BF16)]
    av_d = dscr("av_d", [TM, D], BF16)
    gi_d = dscr("gi_d", [TM, D], BF16)
    ktm_d = [dscr("kftm_d", [TM, D], BF16), dscr("kbtm_d", [TM, D], BF16)]
    g_d = [dscr("gf_d", [TM, D], F32), dscr("gb_d", [TM, D], F32)]
    gogT_d = dscr("gogT_d", [8, 128, TM], BF16)
    gaT_d = dscr("gaT_d", [8, 128, TM], BF16)
    gbT_d = dscr("gbT_d", [8, 128, TM], BF16)
    oaT_d = dscr("oaT_d", [8, 128, TM], BF16)
    oT_d = [dscr("ofT_d", [8, 128, TM], F32), dscr("obT_d", [8, 128, TM], F32)]
    x1_d = dscr("x1_d", [TM, D], F32)
    tab_d = dscr("tab_d", [8, NJ], F32)
    bias_d = dscr("bias_d", [48, 128, 512], F32)

    with ExitStack() as es:
        c = Ctx(nc, es)

        with Pool_(c, nc) as gp:
            cbf = gp.sb([128, 256], BF16, "cbf")
            cf = gp.sb([128, 384], F32, "cf")
            c.dma('sp', cbf[:], cbf_in[:, :], writes=[cbf])
            c.dma('sp', cf[:], cf_in[:, :], writes=[cf])
            ident = cbf.t[:, 0:128]
            ones_bf = cbf.t[:, 128:256]
            antiI = cf.t[:, 0:128]
            U_f = cf.t[0:64, 128:192]
            Ut_f = cf.t[0:64, 192:256]
            ones_f = cf.t[:, 256:384]
            c.end_iter()

            for job in os.environ.get("KJOBS", "sp"):
                T = TS if job == "s" else TP
                Tq = TS if job == "s" else TQP
                x_d = xs if job == "s" else xp
                y_d = ys if job == "s" else yp
                run_job(nc, c, job, T, Tq, x_d, y_d, locals())
    print('free regs:', {n: free_regs(e.eng) for n, e in c.E.items()}, 'ninstr', nc.n_instructions if not callable(nc.n_instructions) else nc.n_instructions())
    return nc


def run_job(nc, c, job, T, Tq, x_d, y_d, G):
    ident, ones_bf, antiI, U_f, Ut_f, ones_f = G["ident"], G["ones_bf"], G["antiI"], G["U_f"], G["Ut_f"], G["ones_f"]
    cbf, cf = G["cbf"], G["cf"]
    vc_d, vb_d, oht_d, cfar_d = G["vc_in"][job], G["vb_in"][job], G["oht_in"][job], G["cfar_in"][job]
    w_in, w_fsw = G["w_in"], G["w_fsw"]
    hT_d, aqT_d, akT_d, gqT_d, kT_d = G["hT_d"], G["aqT_d"], G["akT_d"], G["gqT_d"], G["kT_d"]
    av_d, gi_d, ktm_d, g_d = G["av_d"], G["gi_d"], G["ktm_d"], G["g_d"]
    gogT_d, gaT_d, gbT_d, oaT_d, oT_d, x1_d = G["gogT_d"], G["gaT_d"], G["gbT_d"], G["oaT_d"], G["oT_d"], G["x1_d"]
    tab_d, bias_d, relb_in = G["tab_d"], G["bias_d"], G["relb_in"]
    NB = T // 512
    NBQ = Tq // 512
    static = (job == "p")

    with Pool_(c, nc) as jp:
        vc = jp.sb([128, 50], F32, "vc")
        c.dma('sp', vc[:], vc_d[:, :], writes=[vc])
        der = jp.sb([128, 32], F32, "der")
        lamt = jp.sb([128, 256], F32, "lamt")
        c.dma('sp', lamt[:], vb_d[:, 6144:6400], writes=[lamt])
        c.op('dve', lambda e: e.tensor_tensor(out=der.t[:, 0:8], in0=vc.t[:, 24:32], in1=vc.t[:, 16:24], op=ALU.subtract),
             reads=[vc], writes=[der])
        c.op('dve', lambda e: e.tensor_tensor(out=der.t[:, 8:16], in0=vc.t[:, 40:48], in1=vc.t[:, 32:40], op=ALU.subtract),
             reads=[vc], writes=[der])
        c.op('act', lambda e: e.activation(out=der.t[:, 0:16], in_=der.t[:, 0:16], func=AF.Sigmoid), reads=[der], writes=[der])
        c.op('dve', lambda e: e.tensor_scalar(out=der.t[:, 16:17], in0=vc.t[:, 48:49], scalar1=0.8, scalar2=None, op0=ALU.mult),
             reads=[vc], writes=[der])
        c.op('dve', lambda e: e.tensor_tensor(out=lamt.t[:, 0:128], in0=lamt.t[:, 0:128], in1=lamt.t[:, 128:256], op=ALU.mult),
             reads=[lamt], writes=[lamt])
        c.op('dve', lambda e: e.reduce_sum(out=der.t[:, 20:22], in_=lamt.t[:, 0:128].rearrange("p (a b) -> p a b", a=2), axis=AX.X),
             reads=[lamt, der], writes=[der])
        c.op('act', lambda e: e.activation(out=der.t[:, 20:22], in_=der.t[:, 20:22], func=AF.Exp), reads=[der], writes=[der])
        c.op('dve', lambda e: e.tensor_tensor(out=der.t[:, 22:23], in0=der.t[:, 21:22], in1=der.t[:, 20:21], op=ALU.subtract),
             reads=[der], writes=[der])
        c.op('dve', lambda e: e.tensor_scalar(out=der.t[:, 22:23], in0=der.t[:, 22:23], scalar1=-0.2, scalar2=None, op0=ALU.add),
             reads=[der], writes=[der])
        c.end_iter()
        g1col = vc.t[:, 0:8]
        g2col = vc.t[:, 8:16]
        omlb_col = [der.t[:, 0:8], der.t[:, 8:16]]
        gsubcol = der.t[:, 16:17]
        ghcol = vc.t[:, 49:50]
        neglam = der.t[:, 22:23]

        if _en(0):
            with Pool_(c, nc) as p:
                oht = p.sb([32, NJ], F32)
                relb = p.sb([32, 8], F32)
                tabs = p.sb([8, NJ], F32)
                c.dma('sp', oht[:], oht_d[:, :], writes=[oht])
                c.dma('sp', relb[:], relb_in[:, :], writes=[relb])
                pt = p.ps([8, 512], F32)
                for k0 in range(0, NJ, 512):
                    n = min(512, NJ - k0)
                    c.op('pe', lambda e: e.matmul(pt.t[:, 0:n], lhsT=relb.t[:, :], rhs=oht.t[:, k0:k0 + n], start=True, stop=True),
                         reads=[relb, oht], writes=[pt])
                    c.op('act', lambda e: e.activation(out=tabs.t[:, k0:k0 + n], in_=pt.t[:, 0:n], func=AF.Copy),
                         reads=[pt], writes=[tabs])
                c.dma('sp', tab_d[:, :], tabs[:], reads=[tabs])
                c.end_iter()
                hk = [p.sb([128, 512], F32) for _ in range(2)]
                bt = [p.sb([128, 512], F32) for _ in range(2)]
                pb = [p.ps([128, 512], F32) for _ in range(2)]
                n = 0
                for h in range(8):
                    for di in range(6):
                        delta = -128 + 128 * di
                        off = J0 - 127 - delta
                        src = bass.AP(tab_d.tensor, h * NJ + off, [[1, 128], [1, 512]])
                        c.dma('sp', hk[n % 2][:], src, writes=[hk[n % 2]])
                        c.op('pe', lambda e: e.matmul(pb[n % 2].t[:, :], lhsT=antiI, rhs=hk[n % 2].t[:, :], start=True, stop=True),
                             reads=[hk[n % 2], cf], writes=[pb[n % 2]])
                        c.op('act', lambda e: e.activation(out=bt[n % 2].t[:, :], in_=pb[n % 2].t[:, :], func=AF.Copy),
                             reads=[pb[n % 2]], writes=[bt[n % 2]])
                        c.dma('sp', bias_d[h * 6 + di, :, :], bt[n % 2][:], reads=[bt[n % 2]])
                        n += 1
                c.end_iter()

        def norm_transpose(p, xt_list, hts, gcol, scr):
            junk, xn, ss, ptr = scr
            for j in range(4):
                c.op('act', lambda e: e.activation(out=junk.t[:], in_=xt_list.t[:, j, :], func=AF.Square,
                                                   accum_out=ss.t[:, j:j + 1]),
                     reads=[xt_list], writes=[junk, ss])
                c.op('dve', lambda e: e.tensor_scalar(out=ss.t[:, 4 + j:5 + j], in0=ss.t[:, j:j + 1], scalar1=1.0 / D,
                                                      scalar2=EPS, op0=ALU.mult, op1=ALU.add), reads=[ss], writes=[ss])
                c.op('act', lambda e: e.activation(out=ss.t[:, 4 + j:5 + j], in_=ss.t[:, 4 + j:5 + j], func=AF.Sqrt),
                     reads=[ss], writes=[ss])
                c.op('dve', lambda e: e.reciprocal(out=ss.t[:, 8 + j:9 + j], in_=ss.t[:, 4 + j:5 + j]), reads=[ss], writes=[ss])
                c.op('dve', lambda e: e.tensor_scalar(out=xn[j % 2].t[:], in0=xt_list.t[:, j, :], scalar1=ss.t[:, 8 + j:9 + j],
                                                      scalar2=None, op0=ALU.mult),
                     reads=[xt_list, ss], writes=[xn[j % 2]])
                for cc in range(8):
                    c.op('pe', lambda e: e.transpose(out=ptr[j % 2].t[:, cc, :], in_=xn[j % 2].t[:, cc * 128:(cc + 1) * 128],
                                                     identity=ident),
                         reads=[xn[j % 2], cbf], writes=[ptr[j % 2]])
                c.op('act', lambda e: e.activation(out=hts.t[:, :, j * 128:(j + 1) * 128], in_=ptr[j % 2].t[:], func=AF.Copy),
                     reads=[ptr[j % 2]], writes=[hts])
            c.op('pool', lambda e: e.tensor_tensor(out=hts.t[:], in0=hts.t[:], in1=gcol.unsqueeze(2).to_broadcast([128, 8, 512]),
                                                   op=ALU.mult), reads=[hts, vc], writes=[hts])

        def nt_scratch(p):
            return (p.sb([128, 1024], BF16), [p.sb([128, 1024], BF16) for _ in range(2)], p.sb([128, 12], F32),
                    [p.ps([128, 8, 128], BF16) for _ in range(2)])

        if _en(1):
            with Pool_(c, nc) as p:
                xt4 = p.sb([128, 4, 1024], F32)
                hts = p.sb([128, 8, 512], BF16)
                scr = nt_scratch(p)

                for i in loop_iter(nc, NB, static):
                    c.dma('sp', xt4[:], x_d[ds(i * 512, 512), :].rearrange("(j p) d -> p j d", p=128), writes=[xt4])
                    norm_transpose(p, xt4, hts, g1col, scr)
                    c.dma('sp', hT_d[:, :, ds(i * 512, 512)].rearrange("c p t -> p c t"), hts[:], reads=[hts])
                    c.end_iter()

        def load_w(p, src_ap, ncols=1024, nk=8):
            w = p.sb([128, nk, ncols], BF16, "w")
            for kc in range(nk):
                for c0 in range(0, ncols, 1024):
                    c.dma('pool', w.t[:, kc, c0:c0 + 1024], src_ap[kc * 128:(kc + 1) * 128, c0:c0 + 1024], writes=[w])
            return w

        class PsRot:
            def __init__(self, p, n):
                self.b = [p.ps([128, 512], F32) for _ in range(n)]
                self.i = 0

            def next(self):
                b = self.b[self.i % len(self.b)]
                self.i += 1
                return b

        def proj_cm(w, act, psr, evac, ncol_chunks=8, nk=8):
            for j in range(ncol_chunks):
                ps = psr.next()
                for k in range(nk):
                    c.op('pe', lambda e: e.matmul(ps.t[:, :], lhsT=w.t[:, k, j * 128:(j + 1) * 128], rhs=act.t[:, k, :],
                                                  start=(k == 0), stop=(k == nk - 1)),
                         reads=[w, act], writes=[ps])
                evac(j, ps)

        def proj_tm(w, act, psr, evac, ncols=1024, nk=8, ntok=4):
            for s in range(ntok):
                for hf in range(ncols // 512):
                    ps = psr.next()
                    for k in range(nk):
                        c.op('pe', lambda e: e.matmul(ps.t[:, :], lhsT=act.t[:, k, s * 128:(s + 1) * 128],
                                                      rhs=w.t[:, k, hf * 512:(hf + 1) * 512],
                                                      start=(k == 0), stop=(k == nk - 1)),
                             reads=[w, act], writes=[ps])
                    evac(s, hf, ps)

        wf_src = [w_in[:, 4096:5120], w_in[:, 5120:6144]] if job == "s" else [w_fsw[:, 0:1024], w_fsw[:, 1024:2048]]

        if _en(2):
            with Pool_(c, nc) as p:
                W = {"aq": load_w(p, w_in[:, 0:1024]), "ak": load_w(p, w_in[:, 1024:2048]),
                     "av": load_w(p, w_in[:, 2048:3072]), "gq": load_w(p, w_in[:, 3072:4096]),
                     "f0": load_w(p, wf_src[0]), "f1": load_w(p, wf_src[1]), "gi": load_w(p, w_in[:, 6144:7168])}
                lbB = p.sb([128, 4, 1024], F32, "lbB")
                c.dma('sp', lbB[:], vb_d[:, 2048:6144].rearrange("p (a b) -> p a b", a=4), writes=[lbB])
                with Pool_(c, nc) as tp_:
                    dtmp = tp_.sb([128, 2, 1024], F32, "dtmp")
                    for d_ in range(2):
                        c.op('dve', lambda e: e.tensor_tensor(out=dtmp.t[:, d_, :], in0=lbB.t[:, 2 * d_, :], in1=lbB.t[:, 2 * d_ + 1, :],
                                                              op=ALU.subtract), reads=[lbB], writes=[dtmp])
                    for d_ in range(2):
                        c.op('act', lambda e: e.activation(out=lbB.t[:, 2 * d_, :], in_=dtmp.t[:, d_, :], func=AF.Sigmoid),
                             reads=[dtmp], writes=[lbB])
                        c.op('act', lambda e: e.activation(out=lbB.t[:, 2 * d_ + 1, :], in_=dtmp.t[:, d_, :], func=AF.Sigmoid, scale=-1.0),
                             reads=[dtmp], writes=[lbB])
                    c.end_iter()
                hts = p.sb([128, 8, 512], BF16, "hts")
                stg_cm = [p.sb([128, 8, 512], BF16, "stgcm") for _ in range(2)]
                stg_bf = [p.sb([128, 1024], BF16, "stgbf") for _ in range(4)]
                stg_f = [p.sb([128, 1024], F32, "stgf") for _ in range(4)]
                sig = [p.sb([128, 512], F32, "sig") for _ in range(8)]
                psr = PsRot(p, 8)
                cnt = {"cm": 0, "tm": 0, "sg": 0}
                tq = 'sp' if static else 'act'
                for i in loop_iter(nc, NB, static):
                    c.dma('sp', hts[:], hT_d[:, :, ds(i * 512, 512)].rearrange("c p t -> p c t"), writes=[hts])

                    def cm_plain(wname, dst):
                        st = stg_cm[cnt["cm"] % 2]
                        cnt["cm"] += 1
                        eng = ['act', 'dve']

                        def ev(j, ps):
                            if j % 2 == 0:
                                c.op('act', lambda e: e.activation(out=st.t[:, j, :], in_=ps.t[:, :], func=AF.Copy),
                                     reads=[ps], writes=[(st, j)])
                            else:
                                c.op('dve', lambda e: e.tensor_copy(out=st.t[:, j, :], in_=ps.t[:, :]), reads=[ps], writes=[(st, j)])
                        proj_cm(W[wname], hts, psr, ev)
                        c.dma('sp', dst[:, :, ds(i * 512, 512)].rearrange("c p t -> p c t"), st[:], reads=[st])

                    def tm_plain(wname, dst):
                        sts = {}

                        def ev(s, hf, ps):
                            if hf == 0:
                                sts[s] = stg_bf[cnt["tm"] % 4]
                                cnt["tm"] += 1
                            st = sts[s]
                            if hf == 0:
                                c.op('act', lambda e: e.activation(out=st.t[:, 0:512], in_=ps.t[:, :], func=AF.Copy),
                                     reads=[ps], writes=[(st, 0)])
                            else:
                                c.op('dve', lambda e: e.tensor_copy(out=st.t[:, 512:1024], in_=ps.t[:, :]), reads=[ps], writes=[(st, 1)])
                                c.dma(tq, dst[ds(i * 512 + s * 128, 128), :], st[:], reads=[st])
                        proj_tm(W[wname], hts, psr, ev)

                    def f_cm(d_):
                        st = stg_cm[cnt["cm"] % 2]
                        cnt["cm"] += 1

                        def ev(j, ps):
                            sg = sig[cnt["sg"] % 8]
                            cnt["sg"] += 1
                            c.op('act', lambda e: e.activation(out=sg.t[:, :], in_=ps.t[:, :], func=AF.Sigmoid, scale=-1.0),
                                 reads=[ps], writes=[sg])
                            c.op('dve', lambda e: e.tensor_scalar(out=st.t[:, j, :], in0=sg.t[:, :], scalar1=omlb_col[d_][:, j:j + 1],
                                                                  scalar2=None, op0=ALU.mult), reads=[sg, der], writes=[(st, j)])
                        proj_cm(W["f%d" % d_], hts, psr, ev)
                        c.dma('sp', kT_d[d_][:, :, ds(i * 512, 512)].rearrange("c p t -> p c t"), st[:], reads=[st])

                    def f_tm(d_):
                        sts = {}

                        def ev(s, hf, ps):
                            if hf == 0:
                                sts[s] = (stg_bf[cnt["tm"] % 4], stg_f[cnt["tm"] % 4])
                                cnt["tm"] += 1
                            sb_, sf_ = sts[s]
                            sg = sig[cnt["sg"] % 8]
                            cnt["sg"] += 1
                            cs = slice(hf * 512, (hf + 1) * 512)
                            c.op('act', lambda e: e.activation(out=sg.t[:, :], in_=ps.t[:, :], func=AF.Sigmoid), reads=[ps], writes=[sg])
                            c.op('dve', lambda e: e.tensor_tensor(out=sg.t[:, :], in0=sg.t[:, :], in1=lbB.t[:, 2 * d_ + 1, cs], op=ALU.mult),
                                 reads=[sg, lbB], writes=[sg])
                            c.op('dve', lambda e: e.tensor_tensor(out=sg.t[:, :], in0=sg.t[:, :], in1=lbB.t[:, 2 * d_, cs], op=ALU.add),
                                 reads=[sg, lbB], writes=[sg])
                            def post(sg=sg, sf_=sf_, sb_=sb_, cs=cs, hf=hf, s=s):
                                c.op('act', lambda e: e.activation(out=sf_.t[:, cs], in_=sg.t[:, :], func=AF.Ln), reads=[sg], writes=[(sf_, hf)])
                                c.op('pool', lambda e: e.tensor_scalar(out=sb_.t[:, cs], in0=sg.t[:, :], scalar1=-1.0, scalar2=1.0,
                                                                       op0=ALU.mult, op1=ALU.add), reads=[sg], writes=[(sb_, hf)])
                                if hf == 1:
                                    c.dma(tq, g_d[d_][ds(i * 512 + s * 128, 128), :], sf_[:], reads=[sf_])
                                    c.dma(tq, ktm_d[d_][ds(i * 512 + s * 128, 128), :], sb_[:], reads=[sb_])
                            posts.append(post)
                        posts = []
                        proj_tm(W["f%d" % d_], hts, psr, ev)
                        for po in posts:
                            po()

                    cm_plain("aq", aqT_d)
                    cm_plain("ak", akT_d)
                    tm_plain("av", av_d)
                    cm_plain("gq", gqT_d)
                    tm_plain("gi", gi_d)
                    f_cm(0)
                    f_cm(1)
                    f_tm(0)
                    f_tm(1)
                    c.end_iter()

        if _en(3):
            with Pool_(c, nc) as p:
                W = {"og": load_w(p, w_in[:, 7168:8192]), "ga": load_w(p, w_in[:, 8192:9216]), "gb": load_w(p, w_in[:, 9216:10240])}
                hts = p.sb([128, 8, 512], BF16, "hts")
                stg_cm = [p.sb([128, 8, 512], BF16, "stgcm") for _ in range(2)]
                psr = PsRot(p, 8)
                c.end_iter()
                for i in loop_iter(nc, NBQ, static):
                    c.dma('sp', hts[:], hT_d[:, :, ds(i * 512, 512)].rearrange("c p t -> p c t"), writes=[hts])
                    for n_, (wname, dst, fn) in enumerate((("ga", gaT_d, AF.Sigmoid), ("gb", gbT_d, AF.Sigmoid), ("og", gogT_d, AF.Silu))):
                        st = stg_cm[n_ % 2]

                        def ev(j, ps):
                            c.op('act', lambda e: e.activation(out=st.t[:, j, :], in_=ps.t[:, :], func=fn), reads=[ps], writes=[(st, j)])
                        proj_cm(W[wname], hts, psr, ev)
                        c.dma('sp', dst[:, :, ds(i * 512, 512)].rearrange("c p t -> p c t"), st[:], reads=[st])
                    c.end_iter()

        NKB = T // 128
        if _en(4):
            with Pool_(c, nc) as p:
                kT = p.sb([128, 1, T], BF16, "kT")
                vv = p.sb([128, NKB, 128], BF16, "vv")
                biasT = p.sb([128, 6, 512], F32, "biasT")
                cfar = p.sb([128, 1, 2], F32, "cfar")
                qT = p.sb([128, 1, Tq], BF16, "qT")
                pS = [[p.ps([128, 512], F32, "pS") for _ in range(2)] for _ in range(2)]
                pO = [p.ps([128, 512], F32, "pO") for _ in range(2)]
                pZ = [p.ps([128, 512], F32, "pZ") for _ in range(2)]
                pT = [[p.sb([128, 512], BF16, "pT") for _ in range(4)] for _ in range(2)]
                tmpS = [p.sb([128, 512], F32, "tmpS") for _ in range(2)]
                zacc = [[p.sb([128, 512], F32, "zacc") for _ in range(2)] for _ in range(2)]
                rz = [p.sb([128, 512], F32, "rz") for _ in range(2)]
                o01 = [p.sb([128, 512], F32, "o01") for _ in range(2)]
                osq = p.sb([128, 512], BF16, "osq")
                rs = p.sb([128, 512], F32, "rs")
                oout = p.sb([128, 1, Tq], BF16, "oout")
                for h in loop_iter(nc, 8, static):
                    c.dma('sp', kT[:], akT_d[ds(h, 1), :, 0:T].rearrange("a p t -> p a t"), writes=[kT])
                    c.dma('sp', vv[:], av_d[0:T, ds(h * 128, 128)].rearrange("(kb p) d -> p kb d", p=128), writes=[vv])
                    c.dma('sp', biasT[:], bias_d[ds(h * 6, 6), :, :].rearrange("d p q -> p d q"), writes=[biasT])
                    c.dma('sp', cfar[:], cfar_d[ds(h, 1), :, :].rearrange("a p s -> p a s"), writes=[cfar])
                    c.dma('sp', qT[:], aqT_d[ds(h, 1), :, 0:Tq].rearrange("a p t -> p a t"), writes=[qT])
                    for qb in range(Tq // 512):
                        q_ = qT
                        qsl = slice(qb * 512, (qb + 1) * 512)
                        def qk_exp(kb):
                            delta = kb * 128 - qb * 512
                            near = -128 <= delta <= 512
                            pts = []
                            pss = []
                            for m in range(2):
                                ps = pS[m][kb % 2]
                                pr = slice(m * 64, (m + 1) * 64)
                                c.op('pe', lambda e: e.matmul(ps.t[:, :], lhsT=kT.t[pr, 0, kb * 128:(kb + 1) * 128], rhs=q_.t[pr, 0, qsl],
                                                              start=True, stop=True), reads=[kT, q_], writes=[ps])
                                pss.append(ps)
                            for m in range(2):
                                ps = pss[m]
                                pt_ = pT[m][kb % 4]
                                if near:
                                    di = (delta + 128) // 128
                                    tm_ = tmpS[m]
                                    c.op('dve', lambda e: e.scalar_tensor_tensor(out=tm_.t[:, :], in0=ps.t[:, :], scalar=0.125,
                                                                                 in1=biasT.t[:, di, :], op0=ALU.mult, op1=ALU.add),
                                         reads=[ps, biasT], writes=[tm_])
                                    c.op('act', lambda e: e.activation(out=pt_.t[:, :], in_=tm_.t[:, :], func=AF.Exp),
                                         reads=[tm_], writes=[pt_])
                                else:
                                    side = 0 if delta > 0 else 1
                                    c.op('act', lambda e: e.activation(out=pt_.t[:, :], in_=ps.t[:, :], func=AF.Exp,
                                                                       bias=cfar.t[:, 0, side:side + 1], scale=0.125),
                                         reads=[ps, cfar], writes=[pt_])
                                pts.append(pt_)
                            return pts

                        def pv_z(kb, pts):
                            for m in range(2):
                                pt_ = pts[m]
                                c.op('pe', lambda e: e.matmul(pO[m].t[:, :], lhsT=vv.t[:, kb, :], rhs=pt_.t[:, :],
                                                              start=(kb == 0), stop=(kb == NKB - 1)), reads=[vv, pt_], writes=[pO[m]])
                                zeng = 'dve' if m == 0 else 'pool'
                                za = zacc[m][kb % 2]
                                if kb < 2:
                                    c.op(zeng, lambda e: e.tensor_copy(out=za.t[:, :], in_=pt_.t[:, :]), reads=[pt_], writes=[za])
                                else:
                                    c.op(zeng, lambda e: e.tensor_tensor(out=za.t[:, :], in0=za.t[:, :], in1=pt_.t[:, :], op=ALU.add),
                                         reads=[za, pt_], writes=[za])

                        pend = qk_exp(0)
                        for kb in range(NKB):
                            nxt = qk_exp(kb + 1) if kb + 1 < NKB else None
                            pv_z(kb, pend)
                            pend = nxt
                        for m in range(2):
                            c.op('pe', lambda e: e.matmul(pZ[m].t[:, :], lhsT=ones_f, rhs=zacc[m][0].t[:, :], start=True, stop=False),
                                 reads=[cf, zacc[m][0]], writes=[pZ[m]])
                            c.op('pe', lambda e: e.matmul(pZ[m].t[:, :], lhsT=ones_f, rhs=zacc[m][1].t[:, :], start=False, stop=True),
                                 reads=[cf, zacc[m][1]], writes=[pZ[m]])
                        for m in range(2):
                            c.op('dve', lambda e: e.reciprocal(out=rz[m].t[:, :], in_=pZ[m].t[:, :]), reads=[pZ[m]], writes=[rz[m]])
                            c.op('dve', lambda e: e.tensor_tensor(out=o01[m].t[:, :], in0=pO[m].t[:, :], in1=rz[m].t[:, :], op=ALU.mult),
                                 reads=[pO[m], rz[m]], writes=[o01[m]])
                        c.op('dve', lambda e: e.scalar_tensor_tensor(out=o01[0].t[:, :], in0=o01[1].t[:, :], scalar=neglam,
                                                                     in1=o01[0].t[:, :], op0=ALU.mult, op1=ALU.add),
                             reads=[o01[0], o01[1], der], writes=[o01[0]])
                        c.op('pool', lambda e: e.tensor_tensor(out=osq.t[:, :], in0=o01[0].t[:, :], in1=o01[0].t[:, :], op=ALU.mult),
                             reads=[o01[0]], writes=[osq])
                        c.op('pe', lambda e: e.matmul(pZ[0].t[:, :], lhsT=ones_bf, rhs=osq.t[:, :], start=True, stop=True),
                             reads=[cbf, osq], writes=[pZ[0]])
                        c.op('dve', lambda e: e.tensor_scalar(out=rs.t[:, :], in0=pZ[0].t[:, :], scalar1=1.0 / 128, scalar2=EPS,
                                                              op0=ALU.mult, op1=ALU.add), reads=[pZ[0]], writes=[rs])
                        c.op('act', lambda e: e.activation(out=rs.t[:, :], in_=rs.t[:, :], func=AF.Ln), reads=[rs], writes=[rs])
                        c.op('act', lambda e: e.activation(out=rs.t[:, :], in_=rs.t[:, :], func=AF.Exp, scale=-0.5), reads=[rs], writes=[rs])
                        oo = oout
                        c.op('dve', lambda e: e.scalar_tensor_tensor(out=oo.t[:, 0, qsl], in0=o01[0].t[:, :], scalar=gsubcol,
                                                                     in1=rs.t[:, :], op0=ALU.mult, op1=ALU.mult),
                             reads=[o01[0], rs, der], writes=[oo])
                    c.dma('sp', oaT_d[ds(h, 1), :, 0:Tq].rearrange("a p t -> p a t"), oout[:], reads=[oout])
                    c.end_iter()

        NG = T // 256
        if _en(5):
            with Pool_(c, nc) as p:
                S = [p.sb([128, 8, 128], F32, "S") for _ in range(2)]
                Sbf = [p.sb([128, 8, 128], BF16, "Sbf") for _ in range(2)]
                qTt = [p.sb([128, 8, 256], BF16, "qTt") for _ in range(2)]
                kTt = [p.sb([128, 8, 256], BF16, "kTt") for _ in range(2)]
                ktm = [p.sb([64, 4, 1024], BF16, "ktm") for _ in range(2)]
                gtm = [p.sb([64, 4, 1024], F32, "gtm") for _ in range(2)]
                vtm = [p.sb([64, 4, 1024], BF16, "vtm") for _ in range(2)]
                ostg = [p.sb([128, 8, 256], F32, "ostg") for _ in range(2)]
                ET = p.sb([128, 8, 64], F32, "ET")
                EiT = p.sb([128, 8, 64], F32, "EiT")
                Eitm = p.sb([64, 1024], F32, "Eitm")
                qt_ = p.sb([128, 8, 64], BF16, "qt_")
                kt_ = p.sb([128, 8, 64], BF16, "kt_")
                ktm_ = p.sb([64, 1024], BF16, "ktm_")
                ATm = p.sb([64, 8, 64], BF16, "ATm")
                Stmp = p.sb([128, 8, 128], F32, "Stmp")
                p_bT = p.ps([128, 8, 64], F32, "p_bT")
                p_btm = p.ps([64, 1024], F32, "p_btm")
                p_AT = p.ps([64, 8, 64], F32, "p_AT")
                p_oT = p.ps([128, 8, 64], F32, "p_oT")
                p_dS = p.ps([128, 8, 128], F32, "p_dS")
                for d_ in range(2):
                    c.op('pool', lambda e: e.memset(S[d_].t[:], 0.0), writes=[S[d_]])
                    c.op('pool', lambda e: e.memset(Sbf[d_].t[:], 0.0), writes=[Sbf[d_]])
                c.end_iter()
                dq = 'sp' if static else 'act'
                for i in loop_iter(nc, NG, static):
                    for d_ in range(2):
                        t0 = i * 256 if d_ == 0 else (NG - 1 - i) * 256
                        c.dma(dq, qTt[d_][:], gqT_d[:, :, ds(t0, 256)].rearrange("c p t -> p c t"), writes=[qTt[d_]])
                        c.dma(dq, kTt[d_][:], kT_d[d_][:, :, ds(t0, 256)].rearrange("c p t -> p c t"), writes=[kTt[d_]])
                        c.dma(dq, ktm[d_][:], ktm_d[d_][ds(t0, 256), :].rearrange("(a s) d -> s a d", s=64), writes=[ktm[d_]])
                        c.dma(dq, gtm[d_][:], g_d[d_][ds(t0, 256), :].rearrange("(a s) d -> s a d", s=64), writes=[gtm[d_]])
                        c.dma(dq, vtm[d_][:], gi_d[ds(t0, 256), :].rearrange("(a s) d -> s a d", s=64), writes=[vtm[d_]])
                    for cc in range(4):
                        for d_ in range(2):
                            ch = cc if d_ == 0 else 3 - cc
                            Um = U_f if d_ == 0 else Ut_f
                            last = 63 if d_ == 0 else 0
                            tsl = slice(ch * 64, (ch + 1) * 64)
                            for hh in range(8):
                                c.op('pe', lambda e: e.matmul(p_bT.t[:, hh, :], lhsT=gtm[d_].t[:, ch, hh * 128:(hh + 1) * 128], rhs=Um,
                                                              start=True, stop=True), reads=[gtm[d_], cf], writes=[p_bT])
                            for hf in range(2):
                                c.op('pe', lambda e: e.matmul(p_btm.t[:, hf * 512:(hf + 1) * 512], lhsT=Um,
                                                              rhs=gtm[d_].t[:, ch, hf * 512:(hf + 1) * 512], start=True, stop=True),
                                     reads=[gtm[d_], cf], writes=[p_btm])
                            c.op('act', lambda e: e.activation(out=ET.t[:], in_=p_bT.t[:], func=AF.Exp), reads=[p_bT], writes=[ET])
                            c.op('act', lambda e: e.activation(out=EiT.t[:], in_=p_bT.t[:], func=AF.Exp, scale=-1.0),
                                 reads=[p_bT], writes=[EiT])
                            c.op('act', lambda e: e.activation(out=Eitm.t[:], in_=p_btm.t[:], func=AF.Exp, scale=-1.0),
                                 reads=[p_btm], writes=[Eitm])
                            c.op('dve', lambda e: e.tensor_tensor(out=qt_.t[:], in0=qTt[d_].t[:, :, tsl], in1=ET.t[:], op=ALU.mult),
                                 reads=[qTt[d_], ET], writes=[qt_])
                            c.op('pool', lambda e: e.tensor_tensor(out=kt_.t[:], in0=kTt[d_].t[:, :, tsl], in1=EiT.t[:], op=ALU.mult),
                                 reads=[kTt[d_], EiT], writes=[kt_])
                            c.op('pool', lambda e: e.tensor_tensor(out=ktm_.t[:], in0=ktm[d_].t[:, ch, :], in1=Eitm.t[:], op=ALU.mult),
                                 reads=[ktm[d_], Eitm], writes=[ktm_])
                            for hh in range(8):
                                c.op('pe', lambda e: e.matmul(p_AT.t[:, hh, :], lhsT=kt_.t[:, hh, :], rhs=qt_.t[:, hh, :],
                                                              start=True, stop=True), reads=[kt_, qt_], writes=[p_AT])
                            c.op('dve', lambda e: e.tensor_tensor(out=ATm.t[:], in0=p_AT.t[:],
                                                                  in1=Um.unsqueeze(1).to_broadcast([64, 8, 64]), op=ALU.mult),
                                 reads=[p_AT, cf], writes=[ATm])
                            for hh in range(8):
                                c.op('pe', lambda e: e.matmul(p_oT.t[:, hh, :], lhsT=Sbf[d_].t[:, hh, :], rhs=qt_.t[:, hh, :],
                                                              start=True, stop=False), reads=[Sbf[d_], qt_], writes=[p_oT])
                                c.op('pe', lambda e: e.matmul(p_oT.t[:, hh, :], lhsT=vtm[d_].t[:, ch, hh * 128:(hh + 1) * 128],
                                                              rhs=ATm.t[:, hh, :], start=False, stop=True),
                                     reads=[vtm[d_], ATm], writes=[p_oT])
                            c.op('act', lambda e: e.activation(out=ostg[d_].t[:, :, tsl], in_=p_oT.t[:], func=AF.Copy),
                                 reads=[p_oT], writes=[ostg[d_]])
                            for hh in range(8):
                                c.op('pe', lambda e: e.matmul(p_dS.t[:, hh, :], lhsT=ktm_.t[:, hh * 128:(hh + 1) * 128],
                                                              rhs=vtm[d_].t[:, ch, hh * 128:(hh + 1) * 128], start=True, stop=True),
                                     reads=[ktm_, vtm[d_]], writes=[p_dS])
                            c.op('dve', lambda e: e.tensor_tensor(out=Stmp.t[:], in0=S[d_].t[:], in1=p_dS.t[:], op=ALU.add),
                                 reads=[S[d_], p_dS], writes=[Stmp])
                            c.op('dve', lambda e: e.tensor_tensor(out=S[d_].t[:], in0=Stmp.t[:],
                                                                  in1=ET.t[:, :, last:last + 1].to_broadcast([128, 8, 128]), op=ALU.mult),
                                 reads=[Stmp, ET], writes=[S[d_]])
                            c.op('act', lambda e: e.activation(out=Sbf[d_].t[:], in_=S[d_].t[:], func=AF.Copy), reads=[S[d_]], writes=[Sbf[d_]])
                    for d_ in range(2):
                        t0 = i * 256 if d_ == 0 else (NG - 1 - i) * 256
                        c.dma(dq, oT_d[d_][:, :, ds(t0, 256)].rearrange("c p t -> p c t"), ostg[d_][:], reads=[ostg[d_]])
                    c.end_iter()

        if _en(6):
            with Pool_(c, nc) as p:
                Wa = load_w(p, G["w_pa"][:, :])
                Wb = load_w(p, G["w_pb"][:, :])
                Wo = load_w(p, G["w_o"][:, :])
                gpost = p.sb([128, 1024], F32, "gpost")
                c.dma('sp', gpost[:], vb_d[:, 0:1024], writes=[gpost])
                c.end_iter()
                oa = p.sb([128, 8, 512], BF16, "oa")
                of_ = p.sb([128, 8, 512], F32, "of")
                ob_ = p.sb([128, 8, 512], F32, "ob")
                og = p.sb([128, 8, 512], BF16, "og")
                ga = p.sb([128, 8, 512], BF16, "ga")
                gb = p.sb([128, 8, 512], BF16, "gb")
                obn = p.sb([128, 8, 512], BF16, "obn")
                mT = p.sb([128, 8, 512], BF16, "mT")
                sq = [p.sb([128, 512], BF16, "sq") for _ in range(2)]
                rs = [p.sb([128, 512], F32, "rs") for _ in range(2)]
                t1 = [p.sb([128, 512], F32, "t1") for _ in range(2)]
                t2 = [p.sb([128, 512], F32, "t2") for _ in range(2)]
                xt4 = p.sb([128, 4, 1024], F32, "xt4")
                yt4 = p.sb([128, 4, 1024], F32, "yt4")
                junk = p.sb([128, 512], BF16, "junk")
                ss = p.sb([128, 16], F32, "ss")
                psr = PsRot(p, 8)
                for i in loop_iter(nc, NBQ, static):
                    tsl = ds(i * 512, 512)
                    c.dma('sp', oa[:], oaT_d[:, :, tsl].rearrange("c p t -> p c t"), writes=[oa])
                    c.dma('sp', of_[:], oT_d[0][:, :, tsl].rearrange("c p t -> p c t"), writes=[of_])
                    c.dma('sp', ob_[:], oT_d[1][:, :, tsl].rearrange("c p t -> p c t"), writes=[ob_])
                    c.dma('sp', og[:], gogT_d[:, :, tsl].rearrange("c p t -> p c t"), writes=[og])
                    c.dma('sp', ga[:], gaT_d[:, :, tsl].rearrange("c p t -> p c t"), writes=[ga])
                    c.dma('sp', gb[:], gbT_d[:, :, tsl].rearrange("c p t -> p c t"), writes=[gb])
                    c.dma('sp', xt4[:], x_d[ds(i * 512, 512), :].rearrange("(j p) d -> p j d", p=128), writes=[xt4])
                    c.op('pool', lambda e: e.tensor_tensor(out=of_.t[:], in0=of_.t[:], in1=ob_.t[:], op=ALU.add),
                         reads=[of_, ob_], writes=[of_])
                    for hh in range(8):
                        s_ = sq[hh % 2]
                        r_ = rs[hh % 2]
                        c.op('act', lambda e: e.activation(out=s_.t[:, :], in_=of_.t[:, hh, :], func=AF.Square), reads=[of_], writes=[s_])
                        ps = psr.next()
                        c.op('pe', lambda e: e.matmul(ps.t[:, :], lhsT=ones_bf, rhs=s_.t[:, :], start=True, stop=True),
                             reads=[cbf, s_], writes=[ps])
                        c.op('dve', lambda e: e.tensor_scalar(out=r_.t[:, :], in0=ps.t[:, :], scalar1=1.0 / 128, scalar2=EPS,
                                                              op0=ALU.mult, op1=ALU.add), reads=[ps], writes=[r_])
                        c.op('act', lambda e: e.activation(out=r_.t[:, :], in_=r_.t[:, :], func=AF.Ln), reads=[r_], writes=[r_])
                        c.op('act', lambda e: e.activation(out=r_.t[:, :], in_=r_.t[:, :], func=AF.Exp, scale=-0.5), reads=[r_], writes=[r_])
                        c.op('dve', lambda e: e.scalar_tensor_tensor(out=r_.t[:, :], in0=of_.t[:, hh, :], scalar=ghcol, in1=r_.t[:, :],
                                                                     op0=ALU.mult, op1=ALU.mult), reads=[of_, r_, vc], writes=[r_])
                        c.op('pool', lambda e: e.tensor_tensor(out=obn.t[:, hh, :], in0=r_.t[:, :], in1=og.t[:, hh, :], op=ALU.mult),
                             reads=[r_, og], writes=[(obn, hh)])
                    for j in range(8):
                        psa = psr.next()
                        for k in range(8):
                            c.op('pe', lambda e: e.matmul(psa.t[:, :], lhsT=Wa.t[:, k, j * 128:(j + 1) * 128], rhs=oa.t[:, k, :],
                                                          start=(k == 0), stop=(k == 7)), reads=[Wa, oa], writes=[psa])
                        psb = psr.next()
                        for k in range(8):
                            c.op('pe', lambda e: e.matmul(psb.t[:, :], lhsT=Wb.t[:, k, j * 128:(j + 1) * 128], rhs=obn.t[:, k, :],
                                                          start=(k == 0), stop=(k == 7)), reads=[Wb, obn], writes=[psb])
                        a_, b_ = t1[j % 2], t2[j % 2]
                        c.op('dve', lambda e: e.tensor_tensor(out=a_.t[:, :], in0=psa.t[:, :], in1=ga.t[:, j, :], op=ALU.mult),
                             reads=[psa, ga], writes=[a_])
                        c.op('dve', lambda e: e.tensor_tensor(out=b_.t[:, :], in0=psb.t[:, :], in1=gb.t[:, j, :], op=ALU.mult),
                             reads=[psb, gb], writes=[b_])
                        c.op('pool', lambda e: e.tensor_tensor(out=mT.t[:, j, :], in0=a_.t[:, :], in1=b_.t[:, :], op=ALU.add),
                             reads=[a_, b_], writes=[(mT, j)])
                    for s in range(4):
                        pss = []
                        for hf in range(2):
                            ps = psr.next()
                            pss.append(ps)
                            for k in range(8):
                                c.op('pe', lambda e: e.matmul(ps.t[:, :], lhsT=mT.t[:, k, s * 128:(s + 1) * 128],
                                                              rhs=Wo.t[:, k, hf * 512:(hf + 1) * 512], start=(k == 0), stop=(k == 7)),
                                     reads=[Wo, mT], writes=[ps])
                            c.op('act', lambda e: e.activation(out=junk.t[:, :], in_=ps.t[:, :], func=AF.Square,
                                                               accum_out=ss.t[:, 4 * s + hf:4 * s + hf + 1]), reads=[ps], writes=[junk, (ss, s)])
                        res_tail(c, ss, pss, gpost, xt4, xt4.t[:, s, :], (yt4, s), yt4.t[:, s, :], s)
                    c.dma('sp', x1_d[ds(i * 512, 512), :].rearrange("(j p) d -> p j d", p=128), yt4[:], reads=[yt4])
                    c.end_iter()

        if _en(7):
            with Pool_(c, nc) as p:
                Wu = load_w(p, G["w_up"][:, :], ncols=4096, nk=8)
                Wd = load_w(p, G["w_dn"][:, :], ncols=1024, nk=32)
                gpost = p.sb([128, 1024], F32, "gpost2")
                c.dma('sp', gpost[:], vb_d[:, 1024:2048], writes=[gpost])
                c.end_iter()
                xt2 = p.sb([128, 2, 1024], F32, "xt2")
                yt2 = p.sb([128, 2, 1024], F32, "yt2")
                h2 = p.sb([128, 8, 256], BF16, "h2")
                uT = p.sb([128, 32, 256], BF16, "uT")
                rl = [p.sb([128, 256], BF16, "rl") for _ in range(2)]
                junkb = p.sb([128, 1024], BF16, "junkb")
                xn = [p.sb([128, 1024], BF16, "xn") for _ in range(2)]
                ss = p.sb([128, 16], F32, "ss")
                ptr = [p.ps([128, 8, 128], BF16, "ptr") for _ in range(2)]
                psr = PsRot(p, 6)
                for i in loop_iter(nc, Tq // 256, static):
                    c.dma('sp', xt2[:], x1_d[ds(i * 256, 256), :].rearrange("(j p) d -> p j d", p=128), writes=[xt2])
                    for j in range(2):
                        x_ = xt2
                        c.op('act', lambda e: e.activation(out=junkb.t[:], in_=x_.t[:, j, :], func=AF.Square, accum_out=ss.t[:, 8 + j:9 + j]),
                             reads=[x_], writes=[junkb, ss])
                        c.op('dve', lambda e: e.tensor_scalar(out=ss.t[:, 10 + j:11 + j], in0=ss.t[:, 8 + j:9 + j], scalar1=1.0 / D,
                                                              scalar2=EPS, op0=ALU.mult, op1=ALU.add), reads=[ss], writes=[ss])
                        c.op('act', lambda e: e.activation(out=ss.t[:, 10 + j:11 + j], in_=ss.t[:, 10 + j:11 + j], func=AF.Sqrt),
                             reads=[ss], writes=[ss])
                        c.op('dve', lambda e: e.reciprocal(out=ss.t[:, 12 + j:13 + j], in_=ss.t[:, 10 + j:11 + j]), reads=[ss], writes=[ss])
                        c.op('dve', lambda e: e.tensor_scalar(out=xn[j].t[:], in0=x_.t[:, j, :], scalar1=ss.t[:, 12 + j:13 + j],
                                                              scalar2=None, op0=ALU.mult), reads=[x_, ss], writes=[xn[j]])
                        for cc in range(8):
                            c.op('pe', lambda e: e.transpose(out=ptr[j].t[:, cc, :], in_=xn[j].t[:, cc * 128:(cc + 1) * 128],
                                                             identity=ident), reads=[xn[j], cbf], writes=[ptr[j]])
                        c.op('act', lambda e: e.activation(out=h2.t[:, :, j * 128:(j + 1) * 128], in_=ptr[j].t[:], func=AF.Copy),
                             reads=[ptr[j]], writes=[h2])
                    c.op('pool', lambda e: e.tensor_tensor(out=h2.t[:], in0=h2.t[:], in1=g2col.unsqueeze(2).to_broadcast([128, 8, 256]),
                                                           op=ALU.mult), reads=[h2, vc], writes=[h2])
                    for f in range(32):
                        ps = psr.next()
                        for k in range(8):
                            c.op('pe', lambda e: e.matmul(ps.t[:, 0:256], lhsT=Wu.t[:, k, f * 128:(f + 1) * 128], rhs=h2.t[:, k, :],
                                                          start=(k == 0), stop=(k == 7)), reads=[Wu, h2], writes=[ps])
                        r_ = rl[f % 2]
                        c.op('act', lambda e: e.activation(out=r_.t[:, :], in_=ps.t[:, 0:256], func=AF.Relu), reads=[ps], writes=[r_])
                        c.op('pool' if f % 2 else 'dve', lambda e: e.tensor_tensor(out=uT.t[:, f, :], in0=r_.t[:, :], in1=r_.t[:, :], op=ALU.mult),
                             reads=[r_], writes=[(uT, f)])
                    for s in range(2):
                        pss = []
                        for hf in range(2):
                            ps = psr.next()
                            pss.append(ps)
                            for k in range(32):
                                c.op('pe', lambda e: e.matmul(ps.t[:, :], lhsT=uT.t[:, k, s * 128:(s + 1) * 128],
                                                              rhs=Wd.t[:, k, hf * 512:(hf + 1) * 512], start=(k == 0), stop=(k == 31)),
                                     reads=[Wd, uT], writes=[ps])
                            c.op('act', lambda e: e.activation(out=junkb.t[:, 0:512], in_=ps.t[:, :], func=AF.Square,
                                                               accum_out=ss.t[:, 4 * s + hf:4 * s + hf + 1]), reads=[ps], writes=[junkb, (ss, s)])
                        res_tail(c, ss, pss, gpost, xt2, xt2.t[:, s, :], (yt2, s), yt2.t[:, s, :], s)
                    c.dma('sp', y_d[ds(i * 256, 256), :].rearrange("(j p) d -> p j d", p=128), yt2[:], reads=[yt2])
                    c.end_iter()


def res_tail(c, ss, pss, gpost, xb, xv, yb, yv, s):
    k = (ss, s)
    c0 = 4 * s
    c.op('dve', lambda e: e.tensor_tensor(out=ss.t[:, c0 + 2:c0 + 3], in0=ss.t[:, c0:c0 + 1], in1=ss.t[:, c0 + 1:c0 + 2], op=ALU.add),
         reads=[k], writes=[k])
    c.op('dve', lambda e: e.tensor_scalar(out=ss.t[:, c0 + 2:c0 + 3], in0=ss.t[:, c0 + 2:c0 + 3], scalar1=1.0 / D, scalar2=EPS,
                                          op0=ALU.mult, op1=ALU.add), reads=[k], writes=[k])
    c.op('act', lambda e: e.activation(out=ss.t[:, c0 + 2:c0 + 3], in_=ss.t[:, c0 + 2:c0 + 3], func=AF.Ln), reads=[k], writes=[k])
    c.op('act', lambda e: e.activation(out=ss.t[:, c0 + 3:c0 + 4], in_=ss.t[:, c0 + 2:c0 + 3], func=AF.Exp, scale=-0.5), reads=[k], writes=[k])
    for hf in range(2):
        cs = slice(hf * 512, (hf + 1) * 512)
        c.op('dve', lambda e: e.scalar_tensor_tensor(out=yv[:, cs], in0=pss[hf].t[:, :], scalar=ss.t[:, c0 + 3:c0 + 4], in1=gpost.t[:, cs],
                                                     op0=ALU.mult, op1=ALU.mult), reads=[pss[hf], k, gpost], writes=[yb])
    c.op('pool', lambda e: e.tensor_tensor(out=yv, in0=yv, in1=xv, op=ALU.add), reads=[yb, xb], writes=[yb])


def _t5_bucket_np(rel):
    nb = 16
    ret = (rel > 0).astype(np.int64) * nb
    n = np.abs(rel)
    max_exact = 8
    is_small = n < max_exact
    nf = np.maximum(n, 1).astype(np.float32)
    large = max_exact + (np.log(nf / np.float32(max_exact)) / np.float32(math.log(128 / max_exact))
                         * np.float32(nb - max_exact)).astype(np.int64)
    large = np.minimum(large, nb - 1)
    return ret + np.where(is_small, n, large)


def _consts():
    cbf = np.zeros((128, 256), np.float32)
    cbf[:, 0:128] = np.eye(128)
    cbf[:, 128:256] = 1.0
    cf = np.zeros((128, 384), np.float32)
    cf[:, 256:384] = 1.0
    cf[:, 0:128] = np.eye(128)[::-1]
    U = np.triu(np.ones((64, 64), np.float32))
    cf[0:64, 128:192] = U
    cf[0:64, 192:256] = U.T
    return cbf.astype(ml_dtypes.bfloat16), cf


def _oht(flip):
    d = J0 - np.arange(NJ)
    if flip:
        d = -d
    b = _t5_bucket_np(d)
    oh = np.zeros((32, NJ), np.float32)
    oh[b, np.arange(NJ)] = 1.0
    return oh


def _cols(v):
    return np.ascontiguousarray(v.reshape(8, 128).T)


def _vec_inputs(I, swap):
    lbf, lbb = I["lb_fwd"], I["lb_bwd"]
    if swap:
        lbf, lbb = lbb, lbf
    vc = np.zeros((128, 50), np.float32)
    vc[:, 0:8] = _cols(I["g_mix_pre"][0])
    vc[:, 8:16] = _cols(I["g_mlp_pre"][0])
    vc[:, 16:24] = _cols(lbf[0])
    vc[:, 24:32] = _cols(lbf[1])
    vc[:, 32:40] = _cols(lbb[0])
    vc[:, 40:48] = _cols(lbb[1])
    vc[:, 48] = I["g_attn_sub"][0]
    vc[:, 49] = I["g_hgrn_out"][0]
    row = np.concatenate([I["g_mix_post"][0], I["g_mlp_post"][0], lbf[0], lbf[1], lbb[0], lbb[1],
                          I["lam_q1"][0], I["lam_q2"][0], I["lam_k1"][0], I["lam_k2"][0]]).astype(np.float32)
    vb = np.ascontiguousarray(np.broadcast_to(row[None, :], (128, 6400)))
    return vc, vb


def make_in_maps(I, n_cores=8, TS=None, TP=None):
    I = {k: np.asarray(v) for k, v in I.items()}
    xs_all, xp_all = I["x_sample"], I["x_prompt"]
    w_in = np.ascontiguousarray(I["w_in"][0])
    cbf, cf = _consts()
    relb = np.ascontiguousarray(I["rel_bias"], dtype=np.float32)
    wf = [np.ascontiguousarray(w_in[:, 4096:6144]),
          np.ascontiguousarray(np.concatenate([w_in[:, 5120:6144], w_in[:, 4096:5120]], axis=1))]
    vcs, vbs = _vec_inputs(I, False)
    vcp = [vcs, _vec_inputs(I, True)[0]]
    vbp = [vbs, _vec_inputs(I, True)[1]]
    oht = [_oht(False), _oht(True)]

    def cfar(flip):
        a, b = (31, 15) if not flip else (15, 31)
        out = np.zeros((8, 128, 2), np.float32)
        out[:, :, 0] = relb[a][:, None]
        out[:, :, 1] = relb[b][:, None]
        return out
    cfars = [cfar(False), cfar(True)]
    shared = {"w_in": w_in, "w_pa": np.ascontiguousarray(I["w_proj_a"][0]), "w_pb": np.ascontiguousarray(I["w_proj_b"][0]),
              "w_o": np.ascontiguousarray(I["w_out"][0]), "w_up": np.ascontiguousarray(I["w_mlp_up"][0]),
              "w_dn": np.ascontiguousarray(I["w_mlp_down"][0]), "relb": relb, "cbf": cbf, "cf": cf,
              "vc_s": vcs, "vb_s": vbs, "oht_s": oht[0], "cfar_s": cfars[0]}
    maps = []
    for cid in range(n_cores):
        par = cid % 2
        xp = xp_all[(cid // 2) % xp_all.shape[0]]
        if par:
            xp = xp[::-1]
        m = dict(shared)
        m.update({"xs": np.ascontiguousarray(xs_all[cid % xs_all.shape[0]]), "xp": np.ascontiguousarray(xp),
                  "w_fsw": wf[par], "vc_p": vcp[par], "vb_p": vbp[par], "oht_p": oht[par], "cfar_p": cfars[par]})
        maps.append(m)
    return maps


_NC_CACHE = {}


def kernel(**inputs):
    TS = inputs["x_sample"].shape[1]
    TP = inputs["x_prompt"].shape[1]
    key = (TS, TP)
    if key not in _NC_CACHE:
        _NC_CACHE[key] = build(TS, TP)
    nc = _NC_CACHE[key]
    maps = make_in_maps(inputs)
    res = run_bass_kernel_spmd(nc, maps, core_ids=list(range(8)))
    B, Bs = inputs["x_prompt"].shape[0], inputs["x_sample"].shape[0]
    y_s = np.stack([np.asarray(res.results[cid]["ys"], dtype=np.float32) for cid in range(Bs)], axis=0)
    y_p = np.zeros((B, TP, D), np.float32)
    hq = TP // 2
    for cid in range(8):
        b, par = cid // 2, cid % 2
        yp = np.asarray(res.results[cid]["yp"], dtype=np.float32)
        if par == 0:
            y_p[b, 0:hq] = yp
        else:
            y_p[b, hq:] = yp[::-1]
    return (y_p, y_s)
```

```python
import math
import numpy as np
import ml_dtypes
from contextlib import ExitStack
import concourse.bass as bass
import concourse.mybir as mybir
from concourse.bass_utils import run_bass_kernel_spmd

F32 = mybir.dt.float32
BF16 = mybir.dt.bfloat16
AF = mybir.ActivationFunctionType
ALU = mybir.AluOpType
AX = mybir.AxisListType
NDMA = 16
EPS = 1e-6
D = 1024
NJ = 1280
J0 = 639


class Buf:
    def __init__(self, ctx, t):
        self.t = t
        self.wr = None
        self.rd = []
        self.subs = {}
        ctx.bufs.append(self)

    def __getitem__(self, k):
        return self.t[k]


class EngW:
    def __init__(self, name, eng, sem, same_sync):
        self.name, self.eng, self.sem = name, eng, sem
        self.count = 0
        self.waited = {}
        self.same_sync = same_sync

    def wait(self, deps):
        for d in deps:
            if d is None:
                continue
            sem, val = d
            if val <= 0:
                continue
            if sem is self.sem and not self.same_sync:
                continue
            if self.waited.get(sem, 0) >= val:
                continue
            self.eng.wait_ge(sem, val)
            self.waited[sem] = val


class Ctx:
    def __init__(self, nc, es):
        self.nc = nc
        self.E = {}
        self.bufs = []
        for name, eng, ss in [('pe', nc.tensor, False), ('act', nc.scalar, True),
                              ('dve', nc.vector, True), ('pool', nc.gpsimd, True),
                              ('sp', nc.sync, False)]:
            sem = es.enter_context(nc.semaphore('s_' + name))
            self.E[name] = EngW(name, eng, sem, ss)
        self.dma_sems = [es.enter_context(nc.semaphore('d%d' % i)) for i in range(2 * NDMA)]
        self.dma_val = [0] * (2 * NDMA)
        self.dma_next = [0, 0]

    @staticmethod
    def _bk(x):
        return x if isinstance(x, tuple) else (x, None)

    def _deps(self, reads, writes, extra):
        deps = list(extra)
        for x in reads:
            b, k = self._bk(x)
            deps.append(b.wr)
            if k is None:
                for sw, srd in b.subs.values():
                    deps.append(sw)
            elif k in b.subs:
                deps.append(b.subs[k][0])
        for x in writes:
            b, k = self._bk(x)
            deps.append(b.wr)
            deps.extend(b.rd)
            if k is None:
                for sw, srd in b.subs.values():
                    deps.append(sw)
                    deps.extend(srd)
            elif k in b.subs:
                deps.append(b.subs[k][0])
                deps.extend(b.subs[k][1])
        return deps

    def _mark(self, tok, reads, writes):
        for x in reads:
            b, k = self._bk(x)
            if k is None:
                b.rd.append(tok)
            else:
                b.subs.setdefault(k, [None, []])[1].append(tok)
        for x in writes:
            b, k = self._bk(x)
            if k is None:
                b.wr = tok
                b.rd = []
                b.subs = {}
            else:
                b.subs[k] = [tok, []]

    def op(self, en, fn, reads=(), writes=(), extra=()):
        e = self.E[en]
        e.wait(self._deps(reads, writes, extra))
        ins = fn(e.eng)
        e.count += 1
        ins.then_inc(e.sem, 1)
        tok = (e.sem, e.count)
        self._mark(tok, reads, writes)
        return tok

    def dma(self, en, out, in_, reads=(), writes=(), extra=()):
        e = self.E[en]
        grp = 1 if en == 'pool' else 0
        k = grp * NDMA + self.dma_next[grp]
        self.dma_next[grp] = (self.dma_next[grp] + 1) % NDMA
        sem = self.dma_sems[k]
        e.wait(self._deps(reads, writes, extra) + [(sem, self.dma_val[k])])
        ins = e.eng.dma_start(out=out, in_=in_)
        self.dma_val[k] += 16
        ins.then_inc(sem, 16)
        tok = (sem, self.dma_val[k])
        self._mark(tok, reads, writes)
        return tok

    def end_iter(self):
        nc = self.nc
        sp = self.E['sp']
        sp.wait([(s, v) for s, v in zip(self.dma_sems, self.dma_val)])
        nc.all_engine_barrier()
        for e in self.E.values():
            if e.count:
                nc.sync.sem_clear(e.sem)
        for s, v in zip(self.dma_sems[:NDMA], self.dma_val[:NDMA]):
            if v:
                nc.sync.sem_clear(s)
        nc.all_engine_barrier()
        for e in self.E.values():
            e.count = 0
            e.waited = {}
        self.dma_val = [0] * NDMA + self.dma_val[NDMA:]
        self.dma_next = [0, self.dma_next[1]]
        for b in self.bufs:
            b.wr = None
            b.rd = []
            b.subs = {}


def ds(start, size):
    if isinstance(start, int):
        return slice(start, start + size)
    return bass.ds(start, size)


import os
_LOOPK = [0]


class _Stop(Exception):
    pass


def _en(n):
    ph = os.environ.get("KPH")
    return ph is None or str(n) in ph.split(",")


def loop_iter(nc, n, static):
    k = _LOOPK[0] % 7
    _LOOPK[0] += 1
    en = os.environ.get("KLOOPS")
    if en is not None and str(k) not in en.split(","):
        return
    if static:
        for i in range(n):
            yield i
    else:
        with nc.Fori(0, n) as i:
            yield i


_PROBE = [0]


def free_regs(eng):
    regs = []
    try:
        for k in range(200):
            _PROBE[0] += 1
            regs.append(eng.alloc_register("probe%d" % _PROBE[0]))
    except Exception:
        pass
    for r in regs:
        eng.free_register(r)
    return len(regs)


_UID = [0]


class Pool_:
    def __init__(self, c, nc):
        self.c, self.nc = c, nc
        self.es = ExitStack()
        self.n = 0

    def __enter__(self):
        self.es.__enter__()
        return self

    def __exit__(self, *a):
        return self.es.__exit__(*a)

    def sb(self, shape, dt, name=None):
        self.n += 1
        _UID[0] += 1
        t = self.es.enter_context(self.nc.sbuf_tensor("%s_%d" % (name or "t", _UID[0]), shape, dt))
        return Buf(self.c, t)

    def ps(self, shape, dt=F32, name=None):
        self.n += 1
        _UID[0] += 1
        t = self.es.enter_context(self.nc.psum_tensor("%s_%d" % (name or "p", _UID[0]), shape, dt))
        return Buf(self.c, t)


def build(TS, TP, dbg=False):
    nc = bass.Bass("TRN2", target_bir_lowering=False)
    TM = max(TS, TP)
    TQP = TP // 2

    def din(name, shape, dt=F32):
        return nc.dram_tensor(name, shape, dt, kind="ExternalInput").ap()

    def dscr(name, shape, dt):
        return nc.dram_tensor(name, shape, dt, kind="ExternalOutput" if dbg else "Internal").ap()

    xs = din("xs", [TS, D])
    xp = din("xp", [TP, D])
    w_in = din("w_in", [D, 10 * D])
    w_fsw = din("w_fsw", [D, 2 * D])
    w_pa = din("w_pa", [D, D])
    w_pb = din("w_pb", [D, D])
    w_o = din("w_o", [D, D])
    w_up = din("w_up", [D, 4 * D])
    w_dn = din("w_dn", [4 * D, D])
    vc_in = {"s": din("vc_s", [128, 50]), "p": din("vc_p", [128, 50])}
    vb_in = {"s": din("vb_s", [128, 6400]), "p": din("vb_p", [128, 6400])}
    oht_in = {"s": din("oht_s", [32, NJ]), "p": din("oht_p", [32, NJ])}
    cfar_in = {"s": din("cfar_s", [8, 128, 2]), "p": din("cfar_p", [8, 128, 2])}
    relb_in = din("relb", [32, 8])
    cbf_in = din("cbf", [128, 256], BF16)
    cf_in = din("cf", [128, 384])
    ys = nc.dram_tensor("ys", [TS, D], F32, kind="ExternalOutput").ap()
    yp = nc.dram_tensor("yp", [TQP, D], F32, kind="ExternalOutput").ap()

    hT_d = dscr("hT_d", [8, 128, TM], BF16)
    aqT_d = dscr("aqT_d", [8, 128, TM], BF16)
    akT_d = dscr("akT_d", [8, 128, TM], BF16)
    gqT_d = dscr("gqT_d", [8, 128, TM], BF16)
    kT_d = [dscr("kfT_d", [8, 128, TM], BF16), dscr("kbT_d", [8, 128, TM], BF16)]
    av_d = dscr("av_d", [TM, D], BF16)
    gi_d = dscr("gi_d", [TM, D], BF16)
    ktm_d = [dscr("kftm_d", [TM, D], BF16), dscr("kbtm_d", [TM, D], BF16)]
    g_d = [dscr("gf_d", [TM, D], F32), dscr("gb_d", [TM, D], F32)]
    gogT_d = dscr("gogT_d", [8, 128, TM], BF16)
    gaT_d = dscr("gaT_d", [8, 128, TM], BF16)
    gbT_d = dscr("gbT_d", [8, 128, TM], BF16)
    oaT_d = dscr("oaT_d", [8, 128, TM], BF16)
    oT_d = [dscr("ofT_d", [8, 128, TM], F32), dscr("obT_d", [8, 128, TM], F32)]
    x1_d = dscr("x1_d", [TM, D], F32)
    tab_d = dscr("tab_d", [8, NJ], F32)
    bias_d = dscr("bias_d", [48, 128, 512], F32)

    with ExitStack() as es:
        c = Ctx(nc, es)

        with Pool_(c, nc) as gp:
            cbf = gp.sb([128, 256], BF16, "cbf")
            cf = gp.sb([128, 384], F32, "cf")
            c.dma('sp', cbf[:], cbf_in[:, :], writes=[cbf])
            c.dma('sp', cf[:], cf_in[:, :], writes=[cf])
            ident = cbf.t[:, 0:128]
            ones_bf = cbf.t[:, 128:256]
            antiI = cf.t[:, 0:128]
            U_f = cf.t[0:64, 128:192]
            Ut_f = cf.t[0:64, 192:256]
            ones_f = cf.t[:, 256:384]
            c.end_iter()

            for job in os.environ.get("KJOBS", "sp"):
                T = TS if job == "s" else TP
                Tq = TS if job == "s" else TQP
                x_d = xs if job == "s" else xp
                y_d = ys if job == "s" else yp
                run_job(nc, c, job, T, Tq, x_d, y_d, locals())
    print('free regs:', {n: free_regs(e.eng) for n, e in c.E.items()}, 'ninstr', nc.n_instructions if not callable(nc.n_instructions) else nc.n_instructions())
    return nc


def run_job(nc, c, job, T, Tq, x_d, y_d, G):
    ident, ones_bf, antiI, U_f, Ut_f, ones_f = G["ident"], G["ones_bf"], G["antiI"], G["U_f"], G["Ut_f"], G["ones_f"]
    cbf, cf = G["cbf"], G["cf"]
    vc_d, vb_d, oht_d, cfar_d = G["vc_in"][job], G["vb_in"][job], G["oht_in"][job], G["cfar_in"][job]
    w_in, w_fsw = G["w_in"], G["w_fsw"]
    hT_d, aqT_d, akT_d, gqT_d, kT_d = G["hT_d"], G["aqT_d"], G["akT_d"], G["gqT_d"], G["kT_d"]
    av_d, gi_d, ktm_d, g_d = G["av_d"], G["gi_d"], G["ktm_d"], G["g_d"]
    gogT_d, gaT_d, gbT_d, oaT_d, oT_d, x1_d = G["gogT_d"], G["gaT_d"], G["gbT_d"], G["oaT_d"], G["oT_d"], G["x1_d"]
    tab_d, bias_d, relb_in = G["tab_d"], G["bias_d"], G["relb_in"]
    NB = T // 512
    NBQ = Tq // 512
    static = (job == "p")

    with Pool_(c, nc) as jp:
        vc = jp.sb([128, 50], F32, "vc")
        c.dma('sp', vc[:], vc_d[:, :], writes=[vc])
        der = jp.sb([128, 32], F32, "der")
        lamt = jp.sb([128, 256], F32, "lamt")
        c.dma('sp', lamt[:], vb_d[:, 6144:6400], writes=[lamt])
        c.op('dve', lambda e: e.tensor_tensor(out=der.t[:, 0:8], in0=vc.t[:, 24:32], in1=vc.t[:, 16:24], op=ALU.subtract),
             reads=[vc], writes=[der])
        c.op('dve', lambda e: e.tensor_tensor(out=der.t[:, 8:16], in0=vc.t[:, 40:48], in1=vc.t[:, 32:40], op=ALU.subtract),
             reads=[vc], writes=[der])
        c.op('act', lambda e: e.activation(out=der.t[:, 0:16], in_=der.t[:, 0:16], func=AF.Sigmoid), reads=[der], writes=[der])
        c.op('dve', lambda e: e.tensor_scalar(out=der.t[:, 16:17], in0=vc.t[:, 48:49], scalar1=0.8, scalar2=None, op0=ALU.mult),
             reads=[vc], writes=[der])
        c.op('dve', lambda e: e.tensor_tensor(out=lamt.t[:, 0:128], in0=lamt.t[:, 0:128], in1=lamt.t[:, 128:256], op=ALU.mult),
             reads=[lamt], writes=[lamt])
        c.op('dve', lambda e: e.reduce_sum(out=der.t[:, 20:22], in_=lamt.t[:, 0:128].rearrange("p (a b) -> p a b", a=2), axis=AX.X),
             reads=[lamt, der], writes=[der])
        c.op('act', lambda e: e.activation(out=der.t[:, 20:22], in_=der.t[:, 20:22], func=AF.Exp), reads=[der], writes=[der])
        c.op('dve', lambda e: e.tensor_tensor(out=der.t[:, 22:23], in0=der.t[:, 21:22], in1=der.t[:, 20:21], op=ALU.subtract),
             reads=[der], writes=[der])
        c.op('dve', lambda e: e.tensor_scalar(out=der.t[:, 22:23], in0=der.t[:, 22:23], scalar1=-0.2, scalar2=None, op0=ALU.add),
             reads=[der], writes=[der])
        c.end_iter()
        g1col = vc.t[:, 0:8]
        g2col = vc.t[:, 8:16]
        omlb_col = [der.t[:, 0:8], der.t[:, 8:16]]
        gsubcol = der.t[:, 16:17]
        ghcol = vc.t[:, 49:50]
        neglam = der.t[:, 22:23]

        if _en(0):
            with Pool_(c, nc) as p:
                oht = p.sb([32, NJ], F32)
                relb = p.sb([32, 8], F32)
                tabs = p.sb([8, NJ], F32)
                c.dma('sp', oht[:], oht_d[:, :], writes=[oht])
                c.dma('sp', relb[:], relb_in[:, :], writes=[relb])
                pt = p.ps([8, 512], F32)
                for k0 in range(0, NJ, 512):
                    n = min(512, NJ - k0)
                    c.op('pe', lambda e: e.matmul(pt.t[:, 0:n], lhsT=relb.t[:, :], rhs=oht.t[:, k0:k0 + n], start=True, stop=True),
                         reads=[relb, oht], writes=[pt])
                    c.op('act', lambda e: e.activation(out=tabs.t[:, k0:k0 + n], in_=pt.t[:, 0:n], func=AF.Copy),
                         reads=[pt], writes=[tabs])
                c.dma('sp', tab_d[:, :], tabs[:], reads=[tabs])
                c.end_iter()
                hk = [p.sb([128, 512], F32) for _ in range(2)]
                bt = [p.sb([128, 512], F32) for _ in range(2)]
                pb = [p.ps([128, 512], F32) for _ in range(2)]
                n = 0
                for h in range(8):
                    for di in range(6):
                        delta = -128 + 128 * di
                        off = J0 - 127 - delta
                        src = bass.AP(tab_d.tensor, h * NJ + off, [[1, 128], [1, 512]])
                        c.dma('sp', hk[n % 2][:], src, writes=[hk[n % 2]])
                        c.op('pe', lambda e: e.matmul(pb[n % 2].t[:, :], lhsT=antiI, rhs=hk[n % 2].t[:, :], start=True, stop=True),
                             reads=[hk[n % 2], cf], writes=[pb[n % 2]])
                        c.op('act', lambda e: e.activation(out=bt[n % 2].t[:, :], in_=pb[n % 2].t[:, :], func=AF.Copy),
                             reads=[pb[n % 2]], writes=[bt[n % 2]])
                        c.dma('sp', bias_d[h * 6 + di, :, :], bt[n % 2][:], reads=[bt[n % 2]])
                        n += 1
                c.end_iter()

        def norm_transpose(p, xt_list, hts, gcol, scr):
            junk, xn, ss, ptr = scr
            for j in range(4):
                c.op('act', lambda e: e.activation(out=junk.t[:], in_=xt_list.t[:, j, :], func=AF.Square,
                                                   accum_out=ss.t[:, j:j + 1]),
                     reads=[xt_list], writes=[junk, ss])
                c.op('dve', lambda e: e.tensor_scalar(out=ss.t[:, 4 + j:5 + j], in0=ss.t[:, j:j + 1], scalar1=1.0 / D,
                                                      scalar2=EPS, op0=ALU.mult, op1=ALU.add), reads=[ss], writes=[ss])
                c.op('act', lambda e: e.activation(out=ss.t[:, 4 + j:5 + j], in_=ss.t[:, 4 + j:5 + j], func=AF.Sqrt),
                     reads=[ss], writes=[ss])
                c.op('dve', lambda e: e.reciprocal(out=ss.t[:, 8 + j:9 + j], in_=ss.t[:, 4 + j:5 + j]), reads=[ss], writes=[ss])
                c.op('dve', lambda e: e.tensor_scalar(out=xn[j % 2].t[:], in0=xt_list.t[:, j, :], scalar1=ss.t[:, 8 + j:9 + j],
                                                      scalar2=None, op0=ALU.mult),
                     reads=[xt_list, ss], writes=[xn[j % 2]])
                for cc in range(8):
                    c.op('pe', lambda e: e.transpose(out=ptr[j % 2].t[:, cc, :], in_=xn[j % 2].t[:, cc * 128:(cc + 1) * 128],
                                                     identity=ident),
                         reads=[xn[j % 2], cbf], writes=[ptr[j % 2]])
                c.op('act', lambda e: e.activation(out=hts.t[:, :, j * 128:(j + 1) * 128], in_=ptr[j % 2].t[:], func=AF.Copy),
                     reads=[ptr[j % 2]], writes=[hts])
            c.op('pool', lambda e: e.tensor_tensor(out=hts.t[:], in0=hts.t[:], in1=gcol.unsqueeze(2).to_broadcast([128, 8, 512]),
                                                   op=ALU.mult), reads=[hts, vc], writes=[hts])

        def nt_scratch(p):
            return (p.sb([128, 1024], BF16), [p.sb([128, 1024], BF16) for _ in range(2)], p.sb([128, 12], F32),
                    [p.ps([128, 8, 128], BF16) for _ in range(2)])

        if _en(1):
            with Pool_(c, nc) as p:
                xt4 = p.sb([128, 4, 1024], F32)
                hts = p.sb([128, 8, 512], BF16)
                scr = nt_scratch(p)

                for i in loop_iter(nc, NB, static):
                    c.dma('sp', xt4[:], x_d[ds(i * 512, 512), :].rearrange("(j p) d -> p j d", p=128), writes=[xt4])
                    norm_transpose(p, xt4, hts, g1col, scr)
                    c.dma('sp', hT_d[:, :, ds(i * 512, 512)].rearrange("c p t -> p c t"), hts[:], reads=[hts])
                    c.end_iter()

        def load_w(p, src_ap, ncols=1024, nk=8):
            w = p.sb([128, nk, ncols], BF16, "w")
            for kc in range(nk):
                for c0 in range(0, ncols, 1024):
                    c.dma('pool', w.t[:, kc, c0:c0 + 1024], src_ap[kc * 128:(kc + 1) * 128, c0:c0 + 1024], writes=[w])
            return w

        class PsRot:
            def __init__(self, p, n):
                self.b = [p.ps([128, 512], F32) for _ in range(n)]
                self.i = 0

            def next(self):
                b = self.b[self.i % len(self.b)]
                self.i += 1
                return b

        def proj_cm(w, act, psr, evac, ncol_chunks=8, nk=8):
            for j in range(ncol_chunks):
                ps = psr.next()
                for k in range(nk):
                    c.op('pe', lambda e: e.matmul(ps.t[:, :], lhsT=w.t[:, k, j * 128:(j + 1) * 128], rhs=act.t[:, k, :],
                                                  start=(k == 0), stop=(k == nk - 1)),
                         reads=[w, act], writes=[ps])
                evac(j, ps)

        def proj_tm(w, act, psr, evac, ncols=1024, nk=8, ntok=4):
            for s in range(ntok):
                for hf in range(ncols // 512):
                    ps = psr.next()
                    for k in range(nk):
                        c.op('pe', lambda e: e.matmul(ps.t[:, :], lhsT=act.t[:, k, s * 128:(s + 1) * 128],
                                                      rhs=w.t[:, k, hf * 512:(hf + 1) * 512],
                                                      start=(k == 0), stop=(k == nk - 1)),
                             reads=[w, act], writes=[ps])
                    evac(s, hf, ps)

        wf_src = [w_in[:, 4096:5120], w_in[:, 5120:6144]] if job == "s" else [w_fsw[:, 0:1024], w_fsw[:, 1024:2048]]

        if _en(2):
            with Pool_(c, nc) as p:
                W = {"aq": load_w(p, w_in[:, 0:1024]), "ak": load_w(p, w_in[:, 1024:2048]),
                     "av": load_w(p, w_in[:, 2048:3072]), "gq": load_w(p, w_in[:, 3072:4096]),
                     "f0": load_w(p, wf_src[0]), "f1": load_w(p, wf_src[1]), "gi": load_w(p, w_in[:, 6144:7168])}
                lbB = p.sb([128, 4, 1024], F32, "lbB")
                c.dma('sp', lbB[:], vb_d[:, 2048:6144].rearrange("p (a b) -> p a b", a=4), writes=[lbB])
                with Pool_(c, nc) as tp_:
                    dtmp = tp_.sb([128, 2, 1024], F32, "dtmp")
                    for d_ in range(2):
                        c.op('dve', lambda e: e.tensor_tensor(out=dtmp.t[:, d_, :], in0=lbB.t[:, 2 * d_, :], in1=lbB.t[:, 2 * d_ + 1, :],
                                                              op=ALU.subtract), reads=[lbB], writes=[dtmp])
                    for d_ in range(2):
                        c.op('act', lambda e: e.activation(out=lbB.t[:, 2 * d_, :], in_=dtmp.t[:, d_, :], func=AF.Sigmoid),
                             reads=[dtmp], writes=[lbB])
                        c.op('act', lambda e: e.activation(out=lbB.t[:, 2 * d_ + 1, :], in_=dtmp.t[:, d_, :], func=AF.Sigmoid, scale=-1.0),
                             reads=[dtmp], writes=[lbB])
                    c.end_iter()
                hts = p.sb([128, 8, 512], BF16, "hts")
                stg_cm = [p.sb([128, 8, 512], BF16, "stgcm") for _ in range(2)]
                stg_bf = [p.sb([128, 1024], BF16, "stgbf") for _ in range(4)]
                stg_f = [p.sb([128, 1024], F32, "stgf") for _ in range(4)]
                sig = [p.sb([128, 512], F32, "sig") for _ in range(8)]
                psr = PsRot(p, 8)
                cnt = {"cm": 0, "tm": 0, "sg": 0}
                tq = 'sp' if static else 'act'
                for i in loop_iter(nc, NB, static):
                    c.dma('sp', hts[:], hT_d[:, :, ds(i * 512, 512)].rearrange("c p t -> p c t"), writes=[hts])

                    def cm_plain(wname, dst):
                        st = stg_cm[cnt["cm"] % 2]
                        cnt["cm"] += 1
                        eng = ['act', 'dve']

                        def ev(j, ps):
                            if j % 2 == 0:
                                c.op('act', lambda e: e.activation(out=st.t[:, j, :], in_=ps.t[:, :], func=AF.Copy),
                                     reads=[ps], writes=[(st, j)])
                            else:
                                c.op('dve', lambda e: e.tensor_copy(out=st.t[:, j, :], in_=ps.t[:, :]), reads=[ps], writes=[(st, j)])
                        proj_cm(W[wname], hts, psr, ev)
                        c.dma('sp', dst[:, :, ds(i * 512, 512)].rearrange("c p t -> p c t"), st[:], reads=[st])

                    def tm_plain(wname, dst):
                        sts = {}

                        def ev(s, hf, ps):
                            if hf == 0:
                                sts[s] = stg_bf[cnt["tm"] % 4]
                                cnt["tm"] += 1
                            st = sts[s]
                            if hf == 0:
                                c.op('act', lambda e: e.activation(out=st.t[:, 0:512], in_=ps.t[:, :], func=AF.Copy),
                                     reads=[ps], writes=[(st, 0)])
                            else:
                                c.op('dve', lambda e: e.tensor_copy(out=st.t[:, 512:1024], in_=ps.t[:, :]), reads=[ps], writes=[(st, 1)])
                                c.dma(tq, dst[ds(i * 512 + s * 128, 128), :], st[:], reads=[st])
                        proj_tm(W[wname], hts, psr, ev)

                    def f_cm(d_):
                        st = stg_cm[cnt["cm"] % 2]
                        cnt["cm"] += 1

                        def ev(j, ps):
                            sg = sig[cnt["sg"] % 8]
                            cnt["sg"] += 1
                            c.op('act', lambda e: e.activation(out=sg.t[:, :], in_=ps.t[:, :], func=AF.Sigmoid, scale=-1.0),
                                 reads=[ps], writes=[sg])
                            c.op('dve', lambda e: e.tensor_scalar(out=st.t[:, j, :], in0=sg.t[:, :], scalar1=omlb_col[d_][:, j:j + 1],
                                                                  scalar2=None, op0=ALU.mult), reads=[sg, der], writes=[(st, j)])
                        proj_cm(W["f%d" % d_], hts, psr, ev)
                        c.dma('sp', kT_d[d_][:, :, ds(i * 512, 512)].rearrange("c p t -> p c t"), st[:], reads=[st])

                    def f_tm(d_):
                        sts = {}

                        def ev(s, hf, ps):
                            if hf == 0:
                                sts[s] = (stg_bf[cnt["tm"] % 4], stg_f[cnt["tm"] % 4])
                                cnt["tm"] += 1
                            sb_, sf_ = sts[s]
                            sg = sig[cnt["sg"] % 8]
                            cnt["sg"] += 1
                            cs = slice(hf * 512, (hf + 1) * 512)
                            c.op('act', lambda e: e.activation(out=sg.t[:, :], in_=ps.t[:, :], func=AF.Sigmoid), reads=[ps], writes=[sg])
                            c.op('dve', lambda e: e.tensor_tensor(out=sg.t[:, :], in0=sg.t[:, :], in1=lbB.t[:, 2 * d_ + 1, cs], op=ALU.mult),
                                 reads=[sg, lbB], writes=[sg])
                            c.op('dve', lambda e: e.tensor_tensor(out=sg.t[:, :], in0=sg.t[:, :], in1=lbB.t[:, 2 * d_, cs], op=ALU.add),
                                 reads=[sg, lbB], writes=[sg])
                            def post(sg=sg, sf_=sf_, sb_=sb_, cs=cs, hf=hf, s=s):
                                c.op('act', lambda e: e.activation(out=sf_.t[:, cs], in_=sg.t[:, :], func=AF.Ln), reads=[sg], writes=[(sf_, hf)])
                                c.op('pool', lambda e: e.tensor_scalar(out=sb_.t[:, cs], in0=sg.t[:, :], scalar1=-1.0, scalar2=1.0,
                                                                       op0=ALU.mult, op1=ALU.add), reads=[sg], writes=[(sb_, hf)])
                                if hf == 1:
                                    c.dma(tq, g_d[d_][ds(i * 512 + s * 128, 128), :], sf_[:], reads=[sf_])
                                    c.dma(tq, ktm_d[d_][ds(i * 512 + s * 128, 128), :], sb_[:], reads=[sb_])
                            posts.append(post)
                        posts = []
                        proj_tm(W["f%d" % d_], hts, psr, ev)
                        for po in posts:
                            po()

                    cm_plain("aq", aqT_d)
                    cm_plain("ak", akT_d)
                    tm_plain("av", av_d)
                    cm_plain("gq", gqT_d)
                    tm_plain("gi", gi_d)
                    f_cm(0)
                    f_cm(1)
                    f_tm(0)
                    f_tm(1)
                    c.end_iter()

        if _en(3):
            with Pool_(c, nc) as p:
                W = {"og": load_w(p, w_in[:, 7168:8192]), "ga": load_w(p, w_in[:, 8192:9216]), "gb": load_w(p, w_in[:, 9216:10240])}
                hts = p.sb([128, 8, 512], BF16, "hts")
                stg_cm = [p.sb([128, 8, 512], BF16, "stgcm") for _ in range(2)]
                psr = PsRot(p, 8)
                c.end_iter()
                for i in loop_iter(nc, NBQ, static):
                    c.dma('sp', hts[:], hT_d[:, :, ds(i * 512, 512)].rearrange("c p t -> p c t"), writes=[hts])
                    for n_, (wname, dst, fn) in enumerate((("ga", gaT_d, AF.Sigmoid), ("gb", gbT_d, AF.Sigmoid), ("og", gogT_d, AF.Silu))):
                        st = stg_cm[n_ % 2]

                        def ev(j, ps):
                            c.op('act', lambda e: e.activation(out=st.t[:, j, :], in_=ps.t[:, :], func=fn), reads=[ps], writes=[(st, j)])
                        proj_cm(W[wname], hts, psr, ev)
                        c.dma('sp', dst[:, :, ds(i * 512, 512)].rearrange("c p t -> p c t"), st[:], reads=[st])
                    c.end_iter()

        NKB = T // 128
        if _en(4):
            with Pool_(c, nc) as p:
                kT = p.sb([128, 1, T], BF16, "kT")
                vv = p.sb([128, NKB, 128], BF16, "vv")
                biasT = p.sb([128, 6, 512], F32, "biasT")
                cfar = p.sb([128, 1, 2], F32, "cfar")
                qT = p.sb([128, 1, Tq], BF16, "qT")
                pS = [[p.ps([128, 512], F32, "pS") for _ in range(2)] for _ in range(2)]
                pO = [p.ps([128, 512], F32, "pO") for _ in range(2)]
                pZ = [p.ps([128, 512], F32, "pZ") for _ in range(2)]
                pT = [[p.sb([128, 512], BF16, "pT") for _ in range(4)] for _ in range(2)]
                tmpS = [p.sb([128, 512], F32, "tmpS") for _ in range(2)]
                zacc = [[p.sb([128, 512], F32, "zacc") for _ in range(2)] for _ in range(2)]
                rz = [p.sb([128, 512], F32, "rz") for _ in range(2)]
                o01 = [p.sb([128, 512], F32, "o01") for _ in range(2)]
                osq = p.sb([128, 512], BF16, "osq")
                rs = p.sb([128, 512], F32, "rs")
                oout = p.sb([128, 1, Tq], BF16, "oout")
                for h in loop_iter(nc, 8, static):
                    c.dma('sp', kT[:], akT_d[ds(h, 1), :, 0:T].rearrange("a p t -> p a t"), writes=[kT])
                    c.dma('sp', vv[:], av_d[0:T, ds(h * 128, 128)].rearrange("(kb p) d -> p kb d", p=128), writes=[vv])
                    c.dma('sp', biasT[:], bias_d[ds(h * 6, 6), :, :].rearrange("d p q -> p d q"), writes=[biasT])
                    c.dma('sp', cfar[:], cfar_d[ds(h, 1), :, :].rearrange("a p s -> p a s"), writes=[cfar])
                    c.dma('sp', qT[:], aqT_d[ds(h, 1), :, 0:Tq].rearrange("a p t -> p a t"), writes=[qT])
                    for qb in range(Tq // 512):
                        q_ = qT
                        qsl = slice(qb * 512, (qb + 1) * 512)
                        def qk_exp(kb):
                            delta = kb * 128 - qb * 512
                            near = -128 <= delta <= 512
                            pts = []
                            pss = []
                            for m in range(2):
                                ps = pS[m][kb % 2]
                                pr = slice(m * 64, (m + 1) * 64)
                                c.op('pe', lambda e: e.matmul(ps.t[:, :], lhsT=kT.t[pr, 0, kb * 128:(kb + 1) * 128], rhs=q_.t[pr, 0, qsl],
                                                              start=True, stop=True), reads=[kT, q_], writes=[ps])
                                pss.append(ps)
                            for m in range(2):
                                ps = pss[m]
                                pt_ = pT[m][kb % 4]
                                if near:
                                    di = (delta + 128) // 128
                                    tm_ = tmpS[m]
                                    c.op('dve', lambda e: e.scalar_tensor_tensor(out=tm_.t[:, :], in0=ps.t[:, :], scalar=0.125,
                                                                                 in1=biasT.t[:, di, :], op0=ALU.mult, op1=ALU.add),
                                         reads=[ps, biasT], writes=[tm_])
                                    c.op('act', lambda e: e.activation(out=pt_.t[:, :], in_=tm_.t[:, :], func=AF.Exp),
                                         reads=[tm_], writes=[pt_])
                                else:
                                    side = 0 if delta > 0 else 1
                                    c.op('act', lambda e: e.activation(out=pt_.t[:, :], in_=ps.t[:, :], func=AF.Exp,
                                                                       bias=cfar.t[:, 0, side:side + 1], scale=0.125),
                                         reads=[ps, cfar], writes=[pt_])
                                pts.append(pt_)
                            return pts

                        def pv_z(kb, pts):
                            for m in range(2):
                                pt_ = pts[m]
                                c.op('pe', lambda e: e.matmul(pO[m].t[:, :], lhsT=vv.t[:, kb, :], rhs=pt_.t[:, :],
                                                              start=(kb == 0), stop=(kb == NKB - 1)), reads=[vv, pt_], writes=[pO[m]])
                                if kb % 2 == 0:
                                    c.op('pe', lambda e: e.matmul(pZ[m].t[:, :], lhsT=ones_bf, rhs=pt_.t[:, :],
                                                                  start=(kb == 0), stop=False), reads=[cbf, pt_], writes=[pZ[m]])
                                else:
                                    zeng = 'dve' if m == 0 else 'pool'
                                    za = zacc[m][0]
                                    if kb == 1:
                                        c.op(zeng, lambda e: e.tensor_copy(out=za.t[:, :], in_=pt_.t[:, :]), reads=[pt_], writes=[za])
                                    else:
                                        c.op(zeng, lambda e: e.tensor_tensor(out=za.t[:, :], in0=za.t[:, :], in1=pt_.t[:, :], op=ALU.add),
                                             reads=[za, pt_], writes=[za])

                        pend = qk_exp(0)
                        for kb in range(NKB):
                            nxt = qk_exp(kb + 1) if kb + 1 < NKB else None
                            pv_z(kb, pend)
                            pend = nxt
                        for m in range(2):
                            c.op('pe', lambda e: e.matmul(pZ[m].t[:, :], lhsT=ones_f, rhs=zacc[m][0].t[:, :], start=False, stop=True),
                                 reads=[cf, zacc[m][0]], writes=[pZ[m]])
                        for m in range(2):
                            c.op('dve', lambda e: e.reciprocal(out=rz[m].t[:, :], in_=pZ[m].t[:, :]), reads=[pZ[m]], writes=[rz[m]])
                            c.op('dve', lambda e: e.tensor_tensor(out=o01[m].t[:, :], in0=pO[m].t[:, :], in1=rz[m].t[:, :], op=ALU.mult),
                                 reads=[pO[m], rz[m]], writes=[o01[m]])
                        c.op('dve', lambda e: e.scalar_tensor_tensor(out=o01[0].t[:, :], in0=o01[1].t[:, :], scalar=neglam,
                                                                     in1=o01[0].t[:, :], op0=ALU.mult, op1=ALU.add),
                             reads=[o01[0], o01[1], der], writes=[o01[0]])
                        c.op('pool', lambda e: e.tensor_tensor(out=osq.t[:, :], in0=o01[0].t[:, :], in1=o01[0].t[:, :], op=ALU.mult),
                             reads=[o01[0]], writes=[osq])
                        c.op('pe', lambda e: e.matmul(pZ[0].t[:, :], lhsT=ones_bf, rhs=osq.t[:, :], start=True, stop=True),
                             reads=[cbf, osq], writes=[pZ[0]])
                        c.op('dve', lambda e: e.tensor_scalar(out=rs.t[:, :], in0=pZ[0].t[:, :], scalar1=1.0 / 128, scalar2=EPS,
                                                              op0=ALU.mult, op1=ALU.add), reads=[pZ[0]], writes=[rs])
                        c.op('act', lambda e: e.activation(out=rs.t[:, :], in_=rs.t[:, :], func=AF.Ln), reads=[rs], writes=[rs])
                        c.op('act', lambda e: e.activation(out=rs.t[:, :], in_=rs.t[:, :], func=AF.Exp, scale=-0.5), reads=[rs], writes=[rs])
                        oo = oout
                        c.op('dve', lambda e: e.scalar_tensor_tensor(out=oo.t[:, 0, qsl], in0=o01[0].t[:, :], scalar=gsubcol,
                                                                     in1=rs.t[:, :], op0=ALU.mult, op1=ALU.mult),
                             reads=[o01[0], rs, der], writes=[oo])
                    c.dma('sp', oaT_d[ds(h, 1), :, 0:Tq].rearrange("a p t -> p a t"), oout[:], reads=[oout])
                    c.end_iter()

        NG = T // 256
        if _en(5):
            with Pool_(c, nc) as p:
                S = [p.sb([128, 8, 128], F32, "S") for _ in range(2)]
                Sbf = [p.sb([128, 8, 128], BF16, "Sbf") for _ in range(2)]
                qTt = [p.sb([128, 8, 256], BF16, "qTt") for _ in range(2)]
                kTt = [p.sb([128, 8, 256], BF16, "kTt") for _ in range(2)]
                ktm = [p.sb([64, 4, 1024], BF16, "ktm") for _ in range(2)]
                gtm = [p.sb([64, 4, 1024], F32, "gtm") for _ in range(2)]
                vtm = [p.sb([64, 4, 1024], BF16, "vtm") for _ in range(2)]
                ostg = [p.sb([128, 8, 256], F32, "ostg") for _ in range(2)]
                ET = p.sb([128, 8, 64], F32, "ET")
                EiT = p.sb([128, 8, 64], F32, "EiT")
                Eitm = p.sb([64, 1024], F32, "Eitm")
                qt_ = p.sb([128, 8, 64], BF16, "qt_")
                kt_ = p.sb([128, 8, 64], BF16, "kt_")
                ktm_ = p.sb([64, 1024], BF16, "ktm_")
                ATm = p.sb([64, 8, 64], BF16, "ATm")
                Stmp = p.sb([128, 8, 128], F32, "Stmp")
                p_bT = p.ps([128, 8, 64], F32, "p_bT")
                p_btm = p.ps([64, 1024], F32, "p_btm")
                p_AT = p.ps([64, 8, 64], F32, "p_AT")
                p_oT = p.ps([128, 8, 64], F32, "p_oT")
                p_dS = p.ps([128, 8, 128], F32, "p_dS")
                for d_ in range(2):
                    c.op('pool', lambda e: e.memset(S[d_].t[:], 0.0), writes=[S[d_]])
                    c.op('pool', lambda e: e.memset(Sbf[d_].t[:], 0.0), writes=[Sbf[d_]])
                c.end_iter()
                dq = 'sp' if static else 'act'
                for i in loop_iter(nc, NG, static):
                    for d_ in range(2):
                        t0 = i * 256 if d_ == 0 else (NG - 1 - i) * 256
                        c.dma(dq, qTt[d_][:], gqT_d[:, :, ds(t0, 256)].rearrange("c p t -> p c t"), writes=[qTt[d_]])
                        c.dma(dq, kTt[d_][:], kT_d[d_][:, :, ds(t0, 256)].rearrange("c p t -> p c t"), writes=[kTt[d_]])
                        c.dma(dq, ktm[d_][:], ktm_d[d_][ds(t0, 256), :].rearrange("(a s) d -> s a d", s=64), writes=[ktm[d_]])
                        c.dma(dq, gtm[d_][:], g_d[d_][ds(t0, 256), :].rearrange("(a s) d -> s a d", s=64), writes=[gtm[d_]])
                        c.dma(dq, vtm[d_][:], gi_d[ds(t0, 256), :].rearrange("(a s) d -> s a d", s=64), writes=[vtm[d_]])
                    for cc in range(4):
                        for d_ in range(2):
                            ch = cc if d_ == 0 else 3 - cc
                            Um = U_f if d_ == 0 else Ut_f
                            last = 63 if d_ == 0 else 0
                            tsl = slice(ch * 64, (ch + 1) * 64)
                            for hh in range(8):
                                c.op('pe', lambda e: e.matmul(p_bT.t[:, hh, :], lhsT=gtm[d_].t[:, ch, hh * 128:(hh + 1) * 128], rhs=Um,
                                                              start=True, stop=True), reads=[gtm[d_], cf], writes=[p_bT])
                            for hf in range(2):
                                c.op('pe', lambda e: e.matmul(p_btm.t[:, hf * 512:(hf + 1) * 512], lhsT=Um,
                                                              rhs=gtm[d_].t[:, ch, hf * 512:(hf + 1) * 512], start=True, stop=True),
                                     reads=[gtm[d_], cf], writes=[p_btm])
                            c.op('act', lambda e: e.activation(out=ET.t[:], in_=p_bT.t[:], func=AF.Exp), reads=[p_bT], writes=[ET])
                            c.op('act', lambda e: e.activation(out=EiT.t[:], in_=p_bT.t[:], func=AF.Exp, scale=-1.0),
                                 reads=[p_bT], writes=[EiT])
                            c.op('act', lambda e: e.activation(out=Eitm.t[:], in_=p_btm.t[:], func=AF.Exp, scale=-1.0),
                                 reads=[p_btm], writes=[Eitm])
                            c.op('dve', lambda e: e.tensor_tensor(out=qt_.t[:], in0=qTt[d_].t[:, :, tsl], in1=ET.t[:], op=ALU.mult),
                                 reads=[qTt[d_], ET], writes=[qt_])
                            c.op('pool', lambda e: e.tensor_tensor(out=kt_.t[:], in0=kTt[d_].t[:, :, tsl], in1=EiT.t[:], op=ALU.mult),
                                 reads=[kTt[d_], EiT], writes=[kt_])
                            c.op('pool', lambda e: e.tensor_tensor(out=ktm_.t[:], in0=ktm[d_].t[:, ch, :], in1=Eitm.t[:], op=ALU.mult),
                                 reads=[ktm[d_], Eitm], writes=[ktm_])
                            for hh in range(8):
                                c.op('pe', lambda e: e.matmul(p_AT.t[:, hh, :], lhsT=kt_.t[:, hh, :], rhs=qt_.t[:, hh, :],
                                                              start=True, stop=True), reads=[kt_, qt_], writes=[p_AT])
                            c.op('dve', lambda e: e.tensor_tensor(out=ATm.t[:], in0=p_AT.t[:],
                                                                  in1=Um.unsqueeze(1).to_broadcast([64, 8, 64]), op=ALU.mult),
                                 reads=[p_AT, cf], writes=[ATm])
                            for hh in range(8):
                                c.op('pe', lambda e: e.matmul(p_oT.t[:, hh, :], lhsT=Sbf[d_].t[:, hh, :], rhs=qt_.t[:, hh, :],
                                                              start=True, stop=False), reads=[Sbf[d_], qt_], writes=[p_oT])
                                c.op('pe', lambda e: e.matmul(p_oT.t[:, hh, :], lhsT=vtm[d_].t[:, ch, hh * 128:(hh + 1) * 128],
                                                              rhs=ATm.t[:, hh, :], start=False, stop=True),
                                     reads=[vtm[d_], ATm], writes=[p_oT])
                            c.op('act', lambda e: e.activation(out=ostg[d_].t[:, :, tsl], in_=p_oT.t[:], func=AF.Copy),
                                 reads=[p_oT], writes=[ostg[d_]])
                            for hh in range(8):
                                c.op('pe', lambda e: e.matmul(p_dS.t[:, hh, :], lhsT=ktm_.t[:, hh * 128:(hh + 1) * 128],
                                                              rhs=vtm[d_].t[:, ch, hh * 128:(hh + 1) * 128], start=True, stop=True),
                                     reads=[ktm_, vtm[d_]], writes=[p_dS])
                            c.op('dve', lambda e: e.tensor_tensor(out=Stmp.t[:], in0=S[d_].t[:], in1=p_dS.t[:], op=ALU.add),
                                 reads=[S[d_], p_dS], writes=[Stmp])
                            c.op('dve', lambda e: e.tensor_tensor(out=S[d_].t[:], in0=Stmp.t[:],
                                                                  in1=ET.t[:, :, last:last + 1].to_broadcast([128, 8, 128]), op=ALU.mult),
                                 reads=[Stmp, ET], writes=[S[d_]])
                            c.op('act', lambda e: e.activation(out=Sbf[d_].t[:], in_=S[d_].t[:], func=AF.Copy), reads=[S[d_]], writes=[Sbf[d_]])
                    for d_ in range(2):
                        t0 = i * 256 if d_ == 0 else (NG - 1 - i) * 256
                        c.dma(dq, oT_d[d_][:, :, ds(t0, 256)].rearrange("c p t -> p c t"), ostg[d_][:], reads=[ostg[d_]])
                    c.end_iter()

        if _en(6):
            with Pool_(c, nc) as p:
                Wa = load_w(p, G["w_pa"][:, :])
                Wb = load_w(p, G["w_pb"][:, :])
                Wo = load_w(p, G["w_o"][:, :])
                gpost = p.sb([128, 1024], F32, "gpost")
                c.dma('sp', gpost[:], vb_d[:, 0:1024], writes=[gpost])
                c.end_iter()
                oa = p.sb([128, 8, 512], BF16, "oa")
                of_ = p.sb([128, 8, 512], F32, "of")
                ob_ = p.sb([128, 8, 512], F32, "ob")
                og = p.sb([128, 8, 512], BF16, "og")
                ga = p.sb([128, 8, 512], BF16, "ga")
                gb = p.sb([128, 8, 512], BF16, "gb")
                obn = p.sb([128, 8, 512], BF16, "obn")
                mT = p.sb([128, 8, 512], BF16, "mT")
                sq = [p.sb([128, 512], BF16, "sq") for _ in range(2)]
                rs = [p.sb([128, 512], F32, "rs") for _ in range(2)]
                t1 = [p.sb([128, 512], F32, "t1") for _ in range(2)]
                t2 = [p.sb([128, 512], F32, "t2") for _ in range(2)]
                xt4 = p.sb([128, 4, 1024], F32, "xt4")
                yt4 = p.sb([128, 4, 1024], F32, "yt4")
                junk = p.sb([128, 512], BF16, "junk")
                ss = p.sb([128, 16], F32, "ss")
                psr = PsRot(p, 8)
                for i in loop_iter(nc, NBQ, static):
                    tsl = ds(i * 512, 512)
                    c.dma('sp', oa[:], oaT_d[:, :, tsl].rearrange("c p t -> p c t"), writes=[oa])
                    c.dma('sp', of_[:], oT_d[0][:, :, tsl].rearrange("c p t -> p c t"), writes=[of_])
                    c.dma('sp', ob_[:], oT_d[1][:, :, tsl].rearrange("c p t -> p c t"), writes=[ob_])
                    c.dma('sp', og[:], gogT_d[:, :, tsl].rearrange("c p t -> p c t"), writes=[og])
                    c.dma('sp', ga[:], gaT_d[:, :, tsl].rearrange("c p t -> p c t"), writes=[ga])
                    c.dma('sp', gb[:], gbT_d[:, :, tsl].rearrange("c p t -> p c t"), writes=[gb])
                    c.dma('sp', xt4[:], x_d[ds(i * 512, 512), :].rearrange("(j p) d -> p j d", p=128), writes=[xt4])
                    c.op('pool', lambda e: e.tensor_tensor(out=of_.t[:], in0=of_.t[:], in1=ob_.t[:], op=ALU.add),
                         reads=[of_, ob_], writes=[of_])
                    for hh in range(8):
                        s_ = sq[hh % 2]
                        r_ = rs[hh % 2]
                        c.op('act', lambda e: e.activation(out=s_.t[:, :], in_=of_.t[:, hh, :], func=AF.Square), reads=[of_], writes=[s_])
                        ps = psr.next()
                        c.op('pe', lambda e: e.matmul(ps.t[:, :], lhsT=ones_bf, rhs=s_.t[:, :], start=True, stop=True),
                             reads=[cbf, s_], writes=[ps])
                        c.op('dve', lambda e: e.tensor_scalar(out=r_.t[:, :], in0=ps.t[:, :], scalar1=1.0 / 128, scalar2=EPS,
                                                              op0=ALU.mult, op1=ALU.add), reads=[ps], writes=[r_])
                        c.op('act', lambda e: e.activation(out=r_.t[:, :], in_=r_.t[:, :], func=AF.Ln), reads=[r_], writes=[r_])
                        c.op('act', lambda e: e.activation(out=r_.t[:, :], in_=r_.t[:, :], func=AF.Exp, scale=-0.5), reads=[r_], writes=[r_])
                        c.op('dve', lambda e: e.scalar_tensor_tensor(out=r_.t[:, :], in0=of_.t[:, hh, :], scalar=ghcol, in1=r_.t[:, :],
                                                                     op0=ALU.mult, op1=ALU.mult), reads=[of_, r_, vc], writes=[r_])
                        c.op('pool', lambda e: e.tensor_tensor(out=obn.t[:, hh, :], in0=r_.t[:, :], in1=og.t[:, hh, :], op=ALU.mult),
                             reads=[r_, og], writes=[(obn, hh)])
                    for j in range(8):
                        psa = psr.next()
                        for k in range(8):
                            c.op('pe', lambda e: e.matmul(psa.t[:, :], lhsT=Wa.t[:, k, j * 128:(j + 1) * 128], rhs=oa.t[:, k, :],
                                                          start=(k == 0), stop=(k == 7)), reads=[Wa, oa], writes=[psa])
                        psb = psr.next()
                        for k in range(8):
                            c.op('pe', lambda e: e.matmul(psb.t[:, :], lhsT=Wb.t[:, k, j * 128:(j + 1) * 128], rhs=obn.t[:, k, :],
                                                          start=(k == 0), stop=(k == 7)), reads=[Wb, obn], writes=[psb])
                        a_, b_ = t1[j % 2], t2[j % 2]
                        c.op('dve', lambda e: e.tensor_tensor(out=a_.t[:, :], in0=psa.t[:, :], in1=ga.t[:, j, :], op=ALU.mult),
                             reads=[psa, ga], writes=[a_])
                        c.op('dve', lambda e: e.tensor_tensor(out=b_.t[:, :], in0=psb.t[:, :], in1=gb.t[:, j, :], op=ALU.mult),
                             reads=[psb, gb], writes=[b_])
                        c.op('pool', lambda e: e.tensor_tensor(out=mT.t[:, j, :], in0=a_.t[:, :], in1=b_.t[:, :], op=ALU.add),
                             reads=[a_, b_], writes=[(mT, j)])
                    for s in range(4):
                        pss = []
                        for hf in range(2):
                            ps = psr.next()
                            pss.append(ps)
                            for k in range(8):
                                c.op('pe', lambda e: e.matmul(ps.t[:, :], lhsT=mT.t[:, k, s * 128:(s + 1) * 128],
                                                              rhs=Wo.t[:, k, hf * 512:(hf + 1) * 512], start=(k == 0), stop=(k == 7)),
                                     reads=[Wo, mT], writes=[ps])
                            c.op('act', lambda e: e.activation(out=junk.t[:, :], in_=ps.t[:, :], func=AF.Square,
                                                               accum_out=ss.t[:, 4 * s + hf:4 * s + hf + 1]), reads=[ps], writes=[junk, (ss, s)])
                        res_tail(c, ss, pss, gpost, xt4, xt4.t[:, s, :], (yt4, s), yt4.t[:, s, :], s)
                    c.dma('sp', x1_d[ds(i * 512, 512), :].rearrange("(j p) d -> p j d", p=128), yt4[:], reads=[yt4])
                    c.end_iter()

        if _en(7):
            with Pool_(c, nc) as p:
                Wu = load_w(p, G["w_up"][:, :], ncols=4096, nk=8)
                Wd = load_w(p, G["w_dn"][:, :], ncols=1024, nk=32)
                gpost = p.sb([128, 1024], F32, "gpost2")
                c.dma('sp', gpost[:], vb_d[:, 1024:2048], writes=[gpost])
                c.end_iter()
                xt2 = p.sb([128, 2, 1024], F32, "xt2")
                yt2 = p.sb([128, 2, 1024], F32, "yt2")
                h2 = p.sb([128, 8, 256], BF16, "h2")
                uT = p.sb([128, 32, 256], BF16, "uT")
                rl = [p.sb([128, 256], BF16, "rl") for _ in range(2)]
                junkb = p.sb([128, 1024], BF16, "junkb")
                xn = [p.sb([128, 1024], BF16, "xn") for _ in range(2)]
                ss = p.sb([128, 16], F32, "ss")
                ptr = [p.ps([128, 8, 128], BF16, "ptr") for _ in range(2)]
                psr = PsRot(p, 6)
                for i in loop_iter(nc, Tq // 256, static):
                    c.dma('sp', xt2[:], x1_d[ds(i * 256, 256), :].rearrange("(j p) d -> p j d", p=128), writes=[xt2])
                    for j in range(2):
                        x_ = xt2
                        c.op('act', lambda e: e.activation(out=junkb.t[:], in_=x_.t[:, j, :], func=AF.Square, accum_out=ss.t[:, 8 + j:9 + j]),
                             reads=[x_], writes=[junkb, ss])
                        c.op('dve', lambda e: e.tensor_scalar(out=ss.t[:, 10 + j:11 + j], in0=ss.t[:, 8 + j:9 + j], scalar1=1.0 / D,
                                                              scalar2=EPS, op0=ALU.mult, op1=ALU.add), reads=[ss], writes=[ss])
                        c.op('act', lambda e: e.activation(out=ss.t[:, 10 + j:11 + j], in_=ss.t[:, 10 + j:11 + j], func=AF.Sqrt),
                             reads=[ss], writes=[ss])
                        c.op('dve', lambda e: e.reciprocal(out=ss.t[:, 12 + j:13 + j], in_=ss.t[:, 10 + j:11 + j]), reads=[ss], writes=[ss])
                        c.op('dve', lambda e: e.tensor_scalar(out=xn[j].t[:], in0=x_.t[:, j, :], scalar1=ss.t[:, 12 + j:13 + j],
                                                              scalar2=None, op0=ALU.mult), reads=[x_, ss], writes=[xn[j]])
                        for cc in range(8):
                            c.op('pe', lambda e: e.transpose(out=ptr[j].t[:, cc, :], in_=xn[j].t[:, cc * 128:(cc + 1) * 128],
                                                             identity=ident), reads=[xn[j], cbf], writes=[ptr[j]])
                        c.op('act', lambda e: e.activation(out=h2.t[:, :, j * 128:(j + 1) * 128], in_=ptr[j].t[:], func=AF.Copy),
                             reads=[ptr[j]], writes=[h2])
                    c.op('pool', lambda e: e.tensor_tensor(out=h2.t[:], in0=h2.t[:], in1=g2col.unsqueeze(2).to_broadcast([128, 8, 256]),
                                                           op=ALU.mult), reads=[h2, vc], writes=[h2])
                    for f in range(32):
                        ps = psr.next()
                        for k in range(8):
                            c.op('pe', lambda e: e.matmul(ps.t[:, 0:256], lhsT=Wu.t[:, k, f * 128:(f + 1) * 128], rhs=h2.t[:, k, :],
                                                          start=(k == 0), stop=(k == 7)), reads=[Wu, h2], writes=[ps])
                        r_ = rl[f % 2]
                        c.op('act', lambda e: e.activation(out=r_.t[:, :], in_=ps.t[:, 0:256], func=AF.Relu), reads=[ps], writes=[r_])
                        c.op('pool' if f % 2 else 'dve', lambda e: e.tensor_tensor(out=uT.t[:, f, :], in0=r_.t[:, :], in1=r_.t[:, :], op=ALU.mult),
                             reads=[r_], writes=[(uT, f)])
                    for s in range(2):
                        pss = []
                        for hf in range(2):
                            ps = psr.next()
                            pss.append(ps)
                            for k in range(32):
                                c.op('pe', lambda e: e.matmul(ps.t[:, :], lhsT=uT.t[:, k, s * 128:(s + 1) * 128],
                                                              rhs=Wd.t[:, k, hf * 512:(hf + 1) * 512], start=(k == 0), stop=(k == 31)),
                                     reads=[Wd, uT], writes=[ps])
                            c.op('act', lambda e: e.activation(out=junkb.t[:, 0:512], in_=ps.t[:, :], func=AF.Square,
                                                               accum_out=ss.t[:, 4 * s + hf:4 * s + hf + 1]), reads=[ps], writes=[junkb, (ss, s)])
                        res_tail(c, ss, pss, gpost, xt2, xt2.t[:, s, :], (yt2, s), yt2.t[:, s, :], s)
                    c.dma('sp', y_d[ds(i * 256, 256), :].rearrange("(j p) d -> p j d", p=128), yt2[:], reads=[yt2])
                    c.end_iter()


def res_tail(c, ss, pss, gpost, xb, xv, yb, yv, s):
    k = (ss, s)
    c0 = 4 * s
    c.op('dve', lambda e: e.tensor_tensor(out=ss.t[:, c0 + 2:c0 + 3], in0=ss.t[:, c0:c0 + 1], in1=ss.t[:, c0 + 1:c0 + 2], op=ALU.add),
         reads=[k], writes=[k])
    c.op('dve', lambda e: e.tensor_scalar(out=ss.t[:, c0 + 2:c0 + 3], in0=ss.t[:, c0 + 2:c0 + 3], scalar1=1.0 / D, scalar2=EPS,
                                          op0=ALU.mult, op1=ALU.add), reads=[k], writes=[k])
    c.op('act', lambda e: e.activation(out=ss.t[:, c0 + 2:c0 + 3], in_=ss.t[:, c0 + 2:c0 + 3], func=AF.Ln), reads=[k], writes=[k])
    c.op('act', lambda e: e.activation(out=ss.t[:, c0 + 3:c0 + 4], in_=ss.t[:, c0 + 2:c0 + 3], func=AF.Exp, scale=-0.5), reads=[k], writes=[k])
    for hf in range(2):
        cs = slice(hf * 512, (hf + 1) * 512)
        c.op('dve', lambda e: e.scalar_tensor_tensor(out=yv[:, cs], in0=pss[hf].t[:, :], scalar=ss.t[:, c0 + 3:c0 + 4], in1=gpost.t[:, cs],
                                                     op0=ALU.mult, op1=ALU.mult), reads=[pss[hf], k, gpost], writes=[yb])
    c.op('pool', lambda e: e.tensor_tensor(out=yv, in0=yv, in1=xv, op=ALU.add), reads=[yb, xb], writes=[yb])


def _t5_bucket_np(rel):
    nb = 16
    ret = (rel > 0).astype(np.int64) * nb
    n = np.abs(rel)
    max_exact = 8
    is_small = n < max_exact
    nf = np.maximum(n, 1).astype(np.float32)
    large = max_exact + (np.log(nf / np.float32(max_exact)) / np.float32(math.log(128 / max_exact))
                         * np.float32(nb - max_exact)).astype(np.int64)
    large = np.minimum(large, nb - 1)
    return ret + np.where(is_small, n, large)


def _consts():
    cbf = np.zeros((128, 256), np.float32)
    cbf[:, 0:128] = np.eye(128)
    cbf[:, 128:256] = 1.0
    cf = np.zeros((128, 384), np.float32)
    cf[:, 256:384] = 1.0
    cf[:, 0:128] = np.eye(128)[::-1]
    U = np.triu(np.ones((64, 64), np.float32))
    cf[0:64, 128:192] = U
    cf[0:64, 192:256] = U.T
    return cbf.astype(ml_dtypes.bfloat16), cf


def _oht(flip):
    d = J0 - np.arange(NJ)
    if flip:
        d = -d
    b = _t5_bucket_np(d)
    oh = np.zeros((32, NJ), np.float32)
    oh[b, np.arange(NJ)] = 1.0
    return oh


def _cols(v):
    return np.ascontiguousarray(v.reshape(8, 128).T)


def _vec_inputs(I, swap):
    lbf, lbb = I["lb_fwd"], I["lb_bwd"]
    if swap:
        lbf, lbb = lbb, lbf
    vc = np.zeros((128, 50), np.float32)
    vc[:, 0:8] = _cols(I["g_mix_pre"][0])
    vc[:, 8:16] = _cols(I["g_mlp_pre"][0])
    vc[:, 16:24] = _cols(lbf[0])
    vc[:, 24:32] = _cols(lbf[1])
    vc[:, 32:40] = _cols(lbb[0])
    vc[:, 40:48] = _cols(lbb[1])
    vc[:, 48] = I["g_attn_sub"][0]
    vc[:, 49] = I["g_hgrn_out"][0]
    row = np.concatenate([I["g_mix_post"][0], I["g_mlp_post"][0], lbf[0], lbf[1], lbb[0], lbb[1],
                          I["lam_q1"][0], I["lam_q2"][0], I["lam_k1"][0], I["lam_k2"][0]]).astype(np.float32)
    vb = np.ascontiguousarray(np.broadcast_to(row[None, :], (128, 6400)))
    return vc, vb


def make_in_maps(I, n_cores=8, TS=None, TP=None):
    I = {k: np.asarray(v) for k, v in I.items()}
    xs_all, xp_all = I["x_sample"], I["x_prompt"]
    w_in = np.ascontiguousarray(I["w_in"][0])
    cbf, cf = _consts()
    relb = np.ascontiguousarray(I["rel_bias"], dtype=np.float32)
    wf = [np.ascontiguousarray(w_in[:, 4096:6144]),
          np.ascontiguousarray(np.concatenate([w_in[:, 5120:6144], w_in[:, 4096:5120]], axis=1))]
    vcs, vbs = _vec_inputs(I, False)
    vcp = [vcs, _vec_inputs(I, True)[0]]
    vbp = [vbs, _vec_inputs(I, True)[1]]
    oht = [_oht(False), _oht(True)]

    def cfar(flip):
        a, b = (31, 15) if not flip else (15, 31)
        out = np.zeros((8, 128, 2), np.float32)
        out[:, :, 0] = relb[a][:, None]
        out[:, :, 1] = relb[b][:, None]
        return out
    cfars = [cfar(False), cfar(True)]
    shared = {"w_in": w_in, "w_pa": np.ascontiguousarray(I["w_proj_a"][0]), "w_pb": np.ascontiguousarray(I["w_proj_b"][0]),
              "w_o": np.ascontiguousarray(I["w_out"][0]), "w_up": np.ascontiguousarray(I["w_mlp_up"][0]),
              "w_dn": np.ascontiguousarray(I["w_mlp_down"][0]), "relb": relb, "cbf": cbf, "cf": cf,
              "vc_s": vcs, "vb_s": vbs, "oht_s": oht[0], "cfar_s": cfars[0]}
    maps = []
    for cid in range(n_cores):
        par = cid % 2
        xp = xp_all[(cid // 2) % xp_all.shape[0]]
        if par:
            xp = xp[::-1]
        m = dict(shared)
        m.update({"xs": np.ascontiguousarray(xs_all[cid % xs_all.shape[0]]), "xp": np.ascontiguousarray(xp),
                  "w_fsw": wf[par], "vc_p": vcp[par], "vb_p": vbp[par], "oht_p": oht[par], "cfar_p": cfars[par]})
        maps.append(m)
    return maps


_NC_CACHE = {}


def kernel(**inputs):
    TS = inputs["x_sample"].shape[1]
    TP = inputs["x_prompt"].shape[1]
    key = (TS, TP)
    if key not in _NC_CACHE:
        _NC_CACHE[key] = build(TS, TP)
    nc = _NC_CACHE[key]
    maps = make_in_maps(inputs)
    res = run_bass_kernel_spmd(nc, maps, core_ids=list(range(8)))
    B, Bs = inputs["x_prompt"].shape[0], inputs["x_sample"].shape[0]
    y_s = np.stack([np.asarray(res.results[cid]["ys"], dtype=np.float32) for cid in range(Bs)], axis=0)
    y_p = np.zeros((B, TP, D), np.float32)
    hq = TP // 2
    for cid in range(8):
        b, par = cid // 2, cid % 2
        yp = np.asarray(res.results[cid]["yp"], dtype=np.float32)
        if par == 0:
            y_p[b, 0:hq] = yp
        else:
            y_p[b, hq:] = yp[::-1]
    return (y_p, y_s)
```

```python
import math
import numpy as np
import ml_dtypes
from contextlib import ExitStack
import concourse.bass as bass
import concourse.mybir as mybir
from concourse.bass_utils import run_bass_kernel_spmd

F32 = mybir.dt.float32
BF16 = mybir.dt.bfloat16
AF = mybir.ActivationFunctionType
ALU = mybir.AluOpType
AX = mybir.AxisListType
NDMA = 16
EPS = 1e-6
D = 1024
NJ = 1280
J0 = 639


class Buf:
    def __init__(self, ctx, t):
        self.t = t
        self.wr = None
        self.rd = []
        self.subs = {}
        ctx.bufs.append(self)

    def __getitem__(self, k):
        return self.t[k]


class EngW:
    def __init__(self, name, eng, sem, same_sync):
        self.name, self.eng, self.sem = name, eng, sem
        self.count = 0
        self.waited = {}
        self.same_sync = same_sync

    def wait(self, deps):
        for d in deps:
            if d is None:
                continue
            sem, val = d
            if val <= 0:
                continue
            if sem is self.sem and not self.same_sync:
                continue
            if self.waited.get(sem, 0) >= val:
                continue
            self.eng.wait_ge(sem, val)
            self.waited[sem] = val


class Ctx:
    def __init__(self, nc, es):
        self.nc = nc
        self.E = {}
        self.bufs = []
        for name, eng, ss in [('pe', nc.tensor, False), ('act', nc.scalar, True),
                              ('dve', nc.vector, True), ('pool', nc.gpsimd, True),
                              ('sp', nc.sync, False)]:
            sem = es.enter_context(nc.semaphore('s_' + name))
            self.E[name] = EngW(name, eng, sem, ss)
        self.dma_sems = [es.enter_context(nc.semaphore('d%d' % i)) for i in range(2 * NDMA)]
        self.dma_val = [0] * (2 * NDMA)
        self.dma_next = [0, 0]

    @staticmethod
    def _bk(x):
        return x if isinstance(x, tuple) else (x, None)

    def _deps(self, reads, writes, extra):
        deps = list(extra)
        for x in reads:
            b, k = self._bk(x)
            deps.append(b.wr)
            if k is None:
                for sw, srd in b.subs.values():
                    deps.append(sw)
            elif k in b.subs:
                deps.append(b.subs[k][0])
        for x in writes:
            b, k = self._bk(x)
            deps.append(b.wr)
            deps.extend(b.rd)
            if k is None:
                for sw, srd in b.subs.values():
                    deps.append(sw)
                    deps.extend(srd)
            elif k in b.subs:
                deps.append(b.subs[k][0])
                deps.extend(b.subs[k][1])
        return deps

    def _mark(self, tok, reads, writes):
        for x in reads:
            b, k = self._bk(x)
            if k is None:
                b.rd.append(tok)
            else:
                b.subs.setdefault(k, [None, []])[1].append(tok)
        for x in writes:
            b, k = self._bk(x)
            if k is None:
                b.wr = tok
                b.rd = []
                b.subs = {}
            else:
                b.subs[k] = [tok, []]

    def op(self, en, fn, reads=(), writes=(), extra=()):
        e = self.E[en]
        e.wait(self._deps(reads, writes, extra))
        ins = fn(e.eng)
        e.count += 1
        ins.then_inc(e.sem, 1)
        tok = (e.sem, e.count)
        self._mark(tok, reads, writes)
        return tok

    def dma(self, en, out, in_, reads=(), writes=(), extra=()):
        e = self.E[en]
        grp = 1 if en == 'pool' else 0
        k = grp * NDMA + self.dma_next[grp]
        self.dma_next[grp] = (self.dma_next[grp] + 1) % NDMA
        sem = self.dma_sems[k]
        e.wait(self._deps(reads, writes, extra) + [(sem, self.dma_val[k])])
        ins = e.eng.dma_start(out=out, in_=in_)
        self.dma_val[k] += 16
        ins.then_inc(sem, 16)
        tok = (sem, self.dma_val[k])
        self._mark(tok, reads, writes)
        return tok

    def end_iter(self):
        nc = self.nc
        sp = self.E['sp']
        sp.wait([(s, v) for s, v in zip(self.dma_sems, self.dma_val)])
        nc.all_engine_barrier()
        for e in self.E.values():
            if e.count:
                nc.sync.sem_clear(e.sem)
        for s, v in zip(self.dma_sems[:NDMA], self.dma_val[:NDMA]):
            if v:
                nc.sync.sem_clear(s)
        nc.all_engine_barrier()
        for e in self.E.values():
            e.count = 0
            e.waited = {}
        self.dma_val = [0] * NDMA + self.dma_val[NDMA:]
        self.dma_next = [0, self.dma_next[1]]
        for b in self.bufs:
            b.wr = None
            b.rd = []
            b.subs = {}


def ds(start, size):
    if isinstance(start, int):
        return slice(start, start + size)
    return bass.ds(start, size)


import os
_LOOPK = [0]


class _Stop(Exception):
    pass


def _en(n):
    ph = os.environ.get("KPH")
    return ph is None or str(n) in ph.split(",")


def loop_iter(nc, n, static):
    k = _LOOPK[0] % 7
    _LOOPK[0] += 1
    en = os.environ.get("KLOOPS")
    if en is not None and str(k) not in en.split(","):
        return
    if static:
        for i in range(n):
            yield i
    else:
        with nc.Fori(0, n) as i:
            yield i


_PROBE = [0]


def free_regs(eng):
    regs = []
    try:
        for k in range(200):
            _PROBE[0] += 1
            regs.append(eng.alloc_register("probe%d" % _PROBE[0]))
    except Exception:
        pass
    for r in regs:
        eng.free_register(r)
    return len(regs)


_UID = [0]


class Pool_:
    def __init__(self, c, nc):
        self.c, self.nc = c, nc
        self.es = ExitStack()
        self.n = 0

    def __enter__(self):
        self.es.__enter__()
        return self

    def __exit__(self, *a):
        return self.es.__exit__(*a)

    def sb(self, shape, dt, name=None):
        self.n += 1
        _UID[0] += 1
        t = self.es.enter_context(self.nc.sbuf_tensor("%s_%d" % (name or "t", _UID[0]), shape, dt))
        return Buf(self.c, t)

    def ps(self, shape, dt=F32, name=None):
        self.n += 1
        _UID[0] += 1
        t = self.es.enter_context(self.nc.psum_tensor("%s_%d" % (name or "p", _UID[0]), shape, dt))
        return Buf(self.c, t)


def build(TS, TP, dbg=False):
    nc = bass.Bass("TRN2", target_bir_lowering=False)
    TM = max(TS, TP)
    TQP = TP // 2

    def din(name, shape, dt=F32):
        return nc.dram_tensor(name, shape, dt, kind="ExternalInput").ap()

    def dscr(name, shape, dt):
        return nc.dram_tensor(name, shape, dt, kind="ExternalOutput" if dbg else "Internal").ap()

    xs = din("xs", [TS, D])
    xp = din("xp", [TP, D])
    w_in = din("w_in", [D, 10 * D])
    w_fsw = din("w_fsw", [D, 2 * D])
    w_pa = din("w_pa", [D, D])
    w_pb = din("w_pb", [D, D])
    w_o = din("w_o", [D, D])
    w_up = din("w_up", [D, 4 * D])
    w_dn = din("w_dn", [4 * D, D])
    vc_in = {"s": din("vc_s", [128, 50]), "p": din("vc_p", [128, 50])}
    vb_in = {"s": din("vb_s", [128, 6400]), "p": din("vb_p", [128, 6400])}
    oht_in = {"s": din("oht_s", [32, NJ]), "p": din("oht_p", [32, NJ])}
    cfar_in = {"s": din("cfar_s", [8, 128, 2]), "p": din("cfar_p", [8, 128, 2])}
    relb_in = din("relb", [32, 8])
    cbf_in = din("cbf", [128, 256], BF16)
    cf_in = din("cf", [128, 384])
    ys = nc.dram_tensor("ys", [TS, D], F32, kind="ExternalOutput").ap()
    yp = nc.dram_tensor("yp", [TQP, D], F32, kind="ExternalOutput").ap()

    hT_d = dscr("hT_d", [8, 128, TM], BF16)
    aqT_d = dscr("aqT_d", [8, 128, TM], BF16)
    akT_d = dscr("akT_d", [8, 128, TM], BF16)
    gqT_d = dscr("gqT_d", [8, 128, TM], BF16)
    kT_d = [dscr("kfT_d", [8, 128, TM], BF16), dscr("kbT_d", [8, 128, TM], BF16)]
    av_d = dscr("av_d", [TM, D], BF16)
    gi_d = dscr("gi_d", [TM, D], BF16)
    ktm_d = [dscr("kftm_d", [TM, D], BF16), dscr("kbtm_d", [TM, D], BF16)]
    g_d = [dscr("gf_d", [TM, D], F32), dscr("gb_d", [TM, D], F32)]
    gogT_d = dscr("gogT_d", [8, 128, TM], BF16)
    gaT_d = dscr("gaT_d", [8, 128, TM], BF16)
    gbT_d = dscr("gbT_d", [8, 128, TM], BF16)
    oaT_d = dscr("oaT_d", [8, 128, TM], BF16)
    oT_d = [dscr("ofT_d", [8, 128, TM], F32), dscr("obT_d", [8, 128, TM], F32)]
    x1_d = dscr("x1_d", [TM, D], F32)
    tab_d = dscr("tab_d", [8, NJ], F32)
    bias_d = dscr("bias_d", [48, 128, 512], F32)

    with ExitStack() as es:
        c = Ctx(nc, es)

        with Pool_(c, nc) as gp:
            cbf = gp.sb([128, 256], BF16, "cbf")
            cf = gp.sb([128, 384], F32, "cf")
            c.dma('sp', cbf[:], cbf_in[:, :], writes=[cbf])
            c.dma('sp', cf[:], cf_in[:, :], writes=[cf])
            ident = cbf.t[:, 0:128]
            ones_bf = cbf.t[:, 128:256]
            antiI = cf.t[:, 0:128]
            U_f = cf.t[0:64, 128:192]
            Ut_f = cf.t[0:64, 192:256]
            ones_f = cf.t[:, 256:384]
            c.end_iter()

            for job in os.environ.get("KJOBS", "sp"):
                T = TS if job == "s" else TP
                Tq = TS if job == "s" else TQP
                x_d = xs if job == "s" else xp
                y_d = ys if job == "s" else yp
                run_job(nc, c, job, T, Tq, x_d, y_d, locals())
    print('free regs:', {n: free_regs(e.eng) for n, e in c.E.items()}, 'ninstr', nc.n_instructions if not callable(nc.n_instructions) else nc.n_instructions())
    return nc


def run_job(nc, c, job, T, Tq, x_d, y_d, G):
    ident, ones_bf, antiI, U_f, Ut_f, ones_f = G["ident"], G["ones_bf"], G["antiI"], G["U_f"], G["Ut_f"], G["ones_f"]
    cbf, cf = G["cbf"], G["cf"]
    vc_d, vb_d, oht_d, cfar_d = G["vc_in"][job], G["vb_in"][job], G["oht_in"][job], G["cfar_in"][job]
    w_in, w_fsw = G["w_in"], G["w_fsw"]
    hT_d, aqT_d, akT_d, gqT_d, kT_d = G["hT_d"], G["aqT_d"], G["akT_d"], G["gqT_d"], G["kT_d"]
    av_d, gi_d, ktm_d, g_d = G["av_d"], G["gi_d"], G["ktm_d"], G["g_d"]
    gogT_d, gaT_d, gbT_d, oaT_d, oT_d, x1_d = G["gogT_d"], G["gaT_d"], G["gbT_d"], G["oaT_d"], G["oT_d"], G["x1_d"]
    tab_d, bias_d, relb_in = G["tab_d"], G["bias_d"], G["relb_in"]
    NB = T // 512
    NBQ = Tq // 512
    static = (job == "p")

    with Pool_(c, nc) as jp:
        vc = jp.sb([128, 50], F32, "vc")
        c.dma('sp', vc[:], vc_d[:, :], writes=[vc])
        der = jp.sb([128, 32], F32, "der")
        lamt = jp.sb([128, 256], F32, "lamt")
        c.dma('sp', lamt[:], vb_d[:, 6144:6400], writes=[lamt])
        c.op('dve', lambda e: e.tensor_tensor(out=der.t[:, 0:8], in0=vc.t[:, 24:32], in1=vc.t[:, 16:24], op=ALU.subtract),
             reads=[vc], writes=[der])
        c.op('dve', lambda e: e.tensor_tensor(out=der.t[:, 8:16], in0=vc.t[:, 40:48], in1=vc.t[:, 32:40], op=ALU.subtract),
             reads=[vc], writes=[der])
        c.op('act', lambda e: e.activation(out=der.t[:, 0:16], in_=der.t[:, 0:16], func=AF.Sigmoid), reads=[der], writes=[der])
        c.op('dve', lambda e: e.tensor_scalar(out=der.t[:, 16:17], in0=vc.t[:, 48:49], scalar1=0.8, scalar2=None, op0=ALU.mult),
             reads=[vc], writes=[der])
        c.op('dve', lambda e: e.tensor_tensor(out=lamt.t[:, 0:128], in0=lamt.t[:, 0:128], in1=lamt.t[:, 128:256], op=ALU.mult),
             reads=[lamt], writes=[lamt])
        c.op('dve', lambda e: e.reduce_sum(out=der.t[:, 20:22], in_=lamt.t[:, 0:128].rearrange("p (a b) -> p a b", a=2), axis=AX.X),
             reads=[lamt, der], writes=[der])
        c.op('act', lambda e: e.activation(out=der.t[:, 20:22], in_=der.t[:, 20:22], func=AF.Exp), reads=[der], writes=[der])
        c.op('dve', lambda e: e.tensor_tensor(out=der.t[:, 22:23], in0=der.t[:, 21:22], in1=der.t[:, 20:21], op=ALU.subtract),
             reads=[der], writes=[der])
        c.op('dve', lambda e: e.tensor_scalar(out=der.t[:, 22:23], in0=der.t[:, 22:23], scalar1=-0.2, scalar2=None, op0=ALU.add),
             reads=[der], writes=[der])
        c.end_iter()
        g1col = vc.t[:, 0:8]
        g2col = vc.t[:, 8:16]
        omlb_col = [der.t[:, 0:8], der.t[:, 8:16]]
        gsubcol = der.t[:, 16:17]
        ghcol = vc.t[:, 49:50]
        neglam = der.t[:, 22:23]

        if _en(0):
            with Pool_(c, nc) as p:
                oht = p.sb([32, NJ], F32)
                relb = p.sb([32, 8], F32)
                tabs = p.sb([8, NJ], F32)
                c.dma('sp', oht[:], oht_d[:, :], writes=[oht])
                c.dma('sp', relb[:], relb_in[:, :], writes=[relb])
                pt = p.ps([8, 512], F32)
                for k0 in range(0, NJ, 512):
                    n = min(512, NJ - k0)
                    c.op('pe', lambda e: e.matmul(pt.t[:, 0:n], lhsT=relb.t[:, :], rhs=oht.t[:, k0:k0 + n], start=True, stop=True),
                         reads=[relb, oht], writes=[pt])
                    c.op('act', lambda e: e.activation(out=tabs.t[:, k0:k0 + n], in_=pt.t[:, 0:n], func=AF.Copy),
                         reads=[pt], writes=[tabs])
                c.dma('sp', tab_d[:, :], tabs[:], reads=[tabs])
                c.end_iter()
                hk = [p.sb([128, 512], F32) for _ in range(2)]
                bt = [p.sb([128, 512], F32) for _ in range(2)]
                pb = [p.ps([128, 512], F32) for _ in range(2)]
                n = 0
                for h in range(8):
                    for di in range(6):
                        delta = -128 + 128 * di
                        off = J0 - 127 - delta
                        src = bass.AP(tab_d.tensor, h * NJ + off, [[1, 128], [1, 512]])
                        c.dma('sp', hk[n % 2][:], src, writes=[hk[n % 2]])
                        c.op('pe', lambda e: e.matmul(pb[n % 2].t[:, :], lhsT=antiI, rhs=hk[n % 2].t[:, :], start=True, stop=True),
                             reads=[hk[n % 2], cf], writes=[pb[n % 2]])
                        c.op('act', lambda e: e.activation(out=bt[n % 2].t[:, :], in_=pb[n % 2].t[:, :], func=AF.Copy),
                             reads=[pb[n % 2]], writes=[bt[n % 2]])
                        c.dma('sp', bias_d[h * 6 + di, :, :], bt[n % 2][:], reads=[bt[n % 2]])
                        n += 1
                c.end_iter()

        def norm_transpose(p, xt_list, hts, gcol, scr):
            junk, xn, ss, ptr = scr
            for j in range(4):
                c.op('act', lambda e: e.activation(out=junk.t[:], in_=xt_list.t[:, j, :], func=AF.Square,
                                                   accum_out=ss.t[:, j:j + 1]),
                     reads=[xt_list], writes=[junk, ss])
                c.op('dve', lambda e: e.tensor_scalar(out=ss.t[:, 4 + j:5 + j], in0=ss.t[:, j:j + 1], scalar1=1.0 / D,
                                                      scalar2=EPS, op0=ALU.mult, op1=ALU.add), reads=[ss], writes=[ss])
                c.op('act', lambda e: e.activation(out=ss.t[:, 4 + j:5 + j], in_=ss.t[:, 4 + j:5 + j], func=AF.Sqrt),
                     reads=[ss], writes=[ss])
                c.op('dve', lambda e: e.reciprocal(out=ss.t[:, 8 + j:9 + j], in_=ss.t[:, 4 + j:5 + j]), reads=[ss], writes=[ss])
                c.op('dve', lambda e: e.tensor_scalar(out=xn[j % 2].t[:], in0=xt_list.t[:, j, :], scalar1=ss.t[:, 8 + j:9 + j],
                                                      scalar2=None, op0=ALU.mult),
                     reads=[xt_list, ss], writes=[xn[j % 2]])
                for cc in range(8):
                    c.op('pe', lambda e: e.transpose(out=ptr[j % 2].t[:, cc, :], in_=xn[j % 2].t[:, cc * 128:(cc + 1) * 128],
                                                     identity=ident),
                         reads=[xn[j % 2], cbf], writes=[ptr[j % 2]])
                c.op('act', lambda e: e.activation(out=hts.t[:, :, j * 128:(j + 1) * 128], in_=ptr[j % 2].t[:], func=AF.Copy),
                     reads=[ptr[j % 2]], writes=[hts])
            c.op('pool', lambda e: e.tensor_tensor(out=hts.t[:], in0=hts.t[:], in1=gcol.unsqueeze(2).to_broadcast([128, 8, 512]),
                                                   op=ALU.mult), reads=[hts, vc], writes=[hts])

        def nt_scratch(p):
            return (p.sb([128, 1024], BF16), [p.sb([128, 1024], BF16) for _ in range(2)], p.sb([128, 12], F32),
                    [p.ps([128, 8, 128], BF16) for _ in range(2)])

        if _en(1):
            with Pool_(c, nc) as p:
                xt4 = p.sb([128, 4, 1024], F32)
                hts = p.sb([128, 8, 512], BF16)
                scr = nt_scratch(p)

                for i in loop_iter(nc, NB, static):
                    c.dma('sp', xt4[:], x_d[ds(i * 512, 512), :].rearrange("(j p) d -> p j d", p=128), writes=[xt4])
                    norm_transpose(p, xt4, hts, g1col, scr)
                    c.dma('sp', hT_d[:, :, ds(i * 512, 512)].rearrange("c p t -> p c t"), hts[:], reads=[hts])
                    c.end_iter()

        def load_w(p, src_ap, ncols=1024, nk=8):
            w = p.sb([128, nk, ncols], BF16, "w")
            for kc in range(nk):
                for c0 in range(0, ncols, 1024):
                    c.dma('pool', w.t[:, kc, c0:c0 + 1024], src_ap[kc * 128:(kc + 1) * 128, c0:c0 + 1024], writes=[w])
            return w

        class PsRot:
            def __init__(self, p, n):
                self.b = [p.ps([128, 512], F32) for _ in range(n)]
                self.i = 0

            def next(self):
                b = self.b[self.i % len(self.b)]
                self.i += 1
                return b

        def proj_cm(w, act, psr, evac, ncol_chunks=8, nk=8):
            for j in range(ncol_chunks):
                ps = psr.next()
                for k in range(nk):
                    c.op('pe', lambda e: e.matmul(ps.t[:, :], lhsT=w.t[:, k, j * 128:(j + 1) * 128], rhs=act.t[:, k, :],
                                                  start=(k == 0), stop=(k == nk - 1)),
                         reads=[w, act], writes=[ps])
                evac(j, ps)

        def proj_tm(w, act, psr, evac, ncols=1024, nk=8, ntok=4):
            for s in range(ntok):
                for hf in range(ncols // 512):
                    ps = psr.next()
                    for k in range(nk):
                        c.op('pe', lambda e: e.matmul(ps.t[:, :], lhsT=act.t[:, k, s * 128:(s + 1) * 128],
                                                      rhs=w.t[:, k, hf * 512:(hf + 1) * 512],
                                                      start=(k == 0), stop=(k == nk - 1)),
                             reads=[w, act], writes=[ps])
                    evac(s, hf, ps)

        wf_src = [w_in[:, 4096:5120], w_in[:, 5120:6144]] if job == "s" else [w_fsw[:, 0:1024], w_fsw[:, 1024:2048]]

        if _en(2):
            with Pool_(c, nc) as p:
                W = {"aq": load_w(p, w_in[:, 0:1024]), "ak": load_w(p, w_in[:, 1024:2048]),
                     "av": load_w(p, w_in[:, 2048:3072]), "gq": load_w(p, w_in[:, 3072:4096]),
                     "f0": load_w(p, wf_src[0]), "f1": load_w(p, wf_src[1]), "gi": load_w(p, w_in[:, 6144:7168])}
                lbB = p.sb([128, 4, 1024], F32, "lbB")
                c.dma('sp', lbB[:], vb_d[:, 2048:6144].rearrange("p (a b) -> p a b", a=4), writes=[lbB])
                with Pool_(c, nc) as tp_:
                    dtmp = tp_.sb([128, 2, 1024], F32, "dtmp")
                    for d_ in range(2):
                        c.op('dve', lambda e: e.tensor_tensor(out=dtmp.t[:, d_, :], in0=lbB.t[:, 2 * d_, :], in1=lbB.t[:, 2 * d_ + 1, :],
                                                              op=ALU.subtract), reads=[lbB], writes=[dtmp])
                    for d_ in range(2):
                        c.op('act', lambda e: e.activation(out=lbB.t[:, 2 * d_, :], in_=dtmp.t[:, d_, :], func=AF.Sigmoid),
                             reads=[dtmp], writes=[lbB])
                        c.op('act', lambda e: e.activation(out=lbB.t[:, 2 * d_ + 1, :], in_=dtmp.t[:, d_, :], func=AF.Sigmoid, scale=-1.0),
                             reads=[dtmp], writes=[lbB])
                    c.end_iter()
                hts = p.sb([128, 8, 512], BF16, "hts")
                stg_cm = [p.sb([128, 8, 512], BF16, "stgcm") for _ in range(2)]
                stg_bf = [p.sb([128, 1024], BF16, "stgbf") for _ in range(4)]
                stg_f = [p.sb([128, 1024], F32, "stgf") for _ in range(4)]
                sig = [p.sb([128, 512], F32, "sig") for _ in range(8)]
                psr = PsRot(p, 8)
                cnt = {"cm": 0, "tm": 0, "sg": 0}
                tq = 'sp' if static else 'act'
                for i in loop_iter(nc, NB, static):
                    c.dma('sp', hts[:], hT_d[:, :, ds(i * 512, 512)].rearrange("c p t -> p c t"), writes=[hts])

                    def cm_plain(wname, dst):
                        st = stg_cm[cnt["cm"] % 2]
                        cnt["cm"] += 1
                        eng = ['act', 'dve']

                        def ev(j, ps):
                            if j % 2 == 0:
                                c.op('act', lambda e: e.activation(out=st.t[:, j, :], in_=ps.t[:, :], func=AF.Copy),
                                     reads=[ps], writes=[(st, j)])
                            else:
                                c.op('dve', lambda e: e.tensor_copy(out=st.t[:, j, :], in_=ps.t[:, :]), reads=[ps], writes=[(st, j)])
                        proj_cm(W[wname], hts, psr, ev)
                        c.dma('sp', dst[:, :, ds(i * 512, 512)].rearrange("c p t -> p c t"), st[:], reads=[st])

                    def tm_plain(wname, dst):
                        sts = {}

                        def ev(s, hf, ps):
                            if hf == 0:
                                sts[s] = stg_bf[cnt["tm"] % 4]
                                cnt["tm"] += 1
                            st = sts[s]
                            if hf == 0:
                                c.op('act', lambda e: e.activation(out=st.t[:, 0:512], in_=ps.t[:, :], func=AF.Copy),
                                     reads=[ps], writes=[(st, 0)])
                            else:
                                c.op('dve', lambda e: e.tensor_copy(out=st.t[:, 512:1024], in_=ps.t[:, :]), reads=[ps], writes=[(st, 1)])
                                c.dma(tq, dst[ds(i * 512 + s * 128, 128), :], st[:], reads=[st])
                        proj_tm(W[wname], hts, psr, ev)

                    def f_cm(d_):
                        st = stg_cm[cnt["cm"] % 2]
                        cnt["cm"] += 1

                        def ev(j, ps):
                            sg = sig[cnt["sg"] % 8]
                            cnt["sg"] += 1
                            c.op('act', lambda e: e.activation(out=sg.t[:, :], in_=ps.t[:, :], func=AF.Sigmoid, scale=-1.0),
                                 reads=[ps], writes=[sg])
                            c.op('dve', lambda e: e.tensor_scalar(out=st.t[:, j, :], in0=sg.t[:, :], scalar1=omlb_col[d_][:, j:j + 1],
                                                                  scalar2=None, op0=ALU.mult), reads=[sg, der], writes=[(st, j)])
                        proj_cm(W["f%d" % d_], hts, psr, ev)
                        c.dma('sp', kT_d[d_][:, :, ds(i * 512, 512)].rearrange("c p t -> p c t"), st[:], reads=[st])

                    def f_tm(d_):
                        sts = {}

                        def ev(s, hf, ps):
                            if hf == 0:
                                sts[s] = (stg_bf[cnt["tm"] % 4], stg_f[cnt["tm"] % 4])
                                cnt["tm"] += 1
                            sb_, sf_ = sts[s]
                            sg = sig[cnt["sg"] % 8]
                            cnt["sg"] += 1
                            cs = slice(hf * 512, (hf + 1) * 512)
                            c.op('act', lambda e: e.activation(out=sg.t[:, :], in_=ps.t[:, :], func=AF.Sigmoid), reads=[ps], writes=[sg])
                            c.op('dve', lambda e: e.tensor_tensor(out=sg.t[:, :], in0=sg.t[:, :], in1=lbB.t[:, 2 * d_ + 1, cs], op=ALU.mult),
                                 reads=[sg, lbB], writes=[sg])
                            c.op('dve', lambda e: e.tensor_tensor(out=sg.t[:, :], in0=sg.t[:, :], in1=lbB.t[:, 2 * d_, cs], op=ALU.add),
                                 reads=[sg, lbB], writes=[sg])
                            def post(sg=sg, sf_=sf_, sb_=sb_, cs=cs, hf=hf, s=s):
                                c.op('act', lambda e: e.activation(out=sf_.t[:, cs], in_=sg.t[:, :], func=AF.Ln), reads=[sg], writes=[(sf_, hf)])
                                c.op('pool', lambda e: e.tensor_scalar(out=sb_.t[:, cs], in0=sg.t[:, :], scalar1=-1.0, scalar2=1.0,
                                                                       op0=ALU.mult, op1=ALU.add), reads=[sg], writes=[(sb_, hf)])
                                if hf == 1:
                                    c.dma(tq, g_d[d_][ds(i * 512 + s * 128, 128), :], sf_[:], reads=[sf_])
                                    c.dma(tq, ktm_d[d_][ds(i * 512 + s * 128, 128), :], sb_[:], reads=[sb_])
                            posts.append(post)
                        posts = []
                        proj_tm(W["f%d" % d_], hts, psr, ev)
                        for po in posts:
                            po()

                    cm_plain("aq", aqT_d)
                    cm_plain("ak", akT_d)
                    tm_plain("av", av_d)
                    cm_plain("gq", gqT_d)
                    tm_plain("gi", gi_d)
                    f_cm(0)
                    f_cm(1)
                    f_tm(0)
                    f_tm(1)
                    c.end_iter()

        if _en(3):
            with Pool_(c, nc) as p:
                W = {"og": load_w(p, w_in[:, 7168:8192]), "ga": load_w(p, w_in[:, 8192:9216]), "gb": load_w(p, w_in[:, 9216:10240])}
                hts = p.sb([128, 8, 512], BF16, "hts")
                stg_cm = [p.sb([128, 8, 512], BF16, "stgcm") for _ in range(2)]
                psr = PsRot(p, 8)
                c.end_iter()
                for i in loop_iter(nc, NBQ, static):
                    c.dma('sp', hts[:], hT_d[:, :, ds(i * 512, 512)].rearrange("c p t -> p c t"), writes=[hts])
                    for n_, (wname, dst, fn) in enumerate((("ga", gaT_d, AF.Sigmoid), ("gb", gbT_d, AF.Sigmoid), ("og", gogT_d, AF.Silu))):
                        st = stg_cm[n_ % 2]

                        def ev(j, ps):
                            c.op('act', lambda e: e.activation(out=st.t[:, j, :], in_=ps.t[:, :], func=fn), reads=[ps], writes=[(st, j)])
                        proj_cm(W[wname], hts, psr, ev)
                        c.dma('sp', dst[:, :, ds(i * 512, 512)].rearrange("c p t -> p c t"), st[:], reads=[st])
                    c.end_iter()

        NKB = T // 128
        if _en(4):
            with Pool_(c, nc) as p:
                kT = p.sb([128, 1, T], BF16, "kT")
                vv = p.sb([128, NKB, 128], BF16, "vv")
                biasT = p.sb([128, 6, 512], F32, "biasT")
                cfar = p.sb([128, 1, 2], F32, "cfar")
                qT = p.sb([128, 1, Tq], BF16, "qT")
                pS = [[p.ps([128, 512], F32, "pS") for _ in range(2)] for _ in range(2)]
                pO = [p.ps([128, 512], F32, "pO") for _ in range(2)]
                pZ = [p.ps([128, 512], F32, "pZ") for _ in range(2)]
                pT = [[p.sb([128, 512], BF16, "pT") for _ in range(4)] for _ in range(2)]
                tmpS = [p.sb([128, 512], F32, "tmpS") for _ in range(2)]
                zacc = [[p.sb([128, 512], F32, "zacc") for _ in range(2)] for _ in range(2)]
                rz = [p.sb([128, 512], F32, "rz") for _ in range(2)]
                o01 = [p.sb([128, 512], F32, "o01") for _ in range(2)]
                osq = p.sb([128, 512], BF16, "osq")
                rs = p.sb([128, 512], F32, "rs")
                oout = p.sb([128, 1, Tq], BF16, "oout")
                for h in loop_iter(nc, 8, static):
                    c.dma('sp', kT[:], akT_d[ds(h, 1), :, 0:T].rearrange("a p t -> p a t"), writes=[kT])
                    c.dma('sp', vv[:], av_d[0:T, ds(h * 128, 128)].rearrange("(kb p) d -> p kb d", p=128), writes=[vv])
                    c.dma('sp', biasT[:], bias_d[ds(h * 6, 6), :, :].rearrange("d p q -> p d q"), writes=[biasT])
                    c.dma('sp', cfar[:], cfar_d[ds(h, 1), :, :].rearrange("a p s -> p a s"), writes=[cfar])
                    c.dma('sp', qT[:], aqT_d[ds(h, 1), :, 0:Tq].rearrange("a p t -> p a t"), writes=[qT])
                    for qb in range(Tq // 512):
                        q_ = qT
                        qsl = slice(qb * 512, (qb + 1) * 512)
                        def qk_exp(kb):
                            delta = kb * 128 - qb * 512
                            near = -128 <= delta <= 512
                            pts = []
                            pss = []
                            for m in range(2):
                                ps = pS[m][kb % 2]
                                pr = slice(m * 64, (m + 1) * 64)
                                c.op('pe', lambda e: e.matmul(ps.t[:, :], lhsT=kT.t[pr, 0, kb * 128:(kb + 1) * 128], rhs=q_.t[pr, 0, qsl],
                                                              start=True, stop=True), reads=[kT, q_], writes=[ps])
                                pss.append(ps)
                            for m in range(2):
                                ps = pss[m]
                                pt_ = pT[m][kb % 4]
                                if near:
                                    di = (delta + 128) // 128
                                    tm_ = tmpS[m]
                                    c.op('dve', lambda e: e.scalar_tensor_tensor(out=tm_.t[:, :], in0=ps.t[:, :], scalar=0.125,
                                                                                 in1=biasT.t[:, di, :], op0=ALU.mult, op1=ALU.add),
                                         reads=[ps, biasT], writes=[tm_])
                                    c.op('act', lambda e: e.activation(out=pt_.t[:, :], in_=tm_.t[:, :], func=AF.Exp),
                                         reads=[tm_], writes=[pt_])
                                else:
                                    side = 0 if delta > 0 else 1
                                    c.op('act', lambda e: e.activation(out=pt_.t[:, :], in_=ps.t[:, :], func=AF.Exp,
                                                                       bias=cfar.t[:, 0, side:side + 1], scale=0.125),
                                         reads=[ps, cfar], writes=[pt_])
                                pts.append(pt_)
                            return pts

                        def pv_z(kb, pts):
                            for m in range(2):
                                pt_ = pts[m]
                                c.op('pe', lambda e: e.matmul(pO[m].t[:, :], lhsT=vv.t[:, kb, :], rhs=pt_.t[:, :],
                                                              start=(kb == 0), stop=(kb == NKB - 1)), reads=[vv, pt_], writes=[pO[m]])
                                if kb % 2 == 0:
                                    c.op('pe', lambda e: e.matmul(pZ[m].t[:, :], lhsT=ones_bf, rhs=pt_.t[:, :],
                                                                  start=(kb == 0), stop=False), reads=[cbf, pt_], writes=[pZ[m]])
                                else:
                                    zeng = 'dve' if m == 0 else 'pool'
                                    za = zacc[m][0]
                                    if kb == 1:
                                        c.op(zeng, lambda e: e.tensor_copy(out=za.t[:, :], in_=pt_.t[:, :]), reads=[pt_], writes=[za])
                                    else:
                                        c.op(zeng, lambda e: e.tensor_tensor(out=za.t[:, :], in0=za.t[:, :], in1=pt_.t[:, :], op=ALU.add),
                                             reads=[za, pt_], writes=[za])

                        pend = qk_exp(0)
                        for kb in range(NKB):
                            nxt = qk_exp(kb + 1) if kb + 1 < NKB else None
                            pv_z(kb, pend)
                            pend = nxt
                        for m in range(2):
                            c.op('pe', lambda e: e.matmul(pZ[m].t[:, :], lhsT=ones_f, rhs=zacc[m][0].t[:, :], start=False, stop=True),
                                 reads=[cf, zacc[m][0]], writes=[pZ[m]])
                        for m in range(2):
                            c.op('dve', lambda e: e.reciprocal(out=rz[m].t[:, :], in_=pZ[m].t[:, :]), reads=[pZ[m]], writes=[rz[m]])
                            c.op('dve', lambda e: e.tensor_tensor(out=o01[m].t[:, :], in0=pO[m].t[:, :], in1=rz[m].t[:, :], op=ALU.mult),
                                 reads=[pO[m], rz[m]], writes=[o01[m]])
                        c.op('dve', lambda e: e.scalar_tensor_tensor(out=o01[0].t[:, :], in0=o01[1].t[:, :], scalar=neglam,
                                                                     in1=o01[0].t[:, :], op0=ALU.mult, op1=ALU.add),
                             reads=[o01[0], o01[1], der], writes=[o01[0]])
                        c.op('pool', lambda e: e.tensor_tensor(out=osq.t[:, :], in0=o01[0].t[:, :], in1=o01[0].t[:, :], op=ALU.mult),
                             reads=[o01[0]], writes=[osq])
                        c.op('pe', lambda e: e.matmul(pZ[0].t[:, :], lhsT=ones_bf, rhs=osq.t[:, :], start=True, stop=True),
                             reads=[cbf, osq], writes=[pZ[0]])
                        c.op('dve', lambda e: e.tensor_scalar(out=rs.t[:, :], in0=pZ[0].t[:, :], scalar1=1.0 / 128, scalar2=EPS,
                                                              op0=ALU.mult, op1=ALU.add), reads=[pZ[0]], writes=[rs])
                        c.op('act', lambda e: e.activation(out=rs.t[:, :], in_=rs.t[:, :], func=AF.Ln), reads=[rs], writes=[rs])
                        c.op('act', lambda e: e.activation(out=rs.t[:, :], in_=rs.t[:, :], func=AF.Exp, scale=-0.5), reads=[rs], writes=[rs])
                        oo = oout
                        c.op('dve', lambda e: e.scalar_tensor_tensor(out=oo.t[:, 0, qsl], in0=o01[0].t[:, :], scalar=gsubcol,
                                                                     in1=rs.t[:, :], op0=ALU.mult, op1=ALU.mult),
                             reads=[o01[0], rs, der], writes=[oo])
                    c.dma('sp', oaT_d[ds(h, 1), :, 0:Tq].rearrange("a p t -> p a t"), oout[:], reads=[oout])
                    c.end_iter()

        NG = T // 256
        if _en(5):
            with Pool_(c, nc) as p:
                S = [p.sb([128, 8, 128], F32, "S") for _ in range(2)]
                Sbf = [p.sb([128, 8, 128], BF16, "Sbf") for _ in range(2)]
                qTt = [p.sb([128, 8, 256], BF16, "qTt") for _ in range(2)]
                kTt = [p.sb([128, 8, 256], BF16, "kTt") for _ in range(2)]
                ktm = [p.sb([64, 4, 1024], BF16, "ktm") for _ in range(2)]
                gtm = [p.sb([64, 4, 1024], F32, "gtm") for _ in range(2)]
                vtm = [p.sb([64, 4, 1024], BF16, "vtm") for _ in range(2)]
                ostg = [p.sb([128, 8, 256], F32, "ostg") for _ in range(2)]
                ET = [p.sb([128, 8, 64], F32, "ET") for _ in range(2)]
                EiT = [p.sb([128, 8, 64], F32, "EiT") for _ in range(2)]
                Eitm = [p.sb([64, 1024], F32, "Eitm") for _ in range(2)]
                qt_ = [p.sb([128, 8, 64], BF16, "qt_") for _ in range(2)]
                kt_ = [p.sb([128, 8, 64], BF16, "kt_") for _ in range(2)]
                ktm_ = [p.sb([64, 1024], BF16, "ktm_") for _ in range(2)]
                ATm = [p.sb([64, 8, 64], BF16, "ATm") for _ in range(2)]
                Stmp = [p.sb([128, 8, 128], F32, "Stmp") for _ in range(2)]
                p_bT = p.ps([128, 8, 64], F32, "p_bT")
                p_btm = p.ps([64, 1024], F32, "p_btm")
                p_AT = p.ps([64, 8, 64], F32, "p_AT")
                p_oT = p.ps([128, 8, 64], F32, "p_oT")
                p_dS = p.ps([128, 8, 128], F32, "p_dS")
                for d_ in range(2):
                    c.op('pool', lambda e: e.memset(S[d_].t[:], 0.0), writes=[S[d_]])
                    c.op('pool', lambda e: e.memset(Sbf[d_].t[:], 0.0), writes=[Sbf[d_]])
                c.end_iter()
                dq = 'sp' if static else 'act'
                for i in loop_iter(nc, NG, static):
                    for d_ in range(2):
                        t0 = i * 256 if d_ == 0 else (NG - 1 - i) * 256
                        c.dma(dq, qTt[d_][:], gqT_d[:, :, ds(t0, 256)].rearrange("c p t -> p c t"), writes=[qTt[d_]])
                        c.dma(dq, kTt[d_][:], kT_d[d_][:, :, ds(t0, 256)].rearrange("c p t -> p c t"), writes=[kTt[d_]])
                        c.dma(dq, ktm[d_][:], ktm_d[d_][ds(t0, 256), :].rearrange("(a s) d -> s a d", s=64), writes=[ktm[d_]])
                        c.dma(dq, gtm[d_][:], g_d[d_][ds(t0, 256), :].rearrange("(a s) d -> s a d", s=64), writes=[gtm[d_]])
                        c.dma(dq, vtm[d_][:], gi_d[ds(t0, 256), :].rearrange("(a s) d -> s a d", s=64), writes=[vtm[d_]])
                    for cc in range(4):
                        for d_ in range(2):
                            ch = cc if d_ == 0 else 3 - cc
                            Um = U_f if d_ == 0 else Ut_f
                            last = 63 if d_ == 0 else 0
                            tsl = slice(ch * 64, (ch + 1) * 64)
                            for hh in range(8):
                                c.op('pe', lambda e: e.matmul(p_bT.t[:, hh, :], lhsT=gtm[d_].t[:, ch, hh * 128:(hh + 1) * 128], rhs=Um,
                                                              start=True, stop=True), reads=[gtm[d_], cf], writes=[p_bT])
                            for hf in range(2):
                                c.op('pe', lambda e: e.matmul(p_btm.t[:, hf * 512:(hf + 1) * 512], lhsT=Um,
                                                              rhs=gtm[d_].t[:, ch, hf * 512:(hf + 1) * 512], start=True, stop=True),
                                     reads=[gtm[d_], cf], writes=[p_btm])
                            c.op('act', lambda e: e.activation(out=ET[d_].t[:], in_=p_bT.t[:], func=AF.Exp), reads=[p_bT], writes=[ET[d_]])
                            c.op('act', lambda e: e.activation(out=EiT[d_].t[:], in_=p_bT.t[:], func=AF.Exp, scale=-1.0),
                                 reads=[p_bT], writes=[EiT[d_]])
                            c.op('act', lambda e: e.activation(out=Eitm[d_].t[:], in_=p_btm.t[:], func=AF.Exp, scale=-1.0),
                                 reads=[p_btm], writes=[Eitm[d_]])
                            c.op('dve', lambda e: e.tensor_tensor(out=qt_[d_].t[:], in0=qTt[d_].t[:, :, tsl], in1=ET[d_].t[:], op=ALU.mult),
                                 reads=[qTt[d_], ET[d_]], writes=[qt_[d_]])
                            c.op('pool', lambda e: e.tensor_tensor(out=kt_[d_].t[:], in0=kTt[d_].t[:, :, tsl], in1=EiT[d_].t[:], op=ALU.mult),
                                 reads=[kTt[d_], EiT[d_]], writes=[kt_[d_]])
                            c.op('pool', lambda e: e.tensor_tensor(out=ktm_[d_].t[:], in0=ktm[d_].t[:, ch, :], in1=Eitm[d_].t[:], op=ALU.mult),
                                 reads=[ktm[d_], Eitm[d_]], writes=[ktm_[d_]])
                            for hh in range(8):
                                c.op('pe', lambda e: e.matmul(p_AT.t[:, hh, :], lhsT=kt_[d_].t[:, hh, :], rhs=qt_[d_].t[:, hh, :],
                                                              start=True, stop=True), reads=[kt_[d_], qt_[d_]], writes=[p_AT])
                            c.op('dve', lambda e: e.tensor_tensor(out=ATm[d_].t[:], in0=p_AT.t[:],
                                                                  in1=Um.unsqueeze(1).to_broadcast([64, 8, 64]), op=ALU.mult),
                                 reads=[p_AT, cf], writes=[ATm[d_]])
                            for hh in range(8):
                                c.op('pe', lambda e: e.matmul(p_oT.t[:, hh, :], lhsT=Sbf[d_].t[:, hh, :], rhs=qt_[d_].t[:, hh, :],
                                                              start=True, stop=False), reads=[Sbf[d_], qt_[d_]], writes=[p_oT])
                                c.op('pe', lambda e: e.matmul(p_oT.t[:, hh, :], lhsT=vtm[d_].t[:, ch, hh * 128:(hh + 1) * 128],
                                                              rhs=ATm[d_].t[:, hh, :], start=False, stop=True),
                                     reads=[vtm[d_], ATm[d_]], writes=[p_oT])
                            c.op('act', lambda e: e.activation(out=ostg[d_].t[:, :, tsl], in_=p_oT.t[:], func=AF.Copy),
                                 reads=[p_oT], writes=[ostg[d_]])
                            for hh in range(8):
                                c.op('pe', lambda e: e.matmul(p_dS.t[:, hh, :], lhsT=ktm_[d_].t[:, hh * 128:(hh + 1) * 128],
                                                              rhs=vtm[d_].t[:, ch, hh * 128:(hh + 1) * 128], start=True, stop=True),
                                     reads=[ktm_[d_], vtm[d_]], writes=[p_dS])
                            c.op('dve', lambda e: e.tensor_tensor(out=Stmp[d_].t[:], in0=S[d_].t[:], in1=p_dS.t[:], op=ALU.add),
                                 reads=[S[d_], p_dS], writes=[Stmp[d_]])
                            c.op('dve', lambda e: e.tensor_tensor(out=S[d_].t[:], in0=Stmp[d_].t[:],
                                                                  in1=ET[d_].t[:, :, last:last + 1].to_broadcast([128, 8, 128]), op=ALU.mult),
                                 reads=[Stmp[d_], ET[d_]], writes=[S[d_]])
                            c.op('act', lambda e: e.activation(out=Sbf[d_].t[:], in_=S[d_].t[:], func=AF.Copy), reads=[S[d_]], writes=[Sbf[d_]])
                    for d_ in range(2):
                        t0 = i * 256 if d_ == 0 else (NG - 1 - i) * 256
                        c.dma(dq, oT_d[d_][:, :, ds(t0, 256)].rearrange("c p t -> p c t"), ostg[d_][:], reads=[ostg[d_]])
                    c.end_iter()

        if _en(6):
            with Pool_(c, nc) as p:
                Wa = load_w(p, G["w_pa"][:, :])
                Wb = load_w(p, G["w_pb"][:, :])
                Wo = load_w(p, G["w_o"][:, :])
                gpost = p.sb([128, 1024], F32, "gpost")
                c.dma('sp', gpost[:], vb_d[:, 0:1024], writes=[gpost])
                c.end_iter()
                oa = p.sb([128, 8, 512], BF16, "oa")
                of_ = p.sb([128, 8, 512], F32, "of")
                ob_ = p.sb([128, 8, 512], F32, "ob")
                og = p.sb([128, 8, 512], BF16, "og")
                ga = p.sb([128, 8, 512], BF16, "ga")
                gb = p.sb([128, 8, 512], BF16, "gb")
                obn = p.sb([128, 8, 512], BF16, "obn")
                mT = p.sb([128, 8, 512], BF16, "mT")
                sq = [p.sb([128, 512], BF16, "sq") for _ in range(2)]
                rs = [p.sb([128, 512], F32, "rs") for _ in range(2)]
                t1 = [p.sb([128, 512], F32, "t1") for _ in range(2)]
                t2 = [p.sb([128, 512], F32, "t2") for _ in range(2)]
                xt4 = p.sb([128, 4, 1024], F32, "xt4")
                yt4 = p.sb([128, 4, 1024], F32, "yt4")
                junk = p.sb([128, 512], BF16, "junk")
                ss = p.sb([128, 16], F32, "ss")
                psr = PsRot(p, 8)
                for i in loop_iter(nc, NBQ, static):
                    tsl = ds(i * 512, 512)
                    c.dma('sp', oa[:], oaT_d[:, :, tsl].rearrange("c p t -> p c t"), writes=[oa])
                    c.dma('sp', of_[:], oT_d[0][:, :, tsl].rearrange("c p t -> p c t"), writes=[of_])
                    c.dma('sp', ob_[:], oT_d[1][:, :, tsl].rearrange("c p t -> p c t"), writes=[ob_])
                    c.dma('sp', og[:], gogT_d[:, :, tsl].rearrange("c p t -> p c t"), writes=[og])
                    c.dma('sp', ga[:], gaT_d[:, :, tsl].rearrange("c p t -> p c t"), writes=[ga])
                    c.dma('sp', gb[:], gbT_d[:, :, tsl].rearrange("c p t -> p c t"), writes=[gb])
                    c.dma('sp', xt4[:], x_d[ds(i * 512, 512), :].rearrange("(j p) d -> p j d", p=128), writes=[xt4])
                    c.op('pool', lambda e: e.tensor_tensor(out=of_.t[:], in0=of_.t[:], in1=ob_.t[:], op=ALU.add),
                         reads=[of_, ob_], writes=[of_])
                    for hh in range(8):
                        s_ = sq[hh % 2]
                        r_ = rs[hh % 2]
                        c.op('act', lambda e: e.activation(out=s_.t[:, :], in_=of_.t[:, hh, :], func=AF.Square), reads=[of_], writes=[s_])
                        ps = psr.next()
                        c.op('pe', lambda e: e.matmul(ps.t[:, :], lhsT=ones_bf, rhs=s_.t[:, :], start=True, stop=True),
                             reads=[cbf, s_], writes=[ps])
                        c.op('dve', lambda e: e.tensor_scalar(out=r_.t[:, :], in0=ps.t[:, :], scalar1=1.0 / 128, scalar2=EPS,
                                                              op0=ALU.mult, op1=ALU.add), reads=[ps], writes=[r_])
                        c.op('act', lambda e: e.activation(out=r_.t[:, :], in_=r_.t[:, :], func=AF.Ln), reads=[r_], writes=[r_])
                        c.op('act', lambda e: e.activation(out=r_.t[:, :], in_=r_.t[:, :], func=AF.Exp, scale=-0.5), reads=[r_], writes=[r_])
                        c.op('dve', lambda e: e.scalar_tensor_tensor(out=r_.t[:, :], in0=of_.t[:, hh, :], scalar=ghcol, in1=r_.t[:, :],
                                                                     op0=ALU.mult, op1=ALU.mult), reads=[of_, r_, vc], writes=[r_])
                        c.op('pool', lambda e: e.tensor_tensor(out=obn.t[:, hh, :], in0=r_.t[:, :], in1=og.t[:, hh, :], op=ALU.mult),
                             reads=[r_, og], writes=[(obn, hh)])
                    for j in range(8):
                        psa = psr.next()
                        for k in range(8):
                            c.op('pe', lambda e: e.matmul(psa.t[:, :], lhsT=Wa.t[:, k, j * 128:(j + 1) * 128], rhs=oa.t[:, k, :],
                                                          start=(k == 0), stop=(k == 7)), reads=[Wa, oa], writes=[psa])
                        psb = psr.next()
                        for k in range(8):
                            c.op('pe', lambda e: e.matmul(psb.t[:, :], lhsT=Wb.t[:, k, j * 128:(j + 1) * 128], rhs=obn.t[:, k, :],
                                                          start=(k == 0), stop=(k == 7)), reads=[Wb, obn], writes=[psb])
                        a_, b_ = t1[j % 2], t2[j % 2]
                        c.op('dve', lambda e: e.tensor_tensor(out=a_.t[:, :], in0=psa.t[:, :], in1=ga.t[:, j, :], op=ALU.mult),
                             reads=[psa, ga], writes=[a_])
                        c.op('dve', lambda e: e.tensor_tensor(out=b_.t[:, :], in0=psb.t[:, :], in1=gb.t[:, j, :], op=ALU.mult),
                             reads=[psb, gb], writes=[b_])
                        c.op('pool', lambda e: e.tensor_tensor(out=mT.t[:, j, :], in0=a_.t[:, :], in1=b_.t[:, :], op=ALU.add),
                             reads=[a_, b_], writes=[(mT, j)])
                    for s in range(4):
                        pss = []
                        for hf in range(2):
                            ps = psr.next()
                            pss.append(ps)
                            for k in range(8):
                                c.op('pe', lambda e: e.matmul(ps.t[:, :], lhsT=mT.t[:, k, s * 128:(s + 1) * 128],
                                                              rhs=Wo.t[:, k, hf * 512:(hf + 1) * 512], start=(k == 0), stop=(k == 7)),
                                     reads=[Wo, mT], writes=[ps])
                            c.op('act', lambda e: e.activation(out=junk.t[:, :], in_=ps.t[:, :], func=AF.Square,
                                                               accum_out=ss.t[:, 4 * s + hf:4 * s + hf + 1]), reads=[ps], writes=[junk, (ss, s)])
                        res_tail(c, ss, pss, gpost, xt4, xt4.t[:, s, :], (yt4, s), yt4.t[:, s, :], s)
                    c.dma('sp', x1_d[ds(i * 512, 512), :].rearrange("(j p) d -> p j d", p=128), yt4[:], reads=[yt4])
                    c.end_iter()

        if _en(7):
            with Pool_(c, nc) as p:
                Wu = load_w(p, G["w_up"][:, :], ncols=4096, nk=8)
                Wd = load_w(p, G["w_dn"][:, :], ncols=1024, nk=32)
                gpost = p.sb([128, 1024], F32, "gpost2")
                c.dma('sp', gpost[:], vb_d[:, 1024:2048], writes=[gpost])
                c.end_iter()
                xt2 = p.sb([128, 2, 1024], F32, "xt2")
                yt2 = p.sb([128, 2, 1024], F32, "yt2")
                h2 = p.sb([128, 8, 256], BF16, "h2")
                uT = p.sb([128, 32, 256], BF16, "uT")
                rl = [p.sb([128, 256], BF16, "rl") for _ in range(2)]
                junkb = p.sb([128, 1024], BF16, "junkb")
                xn = [p.sb([128, 1024], BF16, "xn") for _ in range(2)]
                ss = p.sb([128, 16], F32, "ss")
                ptr = [p.ps([128, 8, 128], BF16, "ptr") for _ in range(2)]
                psr = PsRot(p, 6)
                for i in loop_iter(nc, Tq // 256, static):
                    c.dma('sp', xt2[:], x1_d[ds(i * 256, 256), :].rearrange("(j p) d -> p j d", p=128), writes=[xt2])
                    for j in range(2):
                        x_ = xt2
                        c.op('act', lambda e: e.activation(out=junkb.t[:], in_=x_.t[:, j, :], func=AF.Square, accum_out=ss.t[:, 8 + j:9 + j]),
                             reads=[x_], writes=[junkb, ss])
                        c.op('dve', lambda e: e.tensor_scalar(out=ss.t[:, 10 + j:11 + j], in0=ss.t[:, 8 + j:9 + j], scalar1=1.0 / D,
                                                              scalar2=EPS, op0=ALU.mult, op1=ALU.add), reads=[ss], writes=[ss])
                        c.op('act', lambda e: e.activation(out=ss.t[:, 10 + j:11 + j], in_=ss.t[:, 10 + j:11 + j], func=AF.Sqrt),
                             reads=[ss], writes=[ss])
                        c.op('dve', lambda e: e.reciprocal(out=ss.t[:, 12 + j:13 + j], in_=ss.t[:, 10 + j:11 + j]), reads=[ss], writes=[ss])
                        c.op('dve', lambda e: e.tensor_scalar(out=xn[j].t[:], in0=x_.t[:, j, :], scalar1=ss.t[:, 12 + j:13 + j],
                                                              scalar2=None, op0=ALU.mult), reads=[x_, ss], writes=[xn[j]])
                        for cc in range(8):
                            c.op('pe', lambda e: e.transpose(out=ptr[j].t[:, cc, :], in_=xn[j].t[:, cc * 128:(cc + 1) * 128],
                                                             identity=ident), reads=[xn[j], cbf], writes=[ptr[j]])
                        c.op('act', lambda e: e.activation(out=h2.t[:, :, j * 128:(j + 1) * 128], in_=ptr[j].t[:], func=AF.Copy),
                             reads=[ptr[j]], writes=[h2])
                    c.op('pool', lambda e: e.tensor_tensor(out=h2.t[:], in0=h2.t[:], in1=g2col.unsqueeze(2).to_broadcast([128, 8, 256]),
                                                           op=ALU.mult), reads=[h2, vc], writes=[h2])
                    for f in range(32):
                        ps = psr.next()
                        for k in range(8):
                            c.op('pe', lambda e: e.matmul(ps.t[:, 0:256], lhsT=Wu.t[:, k, f * 128:(f + 1) * 128], rhs=h2.t[:, k, :],
                                                          start=(k == 0), stop=(k == 7)), reads=[Wu, h2], writes=[ps])
                        r_ = rl[f % 2]
                        c.op('act', lambda e: e.activation(out=r_.t[:, :], in_=ps.t[:, 0:256], func=AF.Relu), reads=[ps], writes=[r_])
                        c.op('pool' if f % 2 else 'dve', lambda e: e.tensor_tensor(out=uT.t[:, f, :], in0=r_.t[:, :], in1=r_.t[:, :], op=ALU.mult),
                             reads=[r_], writes=[(uT, f)])
                    for s in range(2):
                        pss = []
                        for hf in range(2):
                            ps = psr.next()
                            pss.append(ps)
                            for k in range(32):
                                c.op('pe', lambda e: e.matmul(ps.t[:, :], lhsT=uT.t[:, k, s * 128:(s + 1) * 128],
                                                              rhs=Wd.t[:, k, hf * 512:(hf + 1) * 512], start=(k == 0), stop=(k == 31)),
                                     reads=[Wd, uT], writes=[ps])
                            c.op('act', lambda e: e.activation(out=junkb.t[:, 0:512], in_=ps.t[:, :], func=AF.Square,
                                                               accum_out=ss.t[:, 4 * s + hf:4 * s + hf + 1]), reads=[ps], writes=[junkb, (ss, s)])
                        res_tail(c, ss, pss, gpost, xt2, xt2.t[:, s, :], (yt2, s), yt2.t[:, s, :], s)
                    c.dma('sp', y_d[ds(i * 256, 256), :].rearrange("(j p) d -> p j d", p=128), yt2[:], reads=[yt2])
                    c.end_iter()


def res_tail(c, ss, pss, gpost, xb, xv, yb, yv, s):
    k = (ss, s)
    c0 = 4 * s
    c.op('dve', lambda e: e.tensor_tensor(out=ss.t[:, c0 + 2:c0 + 3], in0=ss.t[:, c0:c0 + 1], in1=ss.t[:, c0 + 1:c0 + 2], op=ALU.add),
         reads=[k], writes=[k])
    c.op('dve', lambda e: e.tensor_scalar(out=ss.t[:, c0 + 2:c0 + 3], in0=ss.t[:, c0 + 2:c0 + 3], scalar1=1.0 / D, scalar2=EPS,
                                          op0=ALU.mult, op1=ALU.add), reads=[k], writes=[k])
    c.op('act', lambda e: e.activation(out=ss.t[:, c0 + 2:c0 + 3], in_=ss.t[:, c0 + 2:c0 + 3], func=AF.Ln), reads=[k], writes=[k])
    c.op('act', lambda e: e.activation(out=ss.t[:, c0 + 3:c0 + 4], in_=ss.t[:, c0 + 2:c0 + 3], func=AF.Exp, scale=-0.5), reads=[k], writes=[k])
    for hf in range(2):
        cs = slice(hf * 512, (hf + 1) * 512)
        c.op('dve', lambda e: e.scalar_tensor_tensor(out=yv[:, cs], in0=pss[hf].t[:, :], scalar=ss.t[:, c0 + 3:c0 + 4], in1=gpost.t[:, cs],
                                                     op0=ALU.mult, op1=ALU.mult), reads=[pss[hf], k, gpost], writes=[yb])
    c.op('pool', lambda e: e.tensor_tensor(out=yv, in0=yv, in1=xv, op=ALU.add), reads=[yb, xb], writes=[yb])


def _t5_bucket_np(rel):
    nb = 16
    ret = (rel > 0).astype(np.int64) * nb
    n = np.abs(rel)
    max_exact = 8
    is_small = n < max_exact
    nf = np.maximum(n, 1).astype(np.float32)
    large = max_exact + (np.log(nf / np.float32(max_exact)) / np.float32(math.log(128 / max_exact))
                         * np.float32(nb - max_exact)).astype(np.int64)
    large = np.minimum(large, nb - 1)
    return ret + np.where(is_small, n, large)


def _consts():
    cbf = np.zeros((128, 256), np.float32)
    cbf[:, 0:128] = np.eye(128)
    cbf[:, 128:256] = 1.0
    cf = np.zeros((128, 384), np.float32)
    cf[:, 256:384] = 1.0
    cf[:, 0:128] = np.eye(128)[::-1]
    U = np.triu(np.ones((64, 64), np.float32))
    cf[0:64, 128:192] = U
    cf[0:64, 192:256] = U.T
    return cbf.astype(ml_dtypes.bfloat16), cf


def _oht(flip):
    d = J0 - np.arange(NJ)
    if flip:
        d = -d
    b = _t5_bucket_np(d)
    oh = np.zeros((32, NJ), np.float32)
    oh[b, np.arange(NJ)] = 1.0
    return oh


def _cols(v):
    return np.ascontiguousarray(v.reshape(8, 128).T)


def _vec_inputs(I, swap):
    lbf, lbb = I["lb_fwd"], I["lb_bwd"]
    if swap:
        lbf, lbb = lbb, lbf
    vc = np.zeros((128, 50), np.float32)
    vc[:, 0:8] = _cols(I["g_mix_pre"][0])
    vc[:, 8:16] = _cols(I["g_mlp_pre"][0])
    vc[:, 16:24] = _cols(lbf[0])
    vc[:, 24:32] = _cols(lbf[1])
    vc[:, 32:40] = _cols(lbb[0])
    vc[:, 40:48] = _cols(lbb[1])
    vc[:, 48] = I["g_attn_sub"][0]
    vc[:, 49] = I["g_hgrn_out"][0]
    row = np.concatenate([I["g_mix_post"][0], I["g_mlp_post"][0], lbf[0], lbf[1], lbb[0], lbb[1],
                          I["lam_q1"][0], I["lam_q2"][0], I["lam_k1"][0], I["lam_k2"][0]]).astype(np.float32)
    vb = np.ascontiguousarray(np.broadcast_to(row[None, :], (128, 6400)))
    return vc, vb


def make_in_maps(I, n_cores=8, TS=None, TP=None):
    I = {k: np.asarray(v) for k, v in I.items()}
    xs_all, xp_all = I["x_sample"], I["x_prompt"]
    w_in = np.ascontiguousarray(I["w_in"][0])
    cbf, cf = _consts()
    relb = np.ascontiguousarray(I["rel_bias"], dtype=np.float32)
    wf = [np.ascontiguousarray(w_in[:, 4096:6144]),
          np.ascontiguousarray(np.concatenate([w_in[:, 5120:6144], w_in[:, 4096:5120]], axis=1))]
    vcs, vbs = _vec_inputs(I, False)
    vcp = [vcs, _vec_inputs(I, True)[0]]
    vbp = [vbs, _vec_inputs(I, True)[1]]
    oht = [_oht(False), _oht(True)]

    def cfar(flip):
        a, b = (31, 15) if not flip else (15, 31)
        out = np.zeros((8, 128, 2), np.float32)
        out[:, :, 0] = relb[a][:, None]
        out[:, :, 1] = relb[b][:, None]
        return out
    cfars = [cfar(False), cfar(True)]
    shared = {"w_in": w_in, "w_pa": np.ascontiguousarray(I["w_proj_a"][0]), "w_pb": np.ascontiguousarray(I["w_proj_b"][0]),
              "w_o": np.ascontiguousarray(I["w_out"][0]), "w_up": np.ascontiguousarray(I["w_mlp_up"][0]),
              "w_dn": np.ascontiguousarray(I["w_mlp_down"][0]), "relb": relb, "cbf": cbf, "cf": cf,
              "vc_s": vcs, "vb_s": vbs, "oht_s": oht[0], "cfar_s": cfars[0]}
    maps = []
    for cid in range(n_cores):
        par = cid % 2
        xp = xp_all[(cid // 2) % xp_all.shape[0]]
        if par:
            xp = xp[::-1]
        m = dict(shared)
        m.update({"xs": np.ascontiguousarray(xs_all[cid % xs_all.shape[0]]), "xp": np.ascontiguousarray(xp),
                  "w_fsw": wf[par], "vc_p": vcp[par], "vb_p": vbp[par], "oht_p": oht[par], "cfar_p": cfars[par]})
        maps.append(m)
    return maps


_NC_CACHE = {}


def kernel(**inputs):
    TS = inputs["x_sample"].shape[1]
    TP = inputs["x_prompt"].shape[1]
    key = (TS, TP)
    if key not in _NC_CACHE:
        _NC_CACHE[key] = build(TS, TP)
    nc = _NC_CACHE[key]
    maps = make_in_maps(inputs)
    res = run_bass_kernel_spmd(nc, maps, core_ids=list(range(8)))
    B, Bs = inputs["x_prompt"].shape[0], inputs["x_sample"].shape[0]
    y_s = np.stack([np.asarray(res.results[cid]["ys"], dtype=np.float32) for cid in range(Bs)], axis=0)
    y_p = np.zeros((B, TP, D), np.float32)
    hq = TP // 2
    for cid in range(8):
        b, par = cid // 2, cid % 2
        yp = np.asarray(res.results[cid]["yp"], dtype=np.float32)
        if par == 0:
            y_p[b, 0:hq] = yp
        else:
            y_p[b, hq:] = yp[::-1]
    return (y_p, y_s)
```
